# Optimizing a Trainium2 kernel written in Bass

```python
import jax
import jax.numpy as jnp
from jax import lax
import numpy as np

D_MODEL = 1024
BATCH = 8
SEQ = 2048
DEPTH = 2

HEAD_DIM = 64
ROPE_THETA = 10000.0
RMS_EPS = 1e-6
NEG_INF = -1e30
FORCE_SCORE = 1e9
BAND_BLK = 128

DIL_GROUPS = ((128, 1), (512, 4), (2048, 16))
A_GROUPS = len(DIL_GROUPS)
A_HEADS = 4
A_WIDTH = A_GROUPS * A_HEADS * HEAD_DIM
A_OUT = A_HEADS * HEAD_DIM

NSA_HEADS = 8
NSA_KV = 2
NSA_REP = NSA_HEADS // NSA_KV
NSA_Q_WIDTH = NSA_HEADS * HEAD_DIM
NSA_KV_WIDTH = NSA_KV * HEAD_DIM
NSA_BRANCHES = 3
CMP_LEN = 32
CMP_STRIDE = 16
CMP_HID = 256
SEL_LEN = 64
SEL_TOPK = 16
SEL_QBLK = 64
WIN_LEN = 512

D_FF = 2816
CONV_W = 3
N_MOD = 6

IN_SIZES = (A_WIDTH, A_WIDTH, A_WIDTH, NSA_Q_WIDTH,
            NSA_KV_WIDTH, NSA_KV_WIDTH, NSA_KV_WIDTH, NSA_KV_WIDTH, NSA_KV_WIDTH, NSA_KV_WIDTH,
            NSA_HEADS * NSA_BRANCHES, D_MODEL, D_MODEL)
N_IN = sum(IN_SIZES)

kernel_name = 'hybrid_dilated_nsa_convffn_adaln_block'


def rms_norm(x, g):
    xf = x.astype(jnp.float32)
    y = xf * lax.rsqrt(jnp.mean(xf * xf, axis=-1, keepdims=True) + RMS_EPS)
    return (y * g.astype(jnp.float32)).astype(x.dtype)


def rope_tables(seq):
    inv = 1.0 / (ROPE_THETA ** (jnp.arange(0, HEAD_DIM, 2, dtype=jnp.float32) / HEAD_DIM))
    ang = jnp.arange(seq, dtype=jnp.float32)[:, None] * inv[None, :]
    return jnp.cos(ang), jnp.sin(ang)


def apply_rope(x, cos, sin):
    x1, x2 = jnp.split(x, 2, axis=-1)
    c = cos[None, :, None, :].astype(x.dtype)
    s = sin[None, :, None, :].astype(x.dtype)
    return jnp.concatenate([x1 * c - x2 * s, x2 * c + x1 * s], axis=-1)


def banded_attention(q, k, v, n_back):
    n, L, G, R, hd = q.shape
    nb = -(-L // BAND_BLK)
    n_prev = -(-n_back // BAND_BLK)
    pad = nb * BAND_BLK - L
    kw_len = (n_prev + 1) * BAND_BLK
    qb = jnp.pad(q, ((0, 0), (0, pad), (0, 0), (0, 0), (0, 0))).reshape(n, nb, BAND_BLK, G, R, hd)
    kv_pad = ((0, 0), (n_prev * BAND_BLK, pad), (0, 0), (0, 0))
    win = jnp.arange(nb)[:, None] + jnp.arange(n_prev + 1)[None, :]
    kb = jnp.pad(k, kv_pad).reshape(n, nb + n_prev, BAND_BLK, G, hd)[:, win].reshape(n, nb, kw_len, G, hd)
    vb = jnp.pad(v, kv_pad).reshape(n, nb + n_prev, BAND_BLK, G, hd)[:, win].reshape(n, nb, kw_len, G, hd)
    s = jnp.einsum('nbqgrd,nbkgd->nbgrqk', qb, kb).astype(jnp.float32) * (hd ** -0.5)
    qpos = jnp.arange(nb)[:, None] * BAND_BLK + jnp.arange(BAND_BLK)[None, :]
    kpos = (jnp.arange(nb)[:, None] - n_prev) * BAND_BLK + jnp.arange(kw_len)[None, :]
    dist = qpos[:, :, None] - kpos[:, None, :]
    mask = (dist >= 0) & (dist <= n_back) & (kpos[:, None, :] >= 0)
    s = jnp.where(mask[None, :, None, None], s, NEG_INF)
    m = jnp.max(s, axis=-1, keepdims=True)
    e = jnp.exp(s - m)
    den = jnp.sum(e, axis=-1, keepdims=True)
    o = jnp.einsum('nbgrqk,nbkgd->nbqgrd', (e / den).astype(v.dtype), vb)
    o = o.reshape(n, nb * BAND_BLK, G, R, hd)[:, :L]
    lse = (m + jnp.log(den))[..., 0].transpose(0, 1, 4, 2, 3).reshape(n, nb * BAND_BLK, G, R)[:, :L]
    return o, lse


def to_strided(t, dil):
    B, S = t.shape[:2]
    rest = t.shape[2:]
    return jnp.swapaxes(t.reshape(B, S // dil, dil, *rest), 1, 2).reshape(B * dil, S // dil, *rest)


def from_strided(t, B, dil):
    n, L = t.shape[:2]
    rest = t.shape[2:]
    return jnp.swapaxes(t.reshape(B, dil, L, *rest), 1, 2).reshape(B, L * dil, *rest)


def dilated_attention(q, k, v, cos, sin):
    B, S = q.shape[:2]
    flat = (B, S, A_GROUPS * A_HEADS, HEAD_DIM)
    q = apply_rope(q.reshape(flat), cos, sin).reshape(q.shape)
    k = apply_rope(k.reshape(flat), cos, sin).reshape(k.shape)
    outs, lses = [], []
    for g, (window, dil) in enumerate(DIL_GROUPS):
        o, lse = banded_attention(to_strided(q[:, :, g], dil)[:, :, :, None],
                                  to_strided(k[:, :, g], dil), to_strided(v[:, :, g], dil), window // dil)
        outs.append(from_strided(o[:, :, :, 0], B, dil))
        lses.append(from_strided(lse[:, :, :, 0], B, dil))
    w = jax.nn.softmax(jnp.stack(lses, axis=0), axis=0)
    out = jnp.einsum('gbsh,gbshd->bshd', w.astype(v.dtype), jnp.stack(outs, axis=0))
    return out.reshape(B, S, A_OUT)


def nsa_attention(q, k_cmp, v_cmp, k_sel, v_sel, k_win, v_win, gate_logits,
                  pe_k, pe_v, w1_k, w2_k, w1_v, w2_v, cos, sin):
    B, S = q.shape[:2]
    scale = HEAD_DIM ** -0.5
    qh = q.reshape(B, S, NSA_HEADS, HEAD_DIM)
    q_rot = apply_rope(qh, cos, sin).reshape(B, S, NSA_KV, NSA_REP, HEAD_DIM)
    q_nope = qh.reshape(B, S, NSA_KV, NSA_REP, HEAD_DIM)
    tpos = jnp.arange(S)

    n_cmp = (S - CMP_LEN) // CMP_STRIDE + 1
    starts = jnp.arange(n_cmp) * CMP_STRIDE
    blk_idx = starts[:, None] + jnp.arange(CMP_LEN)[None, :]

    def compress(t, pe, w1, w2):
        blocks = t[:, blk_idx] + pe[:, None, :]
        flat = blocks.transpose(0, 1, 3, 2, 4).reshape(B, n_cmp, NSA_KV, CMP_LEN * HEAD_DIM)
        return jax.nn.silu(flat @ w1) @ w2

    kc = compress(k_cmp, pe_k, w1_k, w2_k)
    vc = compress(v_cmp, pe_v, w1_v, w2_v)
    s_cmp = jnp.einsum('bsgrd,bngd->bsgrn', q_nope, kc).astype(jnp.float32) * scale
    cmask = ((starts + CMP_LEN - 1)[None, :] <= tpos[:, None])[None, :, None, None, :]
    p_cmp = jnp.where(cmask, jax.nn.softmax(jnp.where(cmask, s_cmp, NEG_INF), axis=-1), 0.0)
    o_cmp = jnp.einsum('bsgrn,bngd->bsgrd', p_cmp.astype(vc.dtype), vc)

    n_blk = S // SEL_LEN
    bstart = jnp.arange(n_blk) * SEL_LEN
    overlap = ((starts[:, None] < bstart[None, :] + SEL_LEN) &
               (starts[:, None] + CMP_LEN > bstart[None, :])).astype(jnp.float32)
    imp = jnp.einsum('bsgrn,nj->bsgj', p_cmp, overlap)
    cur = tpos // SEL_LEN
    jb = jnp.arange(n_blk)
    forced = (jb[None, :] == 0) | (jb[None, :] == cur[:, None]) | (jb[None, :] == cur[:, None] - 1)
    valid = bstart[None, :] <= tpos[:, None]
    imp = jnp.where(forced[None, :, None, :], FORCE_SCORE, imp)
    imp = jnp.where(valid[None, :, None, :], imp, NEG_INF)
    n_top = min(SEL_TOPK, n_blk)
    _, sel_idx = lax.top_k(imp, n_top)

    kb = apply_rope(k_sel, cos, sin).reshape(B, n_blk, SEL_LEN, NSA_KV, HEAD_DIM).transpose(0, 3, 1, 2, 4)
    vb = v_sel.reshape(B, n_blk, SEL_LEN, NSA_KV, HEAD_DIM).transpose(0, 3, 1, 2, 4)
    nq = S // SEL_QBLK
    bi = jnp.arange(B)[:, None, None, None]
    gi = jnp.arange(NSA_KV)[None, None, :, None]

    def chunks(t):
        return jnp.swapaxes(t.reshape(B, nq, SEL_QBLK, *t.shape[2:]), 0, 1)

    def sel_block(args):
        qc, ic, ci = args
        kg = kb[bi, gi, ic]
        vg = vb[bi, gi, ic]
        s = jnp.einsum('bcgrd,bcgjld->bcgrjl', qc, kg).astype(jnp.float32) * scale
        qp = ci * SEL_QBLK + jnp.arange(SEL_QBLK)
        kp = ic[..., None] * SEL_LEN + jnp.arange(SEL_LEN)
        mask = (kp <= qp[None, :, None, None, None])[:, :, :, None]
        s = jnp.where(mask, s, NEG_INF)
        p = jax.nn.softmax(s.reshape(*s.shape[:4], -1), axis=-1).reshape(s.shape)
        return jnp.einsum('bcgrjl,bcgjld->bcgrd', p.astype(vg.dtype), vg)

    o_sel = lax.map(sel_block, (chunks(q_rot), chunks(sel_idx), jnp.arange(nq)))
    o_sel = jnp.swapaxes(o_sel, 0, 1).reshape(B, S, NSA_KV, NSA_REP, HEAD_DIM)

    o_win, _ = banded_attention(q_rot, apply_rope(k_win, cos, sin), v_win, WIN_LEN - 1)

    g = jax.nn.sigmoid(gate_logits.astype(jnp.float32)).reshape(B, S, NSA_KV, NSA_REP, NSA_BRANCHES)
    g = g.astype(q.dtype)
    out = g[..., 0:1] * o_cmp + g[..., 1:2] * o_sel + g[..., 2:3] * o_win
    return out.reshape(B, S, NSA_Q_WIDTH)


def causal_depthwise_conv(u, w, b):
    y = lax.conv_general_dilated(u, w[:, None, :], window_strides=(1,), padding=((CONV_W - 1, 0),),
                                 dimension_numbers=('NWC', 'WIO', 'NWC'), feature_group_count=u.shape[-1])
    return y + b


def setup_inputs(seed: int = 0) -> dict:
    key = jax.random.key(seed)
    ks = jax.random.split(key, 24)
    f32 = jnp.float32
    D = D_MODEL

    def nrm(k, shape, std):
        return jax.random.normal(k, shape, f32) * std

    return {
        'x': nrm(ks[0], (BATCH, SEQ, D), 1.0),
        'c': nrm(ks[1], (BATCH, D), 1.0),
        'norm1_g': 1.0 + nrm(ks[2], (DEPTH, D), 0.02),
        'norm2_g': 1.0 + nrm(ks[3], (DEPTH, D), 0.02),
        'final_g': 1.0 + nrm(ks[4], (D,), 0.02),
        'w_mod': nrm(ks[5], (DEPTH, D, N_MOD * D), 0.5 * D ** -0.5),
        'b_mod': nrm(ks[6], (DEPTH, N_MOD * D), 0.01),
        'w_in': nrm(ks[7], (DEPTH, D, N_IN), D ** -0.5),
        'cmp_pe_k': nrm(ks[8], (DEPTH, CMP_LEN, HEAD_DIM), 0.1),
        'cmp_pe_v': nrm(ks[9], (DEPTH, CMP_LEN, HEAD_DIM), 0.1),
        'cmp_w1_k': nrm(ks[10], (DEPTH, CMP_LEN * HEAD_DIM, CMP_HID), (CMP_LEN * HEAD_DIM) ** -0.5),
        'cmp_w2_k': nrm(ks[11], (DEPTH, CMP_HID, HEAD_DIM), CMP_HID ** -0.5),
        'cmp_w1_v': nrm(ks[12], (DEPTH, CMP_LEN * HEAD_DIM, CMP_HID), (CMP_LEN * HEAD_DIM) ** -0.5),
        'cmp_w2_v': nrm(ks[13], (DEPTH, CMP_HID, HEAD_DIM), CMP_HID ** -0.5),
        'w_br_a': nrm(ks[14], (DEPTH, A_OUT, D), A_OUT ** -0.5),
        'w_br_b': nrm(ks[15], (DEPTH, NSA_Q_WIDTH, D), NSA_Q_WIDTH ** -0.5),
        'w_out': nrm(ks[16], (DEPTH, D, D), D ** -0.5),
        'w_up': nrm(ks[17], (DEPTH, D, 2 * D_FF), D ** -0.5),
        'conv_w': nrm(ks[18], (DEPTH, CONV_W, 2 * D_FF), 0.5),
        'conv_b': nrm(ks[19], (DEPTH, 2 * D_FF), 0.01),
        'w_down': nrm(ks[20], (DEPTH, D_FF, D), D_FF ** -0.5),
    }


def reference(x, c, norm1_g, norm2_g, final_g, w_mod, b_mod, w_in, cmp_pe_k, cmp_pe_v,
              cmp_w1_k, cmp_w2_k, cmp_w1_v, cmp_w2_v, w_br_a, w_br_b, w_out, w_up,
              conv_w, conv_b, w_down):
    B, S, D = x.shape
    cos, sin = rope_tables(S)
    split_at = np.cumsum(IN_SIZES)[:-1].tolist()
    a_shape = (B, S, A_GROUPS, A_HEADS, HEAD_DIM)
    kv_shape = (B, S, NSA_KV, HEAD_DIM)
    for layer in range(DEPTH):
        mod = jax.nn.silu(c) @ w_mod[layer] + b_mod[layer]
        shift1, scale1, gate1, shift2, scale2, gate2 = jnp.split(mod[:, None, :], N_MOD, axis=-1)

        h = rms_norm(x, norm1_g[layer]) * (1 + scale1) + shift1
        (aq, ak, av, bq, kc, vc, ksl, vsl, kwn, vwn, blog, ga, gb) = jnp.split(
            h @ w_in[layer], split_at, axis=-1)
        y_a = dilated_attention(aq.reshape(a_shape), ak.reshape(a_shape), av.reshape(a_shape), cos, sin)
        y_b = nsa_attention(bq, kc.reshape(kv_shape), vc.reshape(kv_shape), ksl.reshape(kv_shape),
                            vsl.reshape(kv_shape), kwn.reshape(kv_shape), vwn.reshape(kv_shape), blog,
                            cmp_pe_k[layer], cmp_pe_v[layer], cmp_w1_k[layer], cmp_w2_k[layer],
                            cmp_w1_v[layer], cmp_w2_v[layer], cos, sin)
        merged = jax.nn.sigmoid(ga) * (y_a @ w_br_a[layer]) + jax.nn.sigmoid(gb) * (y_b @ w_br_b[layer])
        x = x + gate1 * (merged @ w_out[layer])

        h = rms_norm(x, norm2_g[layer]) * (1 + scale2) + shift2
        u = causal_depthwise_conv(h @ w_up[layer], conv_w[layer], conv_b[layer])
        u_gate, u_val = jnp.split(u, 2, axis=-1)
        x = x + gate2 * ((jax.nn.silu(u_gate) * u_val) @ w_down[layer])
    return rms_norm(x, final_g)
```

```python
import numpy as np
import ml_dtypes
from contextlib import ExitStack
import concourse.bass as bass
import concourse.mybir as mybir
from concourse.bass_utils import run_bass_kernel_spmd

F32 = mybir.dt.float32
BF16 = mybir.dt.bfloat16
ALU = mybir.AluOpType
AF = mybir.ActivationFunctionType
AX = mybir.AxisListType
ENGS = ['pe', 'act', 'dve', 'pool', 'sp']

S, D, KC, NT, NCH, CH = 2048, 1024, 8, 16, 4, 512
DFF = 2816
NFF = 22
NEG = -30000.0
SCALE = 0.125
N_IN = 5656
OFF_AQ, OFF_AK, OFF_AV, OFF_BQ = 0, 768, 1536, 2304
OFF_KC, OFF_VC, OFF_KSL, OFF_VSL, OFF_KWN, OFF_VWN = 2816, 2944, 3072, 3200, 3328, 3456
OFF_BLOG, OFF_GA, OFF_GB = 3584, 3608, 4632


class Prog:
    def __init__(self, nc, stack):
        self.nc = nc
        self.stack = stack
        self.gstack = stack
        self.q = {e: [] for e in ENGS}
        self.cnt = {e: 0 for e in ENGS}
        self.sems = {e: stack.enter_context(nc.semaphore('s_' + e)) for e in ENGS}
        self.waited = {e: {} for e in ENGS}
        self.res = {}
        self.dslots = {}

    def sb(self, name, shape, dtype, glob=False):
        st = self.gstack if glob else self.stack
        self.nalloc = getattr(self, 'nalloc', 0) + 1
        return st.enter_context(self.nc.sbuf_tensor('%s_%d' % (name, self.nalloc), list(shape), dtype))

    def _norm(self, reads, writes):
        excl = [r for r in reads if isinstance(r, str) and r.startswith('ps')]
        if excl:
            reads = [r for r in reads if r not in excl]
            writes = list(writes) + excl
        return reads, writes

    def _deps(self, eng, reads, writes):
        need = {}

        def add(k, v):
            if need.get(k, 0) < v:
                need[k] = v
        reads, writes = self._norm(reads, writes)
        for r in reads:
            st = self.res.get(r)
            if st and st['w']:
                add(*st['w'])
        for w in writes:
            st = self.res.get(w)
            if st:
                if st['w']:
                    add(*st['w'])
                for k, v in st['r'].items():
                    add(k, v)
        out = []
        for k, v in need.items():
            if k == eng and eng == 'pe':
                continue
            if self.waited[eng].get(k, 0) >= v:
                continue
            self.waited[eng][k] = v
            out.append((k, v))
        return out

    def _mark(self, tok, reads, writes):
        reads, writes = self._norm(reads, writes)
        for r in reads:
            st = self.res.setdefault(r, {'w': None, 'r': {}})
            if st['r'].get(tok[0], 0) < tok[1]:
                st['r'][tok[0]] = tok[1]
        for w in writes:
            self.res[w] = {'w': tok, 'r': {}}

    def _sem(self, k):
        return self.sems[k] if k in self.sems else self.dslots[k]['sem']

    def op(self, eng, fn, reads=(), writes=()):
        waits = [(self._sem(k), v) for k, v in self._deps(eng, reads, writes)]
        self.cnt[eng] += 1
        tok = (eng, self.cnt[eng])
        mysem = self.sems[eng]

        def run(e, waits=waits, fn=fn, mysem=mysem):
            for s, v in waits:
                e.wait_ge(s, v)
            fn(e).then_inc(mysem, 1)
        self.q[eng].append(run)
        self._mark(tok, reads, writes)
        return tok

    def dma(self, eng, out, in_, reads=(), writes=(), slot='d0'):
        slot = eng + '_' + slot
        if slot not in self.dslots:
            self.dslots[slot] = {'sem': self.gstack.enter_context(self.nc.semaphore('d_' + slot)), 'n': 0}
        ds = self.dslots[slot]
        deps = self._deps(eng, reads, writes)
        if ds['n'] > 0 and self.waited[eng].get(slot, 0) < 16 * ds['n']:
            self.waited[eng][slot] = 16 * ds['n']
            deps = [d for d in deps if d[0] != slot] + [(slot, 16 * ds['n'])]
        waits = [(self._sem(k), v) for k, v in deps]
        ds['n'] += 1
        tok = (slot, 16 * ds['n'])
        dsem = ds['sem']

        def run(e, waits=waits, out=out, in_=in_, dsem=dsem):
            for s, v in waits:
                e.wait_ge(s, v)
            e.dma_start(out=out, in_=in_).then_inc(dsem, 16)
        self.q[eng].append(run)
        self._mark(tok, reads, writes)
        return tok

    def report(self, tag):
        import os
        if os.environ.get('SBUF_REPORT'):
            print('SBUF', tag, 'remaining', self.nc.sbuf_bytes_remaining, flush=True)

    def barrier(self):
        targets = [(k, 16 * ds['n']) for k, ds in self.dslots.items() if ds['n'] > 0]
        targets += [(e, self.cnt[e]) for e in ENGS if self.cnt[e] > 0]
        for eng in ENGS:
            waits = []
            for k, v in targets:
                if k == eng:
                    continue
                if self.waited[eng].get(k, 0) >= v:
                    continue
                self.waited[eng][k] = v
                waits.append((self._sem(k), v))

            def run(e, waits=waits):
                for s, v in waits:
                    e.wait_ge(s, v)
            self.q[eng].append(run)

    def emit(self):
        q = self.q
        with self.nc.Block() as block:
            @block.tensor
            def _(e):
                for f in q['pe']:
                    f(e)

            @block.scalar
            def _(e):
                for f in q['act']:
                    f(e)

            @block.vector
            def _(e):
                for f in q['dve']:
                    f(e)

            @block.gpsimd
            def _(e):
                for f in q['pool']:
                    f(e)

            @block.sync
            def _(e):
                for f in q['sp']:
                    f(e)
        self.q = {e: [] for e in ENGS}


def make_consts():
    bf = ml_dtypes.bfloat16
    C = {}
    C['k_identb'] = np.eye(128, dtype=np.float32).astype(bf)
    C['k_identf'] = np.eye(128, dtype=np.float32)
    C['k_onesf'] = np.ones((128, 128), np.float32)
    pm = np.zeros((128, 128), np.float32)
    for m in range(128):
        blk, d = divmod(m, 64)
        pm[blk * 64 + (d + 32) % 64, m] = 1.0
    C['k_pm'] = pm.astype(bf)
    inv = (1.0 / (np.float32(10000.0) ** (np.arange(0, 64, 2, dtype=np.float32) / np.float32(64)))).astype(np.float32)
    ang = (np.arange(S, dtype=np.float32)[:, None] * inv[None, :]).astype(np.float32)
    cos, sin = np.cos(ang).astype(np.float32), np.sin(ang).astype(np.float32)
    cosT = np.zeros((128, S), np.float32)
    sinT = np.zeros((128, S), np.float32)
    for row in range(128):
        d = row % 64
        cosT[row] = cos[:, d % 32]
        sinT[row] = (-1.0 if d < 32 else 1.0) * sin[:, d % 32]
    C['k_cos'] = cosT
    C['k_sin'] = sinT
    kk = np.arange(128)[:, None]
    qq = np.arange(128)[None, :]
    mc = np.where(kk > qq, NEG, 0.0).astype(np.float32)
    ml = np.where(kk < qq, NEG, 0.0).astype(np.float32)
    mf = np.where(kk <= qq, NEG, 0.0).astype(np.float32)
    C['k_mpair'] = np.concatenate([mc, ml], 1).astype(bf)
    C['k_mc4'] = np.tile(mc, (1, 4)).astype(bf)
    C['k_mfar'] = mf.astype(bf)
    n = np.arange(128)[:, None, None]
    c = np.arange(4)[None, :, None]
    q = np.arange(512)[None, None, :]
    cneg = np.where((16 * n + 31 > 512 * c + q) | (n == 127), NEG, 0.0).astype(np.float32)
    C['k_cneg'] = cneg.reshape(128, 2048).astype(bf)
    p = np.arange(128)[:, None]
    u = np.arange(256)[None, :]
    C['k_mwide'] = np.where(16 * (u - 120) + 31 > p, NEG, 0.0).astype(np.float32).astype(bf)
    E = np.zeros((128, 2048), np.float32)
    for j in range(32):
        E[j, 64 * j:64 * j + 64] = 1.0
    C['k_E'] = E.astype(bf)
    Fa = np.full((128, 8, 32), -1e30, np.float32)
    for i in range(8, 16):
        for pp in range(128):
            cur = 2 * i + (1 if pp >= 64 else 0)
            for j in (0, cur, cur - 1):
                Fa[pp, i - 8, j] = 1e9
    C['k_F'] = Fa.reshape(128, 256)
    sg = np.zeros((24, 12, 128), np.float32)
    for h in range(8):
        for b in range(3):
            pr, par = divmod(h, 2)
            sg[h * 3 + b, pr * 3 + b, par * 64:(par + 1) * 64] = 1.0
    C['k_selg'] = sg.reshape(24, 1536).astype(bf)
    ol = np.zeros((128, 128), np.float32)
    ol[:, 0:64] = 1.0
    oh = np.zeros((128, 128), np.float32)
    oh[:, 64:128] = 1.0
    C['k_oneslo'] = ol.astype(bf)
    C['k_oneshi'] = oh.astype(bf)
    C['k_zeros'] = np.zeros((128, 128), np.float32).astype(bf)
    C['k_one1'] = np.ones((1, 1), np.float32)
    return C


CONST_SHAPES = {
    'k_identb': ([128, 128], BF16), 'k_identf': ([128, 128], F32), 'k_onesf': ([128, 128], F32),
    'k_pm': ([128, 128], BF16), 'k_cos': ([128, S], F32), 'k_sin': ([128, S], F32),
    'k_mpair': ([128, 256], BF16), 'k_mc4': ([128, 512], BF16), 'k_mfar': ([128, 128], BF16),
    'k_cneg': ([128, 2048], BF16), 'k_mwide': ([128, 256], BF16), 'k_E': ([128, 2048], BF16),
    'k_F': ([128, 256], F32), 'k_selg': ([24, 1536], BF16), 'k_oneslo': ([128, 128], BF16),
    'k_oneshi': ([128, 128], BF16), 'k_zeros': ([128, 128], BF16), 'k_one1': ([1, 1], F32),
}

IN_SHAPES = {
    'x': [S, D], 'c': [8, 128], 'norm1_g': [2, 8, 128], 'norm2_g': [2, 8, 128], 'final_g': [8, 128],
    'b_mod': [2, 48, 128], 'conv_w': [2, 3, 44, 128], 'conv_b': [2, 44, 128],
    'wmod_l': [2, 12, 128, KC * 512], 'wina_l': [2, 3, 3, 128, KC * 256], 'wnsa_l': [2, 128, 12480],
    'wmrg_l': [2, 8, 128, 22 * 128], 'wout_l': [2, 8, 128, KC * 128], 'wup_l': [2, NFF, 128, KC * 256],
    'wdn_l': [2, 8, 128, NFF * 128], 'w1d_l': [2, 2, 128, 32 * 256], 'w2d_l': [2, 128, 256], 'w2v_l': [2, 128, 128],
    'pe_l': [2, 2, 128, 32],
}
NSA_OFF = {'wkv': (0, 2048), 'wbq': (2048, 6144), 'wkd': (6144, 10240), 'wvv': (10240, 12288), 'wbl': (12288, 12480)}


def relayout_weights(inp):
    f = lambda a: np.asarray(a, dtype=np.float32)

    def blk(w, c0, n):
        k = w.shape[0] // 128
        return w[:, c0:c0 + n].reshape(k, 128, n).transpose(1, 0, 2)
    o = {}
    w_mod, w_in = f(inp['w_mod']), f(inp['w_in'])
    o['wmod_l'] = np.stack([np.stack([blk(w_mod[l], 512 * j, 512).reshape(128, -1) for j in range(12)]) for l in range(2)])
    o['wina_l'] = np.stack([np.stack([np.stack([blk(w_in[l], off + 256 * g, 256).reshape(128, -1) for off in (OFF_AQ, OFF_AK, OFF_AV)])
                                      for g in range(3)]) for l in range(2)])
    nsa = []
    for l in range(2):
        wkv = blk(w_in[l], OFF_KC, 256).reshape(128, -1)
        wbq = blk(w_in[l], OFF_BQ, 512).reshape(128, -1)
        wkd = np.zeros((128, KC, 2, 2, 2, 64), np.float32)
        for w_, off in ((0, OFF_KSL), (1, OFF_KWN)):
            for g in range(2):
                b = blk(w_in[l], off + 64 * g, 64)
                wkd[:, :, w_, g, 0, :] = b
                wkd[:, :, w_, g, 1, :] = b
        wvv = np.stack([blk(w_in[l], OFF_VSL, 128), blk(w_in[l], OFF_VWN, 128)], axis=2).reshape(128, -1)
        wbl = blk(w_in[l], OFF_BLOG, 24).reshape(128, -1)
        nsa.append(np.concatenate([wkv, wbq, wkd.reshape(128, -1), wvv, wbl], axis=1))
    o['wnsa_l'] = np.stack(nsa)
    wbra, wbrb, wout = f(inp['w_br_a']), f(inp['w_br_b']), f(inp['w_out'])
    o['wmrg_l'] = np.stack([np.stack([np.concatenate([blk(w_in[l], OFF_GA + 128 * fi, 128), blk(w_in[l], OFF_GB + 128 * fi, 128),
                                                      blk(wbra[l], 128 * fi, 128), blk(wbrb[l], 128 * fi, 128)], axis=1).reshape(128, -1)
                                      for fi in range(8)]) for l in range(2)])
    o['wout_l'] = np.stack([np.stack([blk(wout[l], 128 * fi, 128).reshape(128, -1) for fi in range(8)]) for l in range(2)])
    wup, wdn = f(inp['w_up']), f(inp['w_down'])
    o['wup_l'] = np.stack([np.stack([np.stack([blk(wup[l], 128 * fc, 128), blk(wup[l], DFF + 128 * fc, 128)], axis=2).reshape(128, -1)
                                     for fc in range(NFF)]) for l in range(2)])
    o['wdn_l'] = np.stack([np.stack([blk(wdn[l], 128 * fi, 128).reshape(128, -1) for fi in range(8)]) for l in range(2)])
    w1 = [f(inp['cmp_w1_k']), f(inp['cmp_w1_v'])]
    o['w1d_l'] = np.stack([np.stack([np.tile(w1[kv][l].reshape(32, 64, 256).transpose(1, 0, 2), (2, 1, 1)).reshape(128, -1)
                                     for kv in range(2)]) for l in range(2)])
    w2k, w2v = f(inp['cmp_w2_k']), f(inp['cmp_w2_v'])
    o['w2d_l'] = np.stack([np.tile(blk(w2k[l], 0, 64), (1, 1, 2)).reshape(128, -1) for l in range(2)])
    o['w2v_l'] = np.stack([blk(w2v[l], 0, 64).reshape(128, -1) for l in range(2)])
    pe = [f(inp['cmp_pe_k']), f(inp['cmp_pe_v'])]
    o['pe_l'] = np.stack([np.stack([np.tile(pe[kv][l].T, (2, 1)) for kv in range(2)]) for l in range(2)])
    return {k_: np.ascontiguousarray(v) for k_, v in o.items()}


SCOPED_CONSTS = ('k_cos', 'k_sin', 'k_cneg', 'k_E')


class Ctx:
    pass


def load_scoped_consts(K, names):
    P = K.P
    for nm in names:
        shp, dt = CONST_SHAPES[nm]
        if nm in ('k_cos', 'k_sin'):
            K.k[nm] = P.sb('c' + nm, shp, BF16)
            K.wslot += 1
            P.dma('pool', K.k[nm][:], K.d[nm], writes=[nm], slot='w%d' % (K.wslot % 6))
        else:
            K.k[nm] = P.sb('c' + nm, shp, dt)
            P.dma('sp', K.k[nm][:], K.d[nm], writes=[nm], slot='c%d' % (len(nm) % 4))


def build(nlayers=2, taps=(), upto='all'):
    nc = bass.Bass("TRN2", target_bir_lowering=False)
    K = Ctx()
    K.nc = nc
    K.d = {}
    for nm, shp in IN_SHAPES.items():
        K.d[nm] = nc.dram_tensor(nm, list(shp), F32, kind="ExternalInput").ap()
    for nm, (shp, dt) in CONST_SHAPES.items():
        K.d[nm] = nc.dram_tensor(nm, list(shp), dt, kind="ExternalInput").ap()
    K.out = nc.dram_tensor("out", [S, D], F32, kind="ExternalOutput").ap()
    K.xpark = nc.dram_tensor("xpark", [128, KC * S], F32).ap()
    K.taps = {}
    K.tapset = set(taps)

    with ExitStack() as gst:
        P = Prog(nc, gst)
        K.P = P
        K.pb = [gst.enter_context(nc.psum_tensor('pb%d' % i, [128, 512], F32)) for i in range(8)]
        K.arena = P.sb('arena', [128, KC * S], F32, glob=True)
        K.xT = K.arena[:].rearrange("p (k s) -> p k s", k=KC)
        K.hT = P.sb('hT', [128, KC, S], BF16, glob=True)
        K.k = {}
        for nm, (shp, dt) in CONST_SHAPES.items():
            if nm in SCOPED_CONSTS:
                continue
            K.k[nm] = P.sb('c' + nm, shp, dt, glob=True)
        K.modc = P.sb('modc', [128, 2, 48], F32, glob=True)
        K.gs = P.sb('gs', [128, 2, 2, 8], F32, glob=True)
        K.gfin = P.sb('gfin', [128, 8], F32, glob=True)
        K.cvw = P.sb('cvw', [128, 2, 4, 44], F32, glob=True)
        K.halo = P.sb('halo', [128, 44, 2], F32, glob=True)
        K.wslot = 0

        phase_start(K)
        order = ['start', 'norm', 'dil', 'nsa', 'merge', 'ffn', 'all']
        lvl = order.index(upto)
        for layer in range(nlayers):
            with ExitStack() as lst:
                P.stack = lst
                K.yaT = P.sb('yaT', [128, 2, S], BF16)
                K.ybT = P.sb('ybT', [128, 4, S], BF16)
                stop_early = False
                with ExitStack() as dsc:
                    P.stack = dsc
                    K.pre_dil = [P.sb('pwq', [128, KC, 256], BF16), P.sb('pwk', [128, KC, 256], BF16), P.sb('pwv', [128, KC, 256], BF16)]
                    for w_i, nm_ in enumerate(('wq', 'wk', 'wv')):
                        ldflat(K, K.pre_dil[w_i][:].rearrange("p k n -> p (k n)"), K.d['wina_l'][layer, 0, w_i], (nm_, 0))
                    phase_norm(K, layer, 0)
                    tap_h(K, 'h1_%d' % layer)
                    if lvl <= 1:
                        P.barrier()
                        P.emit()
                        stop_early = True
                    else:
                        park_x(K)
                        phase_dilated(K, layer)
                P.stack = lst
                if stop_early:
                    break
                if lvl >= 3:
                    phase_nsa(K, layer)
                if lvl >= 4:
                    unpark_x(K)
                    phase_merge(K, layer)
                P.barrier()
                P.emit()
            P.stack = gst
            if lvl <= 3:
                break
            tap_x(K, 'xm%d' % layer)
            if lvl <= 4:
                break
            with ExitStack() as fst:
                P.stack = fst
                K.pre_wup = [P.sb('pwup%d' % i, [128, KC, 2, 128], BF16) for i in range(4)]
                for fc_ in range(3):
                    ldflat(K, K.pre_wup[fc_][:].rearrange("p k a n -> p (k a n)"), K.d['wup_l'][layer, fc_], ('wup', fc_))
                phase_norm(K, layer, 1)
                phase_ffn(K, layer)
            P.stack = gst
            tap_x(K, 'xf%d' % layer)
        if lvl >= 6:
            phase_final(K)
        P.barrier()
        P.emit()
    return nc, K


def tap_x(K, name):
    if name not in K.tapset:
        return
    P = K.P
    t = K.nc.dram_tensor("tap_" + name, [128, KC * S], F32, kind="ExternalOutput").ap()
    P.dma('sp', t, K.arena[:], reads=['xT'], slot='tap')
    K.taps[name] = t


def tap_h(K, name):
    if name not in K.tapset:
        return
    P = K.P
    t = K.nc.dram_tensor("tap_" + name, [128, KC * S], BF16, kind="ExternalOutput").ap()
    P.dma('sp', t, K.hT[:].rearrange("p k s -> p (k s)"), reads=['hT'], slot='tap')
    K.taps[name] = t


def tap_any(K, name, ap, reads, shape, dt):
    if name not in K.tapset:
        return
    t = K.nc.dram_tensor("tap_" + name, list(shape), dt, kind="ExternalOutput").ap()
    K.P.dma('sp', t, ap, reads=reads, slot='tap')
    K.taps[name] = t


def mm(P, out, lhsT, rhs, start, stop, reads, writes):
    return P.op('pe', lambda e: e.matmul(out, lhsT=lhsT, rhs=rhs, start=start, stop=stop, skip_group_check=True),
                reads=reads, writes=writes)


def load_w(K, dst, src2d, wname, eng='pool'):
    K.wslot += 1
    return K.P.dma(eng, dst, src2d.rearrange("(k p) n -> p k n", p=128), writes=[wname], slot='w%d' % (K.wslot % 6))


def ldflat(K, dst_flat, src, wname, eng='pool'):
    K.wslot += 1
    return K.P.dma(eng, dst_flat, src, writes=[wname], slot='w%d' % (K.wslot % 6))


def proj_fm(K, ps, psname, wt, wname, cols, tok0, ntok, M=128):
    P = K.P
    for kc in range(KC):
        mm(P, ps[0:M, 0:ntok], wt[:, kc, cols], K.hT[:, kc, tok0:tok0 + ntok], kc == 0, kc == KC - 1,
           reads=[wname, 'hT'], writes=[psname])


def phase_start(K):
    P, nc, d = K.P, K.nc, K.d
    _prev_stack = P.stack
    with ExitStack() as st:
        P.stack = st
        for i, (nm, (shp, dt)) in enumerate(CONST_SHAPES.items()):
            if nm in SCOPED_CONSTS:
                continue
            P.dma('sp', K.k[nm][:], d[nm], writes=[nm], slot='c%d' % (i % 4))
        identf = K.k['k_identf']
        rows = P.sb('rows', [128, 128], F32)
        rows2 = P.sb('rows2', [128, 128], F32)
        cols = P.sb('cols', [128, 512], F32)
        P.op('pool', lambda e: e.memset(rows[:], 0.0), writes=['rows'])
        P.op('pool', lambda e: e.memset(rows2[:], 0.0), writes=['rows2'])
        off = 0
        entries = {}
        rnames = []
        for nm, ap, n in [('c', d['c'], 8), ('g10', d['norm1_g'][0], 8), ('g11', d['norm1_g'][1], 8),
                          ('g20', d['norm2_g'][0], 8), ('g21', d['norm2_g'][1], 8), ('gf', d['final_g'], 8),
                          ('bm0', d['b_mod'][0], 48)]:
            entries[nm] = (off, n)
            P.dma('sp', rows[off:off + n, :], ap, reads=['rows'], writes=[('rows', nm)], slot='r%d' % (len(rnames) % 4))
            rnames.append(('rows', nm))
            off += n
        P.dma('sp', rows2[0:48, :], d['b_mod'][1], reads=['rows2'], writes=[('rows2', 0)], slot='r0')
        P.op('pe', lambda e: e.transpose(K.pb[0][:, 0:128], rows[:, :], identf[:]), reads=rnames + ['rows', 'k_identf'], writes=['ps0'])
        P.op('dve', lambda e: e.tensor_copy(out=cols[:, 0:128], in_=K.pb[0][:, 0:128]), reads=['ps0'], writes=['colsA'])
        P.op('pe', lambda e: e.transpose(K.pb[1][:, 0:128], rows2[:, :], identf[:]), reads=[('rows2', 0), 'rows2', 'k_identf'], writes=['ps1'])
        P.op('dve', lambda e: e.tensor_copy(out=cols[:, 128:256], in_=K.pb[1][:, 0:128]), reads=['ps1'], writes=['colsB'])

        def colof(nm):
            o, n = entries[nm]
            return cols[:, o:o + n]
        for l in range(2):
            rc = P.sb('rc%d' % l, [128, 128], F32)
            P.op('pool', lambda e, rc=rc: e.memset(rc[:], 0.0), writes=[('rc', l)])
            for j in range(2):
                P.dma('sp', rc[44 * j:44 * j + 44, :], d['conv_w'][l, j], reads=[('rc', l)], writes=[('rc', l, j)], slot='r2')
            rd = P.sb('rd%d' % l, [128, 128], F32)
            P.op('pool', lambda e, rd=rd: e.memset(rd[:], 0.0), writes=[('rd', l)])
            P.dma('sp', rd[0:44, :], d['conv_w'][l, 2], reads=[('rd', l)], writes=[('rd', l, 0)], slot='r3')
            P.dma('sp', rd[44:88, :], d['conv_b'][l], reads=[('rd', l)], writes=[('rd', l, 1)], slot='r3')
            P.op('pe', lambda e, rc=rc: e.transpose(K.pb[2][:, 0:128], rc[:, :], identf[:]),
                 reads=[('rc', l), ('rc', l, 0), ('rc', l, 1), 'k_identf'], writes=['ps2'])
            P.op('dve', lambda e, l=l: e.tensor_copy(out=K.cvw[:, l, 0:2, :], in_=K.pb[2][:, 0:88].rearrange("p (a b) -> p a b", a=2)),
                 reads=['ps2'], writes=['cvw'])
            P.op('pe', lambda e, rd=rd: e.transpose(K.pb[3][:, 0:128], rd[:, :], identf[:]),
                 reads=[('rd', l), ('rd', l, 0), ('rd', l, 1), 'k_identf'], writes=['ps3'])
            P.op('dve', lambda e, l=l: e.tensor_copy(out=K.cvw[:, l, 2:4, :], in_=K.pb[3][:, 0:88].rearrange("p (a b) -> p a b", a=2)),
                 reads=['ps3'], writes=['cvw'])

        sc = P.sb('sc', [128, 8], F32)
        P.op('act', lambda e: e.activation(out=sc[:], in_=colof('c'), func=AF.Silu), reads=['colsA'], writes=['sc'])
        mrow = [P.sb('mrow%d' % i, [1, 512], F32) for i in range(2)]
        wm = [P.sb('wm%d' % i, [128, KC, 512], F32) for i in range(2)]
        pieces = [(l, j) for l in range(2) for j in range(12)]
        one1 = K.k['k_one1']

        def ld(i):
            l, j = pieces[i]
            ldflat(K, wm[i % 2][:].rearrange("p k n -> p (k n)"), d['wmod_l'][l, j], ('wm', i % 2), eng='sp')
        ld(0)
        for i, (l, j) in enumerate(pieces):
            if i + 1 < len(pieces):
                ld(i + 1)
            ps = K.pb[4 + (i % 2)]
            psn = 'ps%d' % (4 + (i % 2))
            for kc in range(KC):
                mm(P, ps[0:1, :], sc[:, kc:kc + 1], wm[i % 2][:, kc, :], kc == 0, kc == KC - 1,
                   reads=['sc', ('wm', i % 2)], writes=[psn])
            mr = mrow[i % 2]
            P.op('act', lambda e, ps=ps, mr=mr: e.copy(out=mr[0:1, :], in_=ps[0:1, :]), reads=[psn], writes=[('mrow', i % 2)])
            for q_ in range(4):
                col = l * 48 + 4 * j + q_
                mm(P, K.pb[6][:, col:col + 1], mr[0:1, 128 * q_:128 * q_ + 128], one1[0:1, 0:1], True, True,
                   reads=[('mrow', i % 2), 'k_one1'], writes=['ps6'])
        bmc = [colof('bm0'), cols[:, 128:176]]
        for l in range(2):
            P.op('dve', lambda e, l=l: e.tensor_tensor(out=K.modc[:, l, :], in0=K.pb[6][:, l * 48:l * 48 + 48], in1=bmc[l], op=ALU.add),
                 reads=['ps6', 'colsA', 'colsB'], writes=['modc'])
        for l in range(2):
            for w, gname, sidx in ((0, 'g1%d' % l, 1), (1, 'g2%d' % l, 4)):
                P.op('dve', lambda e, l=l, w=w, gname=gname, sidx=sidx: e.scalar_tensor_tensor(
                    out=K.gs[:, l, w, :], in0=K.modc[:, l, 8 * sidx:8 * sidx + 8], scalar=1.0, in1=colof(gname),
                    op0=ALU.add, op1=ALU.mult), reads=['modc', 'colsA'], writes=['gs'])
                P.op('dve', lambda e, l=l, w=w: e.tensor_scalar(out=K.gs[:, l, w, :], in0=K.gs[:, l, w, :], scalar1=32.0, scalar2=None,
                                                             op0=ALU.mult), reads=['gs'], writes=['gs'])
        P.op('dve', lambda e: e.tensor_scalar(out=K.gfin[:], in0=colof('gf'), scalar1=32.0, scalar2=None, op0=ALU.mult),
             reads=['colsA'], writes=['gfin'])
        tap_any(K, 'modc', K.modc[:].rearrange("p a b -> p (a b)"), ['modc'], [128, 96], F32)

        xin = [P.sb('xin%d' % i, [128, D], F32) for i in range(2)]
        for t in range(NT):
            xi = xin[t % 2]
            P.dma('sp', xi[:], d['x'][128 * t:128 * t + 128, :], writes=[('xin', t % 2)], slot='x%d' % (t % 2))
            for half in range(2):
                ps = K.pb[(2 * t + half) % 4]
                psn = 'ps%d' % ((2 * t + half) % 4)
                for j in range(4):
                    kc = half * 4 + j
                    P.op('pe', lambda e, ps=ps, xi=xi, kc=kc, j=j: e.transpose(ps[:, 128 * j:128 * j + 128], xi[:, 128 * kc:128 * kc + 128], identf[:]),
                         reads=[('xin', t % 2), 'k_identf'], writes=[psn])
                eng = 'act' if half == 0 else 'dve'
                if eng == 'act':
                    P.op('act', lambda e, ps=ps, half=half, t=t: e.copy(out=K.xT[:, half * 4:half * 4 + 4, 128 * t:128 * t + 128],
                                                                         in_=ps[:, :].rearrange("p (a b) -> p a b", a=4)),
                         reads=[psn], writes=['xT'])
                else:
                    P.op('dve', lambda e, ps=ps, half=half, t=t: e.tensor_copy(out=K.xT[:, half * 4:half * 4 + 4, 128 * t:128 * t + 128],
                                                                                in_=ps[:, :].rearrange("p (a b) -> p a b", a=4)),
                         reads=[psn], writes=['xT'])
        P.barrier()
        P.emit()
    P.stack = _prev_stack


def norm_A(K, c, st_tiles):
    P = K.P
    sq, rstds, tmp = st_tiles
    rstd = rstds[c % 2]
    ps = K.pb[c % 2]
    psn = 'ps%d' % (c % 2)
    onesf = K.k['k_onesf']
    for kc in range(KC):
        s_ = sq[kc % len(sq)]
        if kc % 2 == 0:
            P.op('act', lambda e, s_=s_, kc=kc: e.activation(out=s_[:], in_=K.xT[:, kc, CH * c:CH * c + CH], func=AF.Square),
                 reads=['xT'], writes=[('sq', kc % len(sq))])
        else:
            P.op('pool', lambda e, s_=s_, kc=kc: e.tensor_tensor(out=s_[:], in0=K.xT[:, kc, CH * c:CH * c + CH],
                                                                 in1=K.xT[:, kc, CH * c:CH * c + CH], op=ALU.mult),
                 reads=['xT'], writes=[('sq', kc % len(sq))])
        mm(P, ps[:, :], onesf[:], s_[:], kc == 0, kc == KC - 1, reads=['k_onesf', ('sq', kc % len(sq))], writes=[psn])
    P.op('act', lambda e: e.activation(out=rstd[:], in_=ps[:, :], func=AF.Sqrt, bias=K.epsb[:, 0:1], scale=1.0),
         reads=[psn, 'epsb'], writes=[('rstd', c % 2)])
    P.op('dve', lambda e: e.reciprocal(out=rstd[:], in_=rstd[:]), reads=[('rstd', c % 2)], writes=[('rstd', c % 2)])


def norm_B(K, c, gcol, out_fn, st_tiles):
    P = K.P
    sq, rstds, tmp = st_tiles
    rstd = rstds[c % 2]
    for kc in range(KC):
        t_ = tmp[kc % len(tmp)]
        P.op('dve', lambda e, t_=t_, kc=kc: e.scalar_tensor_tensor(out=t_[:], in0=K.xT[:, kc, CH * c:CH * c + CH], scalar=gcol[:, kc:kc + 1],
                                                                 in1=rstd[:], op0=ALU.mult, op1=ALU.mult),
             reads=['xT', ('rstd', c % 2), 'gs', 'gfin'], writes=[('ntmp', kc % len(tmp))])
        out_fn(kc, t_, ('ntmp', kc % len(tmp)))


def norm_all(K, gcol, make_out_fn, st_tiles, after_chunk=None):
    norm_A(K, 0, st_tiles)
    for c in range(NCH):
        if c + 1 < NCH:
            norm_A(K, c + 1, st_tiles)
        norm_B(K, c, gcol, make_out_fn(c), st_tiles)
        if after_chunk:
            after_chunk(c)


def phase_norm(K, layer, which):
    P = K.P
    _prev_stack = P.stack
    with ExitStack() as st:
        P.stack = st
        sq = [P.sb('sq%d' % i, [128, CH], F32) for i in range(4)]
        rstd = [P.sb('rstd%d' % i, [128, CH], F32) for i in range(2)]
        tmp = [P.sb('ntmp%d' % i, [128, CH], F32) for i in range(4)]
        K.epsb = P.sb('epsb', [128, 1], F32)
        P.op('pool', lambda e: e.memset(K.epsb[:], 1024.0 * 1e-6), writes=['epsb'])
        gcol = K.gs[:, layer, which, :]
        sh = K.modc[:, layer, (0 if which == 0 else 24):(8 if which == 0 else 32)]

        def make_out_fn(c):
            def out_fn(kc, t_, tname):
                P.op('act', lambda e: e.activation(out=K.hT[:, kc, CH * c:CH * c + CH], in_=t_[:], func=AF.Identity,
                                                   bias=sh[:, kc:kc + 1], scale=1.0),
                     reads=[tname, 'modc'], writes=['hT'])
            return out_fn
        norm_all(K, gcol, make_out_fn, (sq, rstd, tmp))
        P.barrier()
        P.emit()
    P.stack = _prev_stack


def park_x(K):
    P = K.P
    P.dma('sp', K.xpark, K.arena[:], reads=['xT'], writes=['xpark'], slot='park')
    P.barrier()


def unpark_x(K):
    P = K.P
    P.barrier()
    P.dma('sp', K.arena[:], K.xpark, reads=['xpark'], writes=['xT'], slot='park')


def rope_tile(K, ps, psn, tok0, ntok, out_ap, out_name, W, nope_ap=None, nope_name=None, i=0):
    P = K.P
    raw, t1, t2 = W['raw'][i % 2], W['t1'][i % 2], W['t2'][i % 2]
    rn, t1n, t2n = ('raw', i % 2), ('t1', i % 2), ('t2', i % 2)
    swb = W.get('swbanks', (7,))[i % len(W.get('swbanks', (7,)))]
    psw = K.pb[swb]
    if nope_ap is not None:
        P.op('act', lambda e: e.copy(out=nope_ap, in_=ps[:, 0:ntok]), reads=[psn], writes=[nope_name])
        rawap, rawname = nope_ap, nope_name
    else:
        P.op('act', lambda e: e.copy(out=raw[:, 0:ntok], in_=ps[:, 0:ntok]), reads=[psn], writes=[rn])
        rawap, rawname = raw[:, 0:ntok], rn
    P.op('dve', lambda e: e.tensor_tensor(out=t1[:, 0:ntok], in0=ps[:, 0:ntok], in1=K.k['k_cos'][:, tok0:tok0 + ntok], op=ALU.mult),
         reads=[psn, 'k_cos'], writes=[t1n])
    mm(P, psw[:, 0:ntok], K.k['k_pm'][:], rawap, True, True, reads=['k_pm', rawname], writes=['ps%d' % swb])
    P.op('dve', lambda e: e.tensor_tensor(out=t2[:, 0:ntok], in0=psw[:, 0:ntok], in1=K.k['k_sin'][:, tok0:tok0 + ntok], op=ALU.mult),
         reads=['ps%d' % swb, 'k_sin'], writes=[t2n])
    P.op('pool', lambda e: e.tensor_tensor(out=out_ap, in0=t1[:, 0:ntok].rearrange(W['inre'], **W['inkw']) if W.get('inre') else t1[:, 0:ntok],
                                           in1=t2[:, 0:ntok].rearrange(W['inre'], **W['inkw']) if W.get('inre') else t2[:, 0:ntok], op=ALU.add),
         reads=[t1n, t2n], writes=[out_name])


def acc_init(K, bank):
    P = K.P
    mm(P, K.pb[bank][:, :], K.k['k_zeros'][:], K.k['k_mc4'][:], True, False, reads=['k_zeros', 'k_mc4'], writes=['ps%d' % bank])


class Pipe:
    def __init__(self, K, depth=2):
        self.K, self.depth, self.pending = K, depth, []

    def unit(self, st_bank, score_mms, nq, pt, ptname, pv_list, pre=None, post=None, first=False):
        K = self.K
        P = K.P
        ps = K.pb[st_bank]
        psn = 'ps%d' % st_bank
        for idx, (lhsT, rhs, c0, n, rd) in enumerate(score_mms):
            mm(P, ps[0:lhsT.shape[1], c0:c0 + n], lhsT, rhs, idx == 0, idx == len(score_mms) - 1, reads=rd, writes=[psn])
        P.op('act', lambda e: e.activation(out=pt[:, 0:nq], in_=ps[:, 0:nq], func=AF.Exp, scale=SCALE), reads=[psn], writes=[ptname])
        self.pending.append((pv_list, pt, ptname, pre, post, first))
        if len(self.pending) > self.depth:
            self._pop()

    def pair(self, ua, ub):
        K = self.K
        P = K.P
        for u in (ua, ub):
            ps, psn = K.pb[u['bank']], 'ps%d' % u['bank']
            for idx in range(u['nqk']):
                lhsT, rhs, c0, n, rd = u['sm'][idx]
                mm(P, ps[0:lhsT.shape[1], c0:c0 + n], lhsT, rhs, idx == 0, False, reads=rd, writes=[psn])
        for u in (ua, ub):
            ps, psn = K.pb[u['bank']], 'ps%d' % u['bank']
            nsm = len(u['sm'])
            for idx in range(u['nqk'], nsm):
                lhsT, rhs, c0, n, rd = u['sm'][idx]
                mm(P, ps[0:lhsT.shape[1], c0:c0 + n], lhsT, rhs, False, idx == nsm - 1, reads=rd, writes=[psn])
        for u in (ua, ub):
            ps, psn = K.pb[u['bank']], 'ps%d' % u['bank']
            pt, nq = u['pt'], u['nq']
            P.op('act', lambda e, pt=pt, ps=ps, nq=nq: e.activation(out=pt[:, 0:nq], in_=ps[:, 0:nq], func=AF.Exp, scale=SCALE),
                 reads=[psn], writes=[u['ptname']])
            self.pending.append((u['pv'], pt, u['ptname'], None, u.get('post'), u.get('first', False)))
        while len(self.pending) > self.depth:
            self._pop()

    def _pop(self):
        K = self.K
        P = K.P
        pv_list, pt, ptname, pre, post, first = self.pending.pop(0)
        if pre:
            pre()
        started = set()
        for (ab, lhsT, lrd, c0, n, a0) in pv_list:
            st_ = first and ab not in started
            started.add(ab)
            mm(P, K.pb[ab][:, a0:a0 + n], lhsT, pt[:, c0:c0 + n], st_, False, reads=lrd + [ptname], writes=['ps%d' % ab])
        if post:
            post()

    def flush(self):
        while self.pending:
            self._pop()


def phase_dilated(K, layer):
    P, d = K.P, K.d
    ab = K.arena[:].bitcast(BF16)
    QT = ab[:, 0:4096].rearrange("p (a s) -> p a s", a=2)
    KT = ab[:, 4096:8192].rearrange("p (a s) -> p a s", a=2)
    Vp = ab[:, 8192:16384].rearrange("p (t h c) -> p t h c", t=NT, h=4)
    af = K.arena[:]
    numacc = af[:, 8192:12288].rearrange("p (a s) -> p a s", a=2)
    denacc = af[:, 12288:16384].rearrange("p (a s) -> p a s", a=2)
    ident = K.k['k_identb']
    _prev_stack = P.stack
    with ExitStack() as st:
        P.stack = st
        wq = [K.pre_dil[0], P.sb('wq1', [128, KC, 256], BF16)]
        wk = [K.pre_dil[1], P.sb('wk1', [128, KC, 256], BF16)]
        wv = [K.pre_dil[2], P.sb('wv1', [128, KC, 256], BF16)]
        W = {'raw': [P.sb('raw%d' % i, [128, CH], BF16) for i in range(2)],
             't1': [P.sb('t1_%d' % i, [128, CH], F32) for i in range(2)],
             't2': [P.sb('t2_%d' % i, [128, CH], F32) for i in range(2)], 'swbanks': (7, 6)}
        pts = [P.sb('pt%d' % i, [128, CH], BF16) for i in range(4)]
        rden = P.sb('rden', [128, S], F32)
        load_scoped_consts(K, ['k_cos', 'k_sin'])
        P.op('pool', lambda e: e.memset(ab[:, 8192:16384], 0.0), writes=['Vp'])

        def ldw(g):
            ldflat(K, wq[g % 2][:].rearrange("p k n -> p (k n)"), d['wina_l'][layer, g, 0], ('wq', g % 2))
            ldflat(K, wk[g % 2][:].rearrange("p k n -> p (k n)"), d['wina_l'][layer, g, 1], ('wk', g % 2))
            ldflat(K, wv[g % 2][:].rearrange("p k n -> p (k n)"), d['wina_l'][layer, g, 2], ('wv', g % 2))
        rt = 0
        for g in range(3):
            if g + 1 < 3:
                ldw(g + 1)
            dil = (1, 4, 16)[g]
            for which, wt, wn, dst, dname in ((0, wq[g % 2], ('wq', g % 2), QT, 'QT'), (1, wk[g % 2], ('wk', g % 2), KT, 'KT')):
                for pr in range(2):
                    for c in range(NCH):
                        bank = rt % 2
                        proj_fm(K, K.pb[bank], 'ps%d' % bank, wt, wn, slice(128 * pr, 128 * pr + 128), CH * c, CH)
                        if dil == 1:
                            out_ap = dst[:, pr, CH * c:CH * c + CH]
                            W['inre'] = None
                        else:
                            per = CH // dil
                            out_ap = dst[:, pr, :].rearrange("p (r m) -> p m r", r=dil)[:, per * c:per * c + per, :]
                            W['inre'] = "p (m r) -> p m r"
                            W['inkw'] = {'r': dil}
                        rope_tile(K, K.pb[bank], 'ps%d' % bank, CH * c, CH, out_ap, dname, W, i=rt)
                        rt += 1
            for t in range(NT):
                bank = 2 + (t % 2)
                ps = K.pb[bank]
                if dil == 1:
                    tok = lambda kc: K.hT[:, kc, 128 * t:128 * t + 128]
                elif dil == 4:
                    r, j = divmod(t, 4)
                    tok = lambda kc, r=r, j=j: K.hT[:, kc, :].rearrange("p (m r) -> p r m", r=4)[:, r, 128 * j:128 * j + 128]
                else:
                    tok = lambda kc, r=t: K.hT[:, kc, :].rearrange("p (m r) -> p r m", r=16)[:, r, :]
                for kc in range(KC):
                    mm(P, ps[:, 0:256], tok(kc), wv[g % 2][:, kc, :], kc == 0, kc == KC - 1, reads=['hT', ('wv', g % 2)], writes=['ps%d' % bank])
                psv = ps[:, 0:256].rearrange("p (a b c) -> p a b c", a=2, b=2)
                P.op('act', lambda e, t=t, psv=psv: e.copy(out=Vp[:, t, 0:4:2, 0:64], in_=psv[:, :, 0, :]), reads=['ps%d' % bank], writes=['Vp'])
                P.op('dve', lambda e, t=t, psv=psv: e.tensor_copy(out=Vp[:, t, 1:4:2, 64:128], in_=psv[:, :, 1, :]), reads=['ps%d' % bank], writes=['Vp'])
            it = 0
            pipe = Pipe(K, depth=2)
            for pr in range(2):
                for c in range(NCH):
                    nb, db = (3, 4) if (pr * NCH + c) % 2 == 0 else (5, 6)
                    units = []
                    for par in range(2):
                        rows = slice(64 * par, 64 * par + 64)
                        h = 2 * pr + par
                        ones = K.k['k_oneslo'] if par == 0 else K.k['k_oneshi']
                        onesn = 'k_oneslo' if par == 0 else 'k_oneshi'
                        if dil == 16:
                            sm = []
                            pv = []
                            for j in range(4):
                                t = 4 * c + j
                                sm.append((KT[rows, pr, 128 * t:128 * t + 128], QT[rows, pr, 128 * t:128 * t + 128], 128 * j, 128, ['KT', 'QT']))
                                pv.append((nb, Vp[:, t, h, :], ['Vp'], 128 * j, 128, 128 * j))
                                pv.append((db, ones[:], [onesn], 128 * j, 128, 128 * j))
                            sm.append((ident[:], K.k['k_mc4'][:], 0, 512, ['k_identb', 'k_mc4']))
                            units.append((sm, 512, pv, par, 4))
                        else:
                            first_t = 0 if dil == 1 else 4 * c
                            for kt in range(max(4 * c - 1, first_t), 4 * c + 4):
                                q0 = max(128 * kt, CH * c)
                                q1 = min(128 * kt + 256, CH * c + CH)
                                nq = q1 - q0
                                sm = [(KT[rows, pr, 128 * kt:128 * kt + 128], QT[rows, pr, q0:q1], 0, nq, ['KT', 'QT'])]
                                m0 = 0 if q0 == 128 * kt else 128
                                sm.append((ident[:], K.k['k_mpair'][:, m0:m0 + nq], 0, nq, ['k_identb', 'k_mpair']))
                                pv = [(nb, Vp[:, kt, h, :], ['Vp'], 0, nq, q0 - CH * c),
                                      (db, ones[:], [onesn], 0, nq, q0 - CH * c)]
                                units.append((sm, nq, pv, par, 1))

                    def pre(nb=nb, db=db):
                        acc_init(K, nb)
                        acc_init(K, db)

                    def post(nb=nb, db=db, pr=pr, c=c):
                        for bank, acc, an in ((nb, numacc, 'numacc'), (db, denacc, 'denacc')):
                            ps = K.pb[bank]
                            if dil == 1:
                                P.op('act', lambda e, ps=ps, acc=acc: e.copy(out=acc[:, pr, CH * c:CH * c + CH], in_=ps[:, :]),
                                     reads=['ps%d' % bank], writes=[an])
                            elif dil == 4:
                                view = acc[:, pr, :].rearrange("p (m r) -> p r m", r=4)[:, c, :]
                                P.op('dve', lambda e, ps=ps, view=view: e.tensor_tensor(out=view, in0=view, in1=ps[:, :], op=ALU.add),
                                     reads=['ps%d' % bank, an], writes=[an])
                            else:
                                view = acc[:, pr, :].rearrange("p (n r) -> p r n", r=16)[:, 4 * c:4 * c + 4, :]
                                P.op('dve', lambda e, ps=ps, view=view: e.tensor_tensor(out=view, in0=view, in1=ps[:, :].rearrange("p (r n) -> p r n", r=4), op=ALU.add),
                                     reads=['ps%d' % bank, an], writes=[an])
                    ua_ = [u for u in units if u[3] == 0]
                    ub_ = [u for u in units if u[3] == 1]
                    assert len(ua_) == len(ub_)
                    for ui, (a_, b_) in enumerate(zip(ua_, ub_)):
                        da = {'bank': it % 3, 'sm': a_[0], 'nq': a_[1], 'pv': a_[2], 'nqk': a_[4], 'pt': pts[it % 4], 'ptname': ('pt', it % 4),
                              'first': ui == 0}
                        it += 1
                        db_ = {'bank': it % 3, 'sm': b_[0], 'nq': b_[1], 'pv': b_[2], 'nqk': b_[4], 'pt': pts[it % 4], 'ptname': ('pt', it % 4),
                               'post': post if ui == len(ua_) - 1 else None}
                        it += 1
                        pipe.pair(da, db_)
            pipe.flush()
        for pr in range(2):
            P.op('dve', lambda e, pr=pr: e.reciprocal(out=rden[:], in_=denacc[:, pr, :]), reads=['denacc'], writes=['rden'])
            P.op('dve', lambda e, pr=pr: e.tensor_tensor(out=K.yaT[:, pr, :], in0=numacc[:, pr, :], in1=rden[:], op=ALU.mult),
                 reads=['numacc', 'rden'], writes=['yaT'])
        P.report('dil')
        tap_any(K, 'ya%d' % layer, K.yaT[:].rearrange("p a s -> p (a s)"), ['yaT'], [128, 2 * S], BF16)
        P.barrier()
        P.emit()
    P.stack = _prev_stack


def phase_nsa(K, layer):
    P, d = K.P, K.d
    wn = d['wnsa_l'][layer]
    ab = K.arena[:].bitcast(BF16)
    QR = ab[:, 0:8192].rearrange("p (a s) -> p a s", a=4)
    QN = ab[:, 8192:16384].rearrange("p (a s) -> p a s", a=4)
    KS = ab[:, 16384:20480].rearrange("p (g s) -> p g s", g=2)
    KW = ab[:, 20480:24576].rearrange("p (g s) -> p g s", g=2)
    VS = ab[:, 24576:32768].rearrange("p (t g v c) -> p t g v c", t=NT, g=2, v=2)
    ident = K.k['k_identb']
    identf = K.k['k_identf']
    _prev_stack = P.stack
    with ExitStack() as st:
        P.stack = st
        VW = P.sb('VW', [128, NT, 2, 2, 128], BF16)
        gateT = P.sb('gateT', [24, S], BF16)
        negT = P.sb('negT', [128, 2, S], BF16)
        kcT2 = P.sb('kcT2', [128, 2, 128], BF16)
        vcp = P.sb('vcp', [128, 2, 2, 128], BF16)
        P.op('pool', lambda e: e.memset(VW[:].rearrange("p t g v c -> p (t g v c)"), 0.0), writes=['VW'])
        P.op('pool', lambda e: e.memset(vcp[:].rearrange("p g v c -> p (g v c)"), 0.0), writes=['vcp'])
        P.op('pool', lambda e: e.memset(kcT2[:].rearrange("p g c -> p (g c)"), 0.0), writes=['kcT2'])
        P.op('pool', lambda e: e.memset(negT[:].rearrange("p g s -> p (g s)"), 0.0), writes=['negT'])

        with ExitStack() as cst:
            P.stack = cst
            wkv = P.sb('wkv', [128, KC, 256], BF16)
            ldflat(K, wkv[:].rearrange("p k n -> p (k n)"), wn[:, 0:2048], 'wkv')
            cmpT = K.arena[:, 0:4096].rearrange("p (a s) -> p a s", a=2)
            for kv in range(2):
                for c in range(NCH):
                    bank = (kv * NCH + c) % 2
                    proj_fm(K, K.pb[bank], 'ps%d' % bank, wkv, 'wkv', slice(128 * kv, 128 * kv + 128), CH * c, CH)
                    P.op('act', lambda e, bank=bank, kv=kv, c=c: e.copy(out=cmpT[:, kv, CH * c:CH * c + CH], in_=K.pb[bank][:, :]),
                         reads=['ps%d' % bank], writes=['cmpT'])
            w1ds = [ab[:, 8192:16384].rearrange("p (l j) -> p l j", l=32), ab[:, 24576:32768].rearrange("p (l j) -> p l j", l=32)]
            kpes = [ab[:, 16384:20480].rearrange("p (l n) -> p l n", l=32), ab[:, 20480:24576].rearrange("p (l n) -> p l n", l=32)]
            w2d = P.sb('w2d', [128, 2, 128], BF16)
            w2v = P.sb('w2v', [128, 2, 64], BF16)
            peTs = [P.sb('peT%d' % i_, [128, 32], F32) for i_ in range(2)]
            hid = P.sb('hid', [128, 4, 128], BF16)
            for kv in range(2):
                ldflat(K, w1ds[kv].rearrange("p l j -> p (l j)"), d['w1d_l'][layer, kv], ('w1d', kv))
                P.dma('sp', peTs[kv][:], d['pe_l'][layer, kv], writes=[('peT', kv)], slot='r%d' % kv)
            ldflat(K, w2d[:].rearrange("p a b -> p (a b)"), d['w2d_l'][layer], ('w2d', 0))
            ldflat(K, w2v[:].rearrange("p a b -> p (a b)"), d['w2v_l'][layer], 'w2v')
            for kv in range(2):
                w1d, kpe, peT = w1ds[kv], kpes[kv], peTs[kv]
                for l in range(32):
                    P.op('dve', lambda e, l=l, kv=kv, kpe=kpe, peT=peT: e.tensor_scalar(
                        out=kpe[:, l, 0:127], in0=cmpT[:, kv, :].rearrange("p (n r) -> p r n", r=16)[:, l % 16, (l // 16):(l // 16) + 127],
                        scalar1=peT[:, l:l + 1], scalar2=None, op0=ALU.add), reads=['cmpT', ('peT', kv)], writes=[('kpe', kv)])
                for g in range(2):
                    rows = slice(64 * g, 64 * g + 64)
                    for jc in range(2):
                        bank = 3 + ((g * 2 + jc) % 2)
                        for l in range(32):
                            mm(P, K.pb[bank][:, 0:127], w1d[rows, l, 128 * jc:128 * jc + 128], kpe[rows, l, 0:127], l == 0, l == 31,
                               reads=[('w1d', kv), ('kpe', kv)], writes=['ps%d' % bank])
                        P.op('act', lambda e, bank=bank, g=g, jc=jc: e.activation(out=hid[:, g * 2 + jc, 0:127], in_=K.pb[bank][:, 0:127], func=AF.Silu),
                             reads=['ps%d' % bank], writes=['hid'])
                    if kv == 0:
                        for jc in range(2):
                            mm(P, K.pb[5][:, 0:127], w2d[:, jc, :], hid[:, g * 2 + jc, 0:127], jc == 0, jc == 1,
                               reads=[('w2d', 0), 'hid'], writes=['ps5'])
                        P.op('act', lambda e, g=g: e.copy(out=kcT2[:, g, 0:127], in_=K.pb[5][:, 0:127]), reads=['ps5'], writes=['kcT2'])
                    else:
                        for jc in range(2):
                            mm(P, K.pb[5][0:127, 0:64], hid[:, g * 2 + jc, 0:127], w2v[:, jc, :], jc == 0, jc == 1,
                               reads=['w2v', 'hid'], writes=['ps5'])
                        P.op('act', lambda e, g=g: e.copy(out=vcp[0:127, g, 0, 0:64], in_=K.pb[5][0:127, 0:64]), reads=['ps5'], writes=['vcp'])
                        P.op('dve', lambda e, g=g: e.tensor_copy(out=vcp[0:127, g, 1, 64:128], in_=K.pb[5][0:127, 0:64]), reads=['ps5'], writes=['vcp'])
            P.barrier()
            P.emit()
        P.stack = st
        P.op('pool', lambda e: e.memset(ab[:, 24576:32768], 0.0), writes=['VS'])
        import os as _os
        _stop = int(_os.environ.get('NSA_STOP', '9'))
        if _stop <= 1:
            P.barrier(); P.emit(); P.stack = _prev_stack
            return

        esb = P.sb('esb', [128, 4, 128], F32)
        rs = P.sb('rs', [128, 8], F32)
        ps4 = P.sb('ps4', [128, 128], F32)
        imp = P.sb('imp', [128, 32], F32)
        imp2 = P.sb('imp2', [128, 32], F32)
        m8 = P.sb('m8', [128, 16], F32)
        nsels = [P.sb('nsel%d' % i_, [128, 32], BF16) for i_ in range(3)]

        def selA(k_, i, g):
            nsel, nsn = nsels[k_ % 3], ('nsel', k_ % 3)
            ps = K.pb[6]
            psn = 'ps6'
            first = True
            for par_ in range(2):
                for r in (par_, par_ + 2):
                    h = 4 * g + r
                    pr, par = divmod(h, 2)
                    rows = slice(64 * par, 64 * par + 64)
                    mm(P, ps[:, 128 * r:128 * r + 128], QN[rows, pr, 128 * i:128 * i + 128], kcT2[rows, g, :], first, False,
                       reads=['QN', 'kcT2'], writes=[psn])
                    first = False
                for r in (par_, par_ + 2):
                    mm(P, ps[:, 128 * r:128 * r + 128], ident[:], K.k['k_mwide'][:, 120 - 8 * i:120 - 8 * i + 128], False, (par_ == 1 and r == 3),
                       reads=['k_identb', 'k_mwide'], writes=[psn])
            P.op('act', lambda e: e.activation(out=esb[:].rearrange("p a b -> p (a b)"), in_=ps[:, :], func=AF.Exp, scale=SCALE),
                 reads=[psn], writes=['esb'])
            P.op('dve', lambda e: e.tensor_reduce(out=rs[:, 0:4], in_=esb[:], axis=AX.X, op=ALU.add), reads=['esb'], writes=['rs'])
            P.op('dve', lambda e: e.reciprocal(out=rs[:, 4:8], in_=rs[:, 0:4]), reads=['rs'], writes=['rs'])
            P.op('dve', lambda e: e.tensor_tensor(out=esb[:], in0=esb[:], in1=rs[:, 4:8].unsqueeze(2).to_broadcast([128, 4, 128]), op=ALU.mult),
                 reads=['esb', 'rs'], writes=['esb'])
            P.op('dve', lambda e: e.tensor_reduce(out=ps4[:], in_=esb[:].rearrange("p h n -> p n h"), axis=AX.X, op=ALU.add),
                 reads=['esb'], writes=['ps4'])
            P.op('dve', lambda e: e.tensor_reduce(out=imp[:], in_=ps4[:].rearrange("p (j f) -> p j f", f=4), axis=AX.X, op=ALU.add),
                 reads=['ps4'], writes=['imp'])
            P.op('dve', lambda e: e.tensor_tensor(out=imp[:, 1:32], in0=imp[:, 1:32], in1=ps4[:, 3:124:4], op=ALU.add),
                 reads=['imp', 'ps4'], writes=['imp'])
            P.op('dve', lambda e: e.tensor_tensor(out=imp[:], in0=imp[:], in1=K.k['k_F'][:, 32 * (i - 8):32 * (i - 8) + 32], op=ALU.max),
                 reads=['imp', 'k_F'], writes=['imp'])
            P.op('dve', lambda e: e.max(out=m8[:, 0:8], in_=imp[:]), reads=['imp'], writes=['m8'])
            P.op('dve', lambda e: e.match_replace(out=imp2[:], in_to_replace=m8[:, 0:8], in_values=imp[:], imm_value=-1e30),
                 reads=['imp', 'm8'], writes=['imp2'])
            P.op('dve', lambda e: e.max(out=m8[:, 8:16], in_=imp2[:]), reads=['imp2'], writes=['m8'])
            P.op('dve', lambda e: e.tensor_scalar(out=nsel[:], in0=imp[:], scalar1=m8[:, 15:16], scalar2=NEG, op0=ALU.is_lt, op1=ALU.mult),
                 reads=['imp', 'm8'], writes=[nsn])

        def selB(k_, i, g):
            nsel, nsn = nsels[k_ % 3], ('nsel', k_ % 3)
            pst = K.pb[4][:].bitcast(BF16)
            P.op('pe', lambda e: e.transpose(pst[0:32, 0:128], nsel[:, :], ident[:]), reads=[nsn, 'k_identb'], writes=['ps4'])
            P.op('act', lambda e: e.copy(out=negT[0:32, g, 128 * i:128 * i + 128], in_=pst[0:32, 0:128]), reads=['ps4'], writes=['negT'])

        sel_list = [(i, g) for i in range(8, NT) for g in range(2)]
        sel_pos = [0]

        def sel_step():
            k_ = sel_pos[0]
            sel_pos[0] += 1
            if 0 <= k_ - 2 < len(sel_list):
                selB(k_ - 2, *sel_list[k_ - 2])
            if k_ < len(sel_list):
                selA(k_, *sel_list[k_])

        sel_ticks = [0]

        def sel_tick():
            sel_ticks[0] += 1
            sel_step()

        with ExitStack() as pst:
            P.stack = pst
            W = {'raw': [P.sb('raw%d' % i, [128, CH], BF16) for i in range(2)],
                 't1': [P.sb('t1_%d' % i, [128, CH], F32) for i in range(2)],
                 't2': [P.sb('t2_%d' % i, [128, CH], F32) for i in range(2)], 'inre': None, 'swbanks': (7, 5)}
            load_scoped_consts(K, ['k_cos', 'k_sin'])
            wbq = P.sb('wbq', [128, KC, 512], BF16)
            wkd = P.sb('wkd', [128, KC, 2, 2, 2, 64], BF16)
            wvv = P.sb('wvv', [128, KC, 2, 128], BF16)
            wbl = P.sb('wbl', [128, KC, 24], BF16)
            ldflat(K, wbq[:].rearrange("p k n -> p (k n)"), wn[:, 2048:6144], 'wbq')
            ldflat(K, wkd[:].rearrange("p k a b c d -> p (k a b c d)"), wn[:, 6144:10240], 'wkd')
            ldflat(K, wvv[:].rearrange("p k a n -> p (k a n)"), wn[:, 10240:12288], 'wvv')
            ldflat(K, wbl[:].rearrange("p k n -> p (k n)"), wn[:, 12288:12480], 'wbl')
            P.report('nsa-proj')
            rt = 0
            for pr in range(4):
                for c in range(NCH):
                    bank = rt % 2
                    proj_fm(K, K.pb[bank], 'ps%d' % bank, wbq, 'wbq', slice(128 * pr, 128 * pr + 128), CH * c, CH)
                    rope_tile(K, K.pb[bank], 'ps%d' % bank, CH * c, CH, QR[:, pr, CH * c:CH * c + CH], 'QR', W,
                              nope_ap=QN[:, pr, CH * c:CH * c + CH], nope_name='QN', i=rt)
                    rt += 1
            for w_, dst, dn in ((0, KS, 'KS'), (1, KW, 'KW')):
                for g in range(2):
                    for c in range(NCH):
                        bank = rt % 2
                        for kc in range(KC):
                            mm(P, K.pb[bank][:, :], wkd[:, kc, w_, g, :, :].rearrange("p a b -> p (a b)"), K.hT[:, kc, CH * c:CH * c + CH],
                               kc == 0, kc == KC - 1, reads=['wkd', 'hT'], writes=['ps%d' % bank])
                        rope_tile(K, K.pb[bank], 'ps%d' % bank, CH * c, CH, dst[:, g, CH * c:CH * c + CH], dn, W, i=rt)
                        rt += 1
                        sel_tick()
            for w_, dst, dn in ((0, VS, 'VS'), (1, VW[:], 'VW')):
                for t in range(NT):
                    bank = 2 + (t % 2)
                    for kc in range(KC):
                        mm(P, K.pb[bank][:, 0:128], K.hT[:, kc, 128 * t:128 * t + 128], wvv[:, kc, w_, :], kc == 0, kc == KC - 1,
                           reads=['hT', 'wvv'], writes=['ps%d' % bank])
                    psv = K.pb[bank][:, 0:128].rearrange("p (g dd) -> p g dd", g=2)
                    P.op('act', lambda e, dst=dst, t=t, psv=psv: e.copy(out=dst[:, t, :, 0, 0:64], in_=psv), reads=['ps%d' % bank], writes=[dn])
                    P.op('dve', lambda e, dst=dst, t=t, psv=psv: e.tensor_copy(out=dst[:, t, :, 1, 64:128], in_=psv), reads=['ps%d' % bank], writes=[dn])
                    sel_tick()
            for c in range(NCH):
                bank = 4 + (c % 2)
                for kc in range(KC):
                    mm(P, K.pb[bank][0:24, :], wbl[:, kc, :], K.hT[:, kc, CH * c:CH * c + CH], kc == 0, kc == KC - 1,
                       reads=['wbl', 'hT'], writes=['ps%d' % bank])
                P.op('act', lambda e, bank=bank, c=c: e.activation(out=gateT[:, CH * c:CH * c + CH], in_=K.pb[bank][0:24, :], func=AF.Sigmoid),
                     reads=['ps%d' % bank], writes=['gateT'])
            while sel_pos[0] < len(sel_list) + 2:
                sel_step()
            P.barrier()
            P.emit()
        P.stack = st

        P.report('nsa-afterproj')
        if _stop <= 2:
            P.barrier(); P.emit(); P.stack = _prev_stack
            return
        if _stop <= 3:
            P.barrier(); P.emit(); P.stack = _prev_stack
            return
        load_scoped_consts(K, ['k_cneg', 'k_E'])
        pts = [P.sb('pt%d' % i, [128, CH], BF16) for i in range(4)]
        rden = P.sb('rden', [128, CH], F32)
        wgt = P.sb('wgt', [128, CH], F32)
        yacc = P.sb('yacc', [128, CH], F32)
        ytmp = P.sb('ytmp', [128, CH], F32)
        E = K.k['k_E']
        it = 0
        pc = 0
        pipe = Pipe(K, depth=2)
        for pr in range(4):
            g = pr // 2
            for c in range(NCH):
                for br in range(3):
                    nb, db = (3, 4) if pc % 2 == 0 else (5, 6)
                    pc += 1
                    units = []
                    for par in range(2):
                        rows = slice(64 * par, 64 * par + 64)
                        ones = K.k['k_oneslo'] if par == 0 else K.k['k_oneshi']
                        onesn = 'k_oneslo' if par == 0 else 'k_oneshi'
                        if br == 0:
                            sm = [(kcT2[rows, g, :], QN[rows, pr, CH * c:CH * c + CH], 0, CH, ['kcT2', 'QN']),
                                  (ident[:], K.k['k_cneg'][:, CH * c:CH * c + CH], 0, CH, ['k_identb', 'k_cneg'])]
                            pv = [(nb, vcp[:, g, par, :], ['vcp'], 0, CH, 0), (db, ones[:], [onesn], 0, CH, 0)]
                            units.append((sm, CH, pv, par, 1))
                        else:
                            Kt = KS if br == 1 else KW
                            Kn = 'KS' if br == 1 else 'KW'
                            Vt = VS if br == 1 else VW
                            Vn = 'VS' if br == 1 else 'VW'
                            kt0 = 0 if br == 1 else max(0, 4 * c - 4)
                            for kt in range(kt0, 4 * c + 4):
                                q0 = max(128 * kt, CH * c)
                                q1 = CH * c + CH if br == 1 else min(128 * kt + 640, CH * c + CH)
                                nq = q1 - q0
                                sm = [(Kt[rows, g, 128 * kt:128 * kt + 128], QR[rows, pr, q0:q1], 0, nq, [Kn, 'QR'])]
                                if br == 1 and c >= 2:
                                    sm.append((E[:, 128 * kt:128 * kt + 128], negT[:, g, q0:q1], 0, nq, ['k_E', 'negT']))
                                if q0 == 128 * kt:
                                    sm.append((ident[:], K.k['k_mpair'][:, 0:128], 0, 128, ['k_identb', 'k_mpair']))
                                if br == 2 and q1 == 128 * kt + 640:
                                    sm.append((ident[:], K.k['k_mfar'][:], nq - 128, 128, ['k_identb', 'k_mfar']))
                                pv = [(nb, Vt[:, kt, g, par, :], [Vn], 0, nq, q0 - CH * c), (db, ones[:], [onesn], 0, nq, q0 - CH * c)]
                                units.append((sm, nq, pv, par, 1))

                    def pre(nb=nb, db=db):
                        acc_init(K, nb)
                        acc_init(K, db)

                    def post(nb=nb, db=db, pr=pr, c=c, br=br):
                        mm(P, K.pb[7][:, :], K.k['k_selg'][:, 128 * (pr * 3 + br):128 * (pr * 3 + br) + 128], gateT[:, CH * c:CH * c + CH], True, True,
                           reads=['k_selg', 'gateT'], writes=['ps7'])
                        if br == 0:
                            P.op('dve', lambda e: e.tensor_scalar(out=rden[:], in0=K.pb[db][:, :], scalar1=1e-30, scalar2=None, op0=ALU.max),
                                 reads=['ps%d' % db], writes=['rden'])
                            P.op('dve', lambda e: e.reciprocal(out=rden[:], in_=rden[:]), reads=['rden'], writes=['rden'])
                        else:
                            P.op('dve', lambda e: e.reciprocal(out=rden[:], in_=K.pb[db][:, :]), reads=['ps%d' % db], writes=['rden'])
                        P.op('dve', lambda e: e.tensor_tensor(out=wgt[:], in0=K.pb[7][:, :], in1=rden[:], op=ALU.mult), reads=['ps7', 'rden'], writes=['wgt'])
                        if br == 0:
                            P.op('dve', lambda e: e.tensor_tensor(out=yacc[:], in0=K.pb[nb][:, :], in1=wgt[:], op=ALU.mult),
                                 reads=['ps%d' % nb, 'wgt'], writes=['yacc'])
                        elif br == 1:
                            P.op('dve', lambda e: e.tensor_tensor(out=ytmp[:], in0=K.pb[nb][:, :], in1=wgt[:], op=ALU.mult),
                                 reads=['ps%d' % nb, 'wgt'], writes=['ytmp'])
                            P.op('pool', lambda e: e.tensor_tensor(out=yacc[:], in0=yacc[:], in1=ytmp[:], op=ALU.add), reads=['yacc', 'ytmp'], writes=['yacc'])
                        else:
                            P.op('dve', lambda e: e.tensor_tensor(out=ytmp[:], in0=K.pb[nb][:, :], in1=wgt[:], op=ALU.mult),
                                 reads=['ps%d' % nb, 'wgt'], writes=['ytmp'])
                            P.op('pool', lambda e: e.tensor_tensor(out=K.ybT[:, pr, CH * c:CH * c + CH], in0=yacc[:], in1=ytmp[:], op=ALU.add),
                                 reads=['yacc', 'ytmp'], writes=['ybT'])
                    ua_ = [u for u in units if u[3] == 0]
                    ub_ = [u for u in units if u[3] == 1]
                    assert len(ua_) == len(ub_)
                    for ui, (a_, b_) in enumerate(zip(ua_, ub_)):
                        da = {'bank': it % 3, 'sm': a_[0], 'nq': a_[1], 'pv': a_[2], 'nqk': a_[4], 'pt': pts[it % 4], 'ptname': ('pt', it % 4),
                              'first': ui == 0}
                        it += 1
                        db_ = {'bank': it % 3, 'sm': b_[0], 'nq': b_[1], 'pv': b_[2], 'nqk': b_[4], 'pt': pts[it % 4], 'ptname': ('pt', it % 4),
                               'post': post if ui == len(ua_) - 1 else None}
                        it += 1
                        pipe.pair(da, db_)
        pipe.flush()
        P.report('nsa-attn')
        tap_any(K, 'yb%d' % layer, K.ybT[:].rearrange("p a s -> p (a s)"), ['ybT'], [128, 4 * S], BF16)
        P.barrier()
        P.emit()
    P.stack = _prev_stack


def phase_merge(K, layer):
    P, d = K.P, K.d
    _prev_stack = P.stack
    with ExitStack() as st:
        P.stack = st
        mT = P.sb('mT', [128, KC, S], BF16)
        wmg = [P.sb('wmg%d' % i, [128, 22, 128], BF16) for i in range(2)]
        wo = [P.sb('wo%d' % i, [128, KC, 128], BF16) for i in range(2)]
        sga = [P.sb('sga%d' % i, [128, CH], F32) for i in range(1)]
        sgb = [P.sb('sgb%d' % i, [128, CH], F32) for i in range(1)]
        t1 = [P.sb('mt1_%d' % i, [128, CH], F32) for i in range(1)]
        t2 = [P.sb('mt2_%d' % i, [128, CH], F32) for i in range(1)]

        def ldf(f):
            ldflat(K, wmg[f % 2][:].rearrange("p k n -> p (k n)"), d['wmrg_l'][layer, f], ('wmg', f % 2))
        ldf(0)
        it = 0
        for f in range(KC):
            if f + 1 < KC:
                ldf(f + 1)
            s = f % 2
            for c in range(NCH):
                j = 0
                tk = slice(CH * c, CH * c + CH)
                for kc in range(KC):
                    mm(P, K.pb[0][:, :], wmg[s][:, kc, :], K.hT[:, kc, tk], kc == 0, kc == KC - 1, reads=[('wmg', s), 'hT'], writes=['ps0'])
                P.op('act', lambda e, j=j: e.activation(out=sga[j][:], in_=K.pb[0][:, :], func=AF.Sigmoid), reads=['ps0'], writes=[('sga', j)])
                for kc in range(KC):
                    mm(P, K.pb[1][:, :], wmg[s][:, 8 + kc, :], K.hT[:, kc, tk], kc == 0, kc == KC - 1, reads=[('wmg', s), 'hT'], writes=['ps1'])
                P.op('act', lambda e, j=j: e.activation(out=sgb[j][:], in_=K.pb[1][:, :], func=AF.Sigmoid), reads=['ps1'], writes=[('sgb', j)])
                for kc in range(2):
                    mm(P, K.pb[2][:, :], wmg[s][:, 16 + kc, :], K.yaT[:, kc, tk], kc == 0, kc == 1, reads=[('wmg', s), 'yaT'], writes=['ps2'])
                P.op('dve', lambda e, j=j: e.tensor_tensor(out=t1[j][:], in0=K.pb[2][:, :], in1=sga[j][:], op=ALU.mult),
                     reads=['ps2', ('sga', j)], writes=[('mt1', j)])
                for kc in range(4):
                    mm(P, K.pb[3][:, :], wmg[s][:, 18 + kc, :], K.ybT[:, kc, tk], kc == 0, kc == 3, reads=[('wmg', s), 'ybT'], writes=['ps3'])
                P.op('dve', lambda e, j=j: e.tensor_tensor(out=t2[j][:], in0=K.pb[3][:, :], in1=sgb[j][:], op=ALU.mult),
                     reads=['ps3', ('sgb', j)], writes=[('mt2', j)])
                P.op('pool', lambda e, j=j, f=f, tk=tk: e.tensor_tensor(out=mT[:, f, tk], in0=t1[j][:], in1=t2[j][:], op=ALU.add),
                     reads=[('mt1', j), ('mt2', j)], writes=['mT'])
                it += 1
        P.report('merge')
        ldflat(K, wo[0][:].rearrange("p k n -> p (k n)"), d['wout_l'][layer, 0], ('wo', 0))
        for f in range(KC):
            if f + 1 < KC:
                ldflat(K, wo[(f + 1) % 2][:].rearrange("p k n -> p (k n)"), d['wout_l'][layer, f + 1], ('wo', (f + 1) % 2))
            s = f % 2
            for c in range(NCH):
                bank = 4 + (c % 2)
                tk = slice(CH * c, CH * c + CH)
                for kc in range(KC):
                    mm(P, K.pb[bank][:, :], wo[s][:, kc, :], mT[:, kc, tk], kc == 0, kc == KC - 1, reads=[('wo', s), 'mT'], writes=['ps%d' % bank])
                P.op('dve', lambda e, bank=bank, f=f, tk=tk: e.scalar_tensor_tensor(
                    out=K.xT[:, f, tk], in0=K.pb[bank][:, :], scalar=K.modc[:, layer, 16 + f:17 + f], in1=K.xT[:, f, tk],
                    op0=ALU.mult, op1=ALU.add), reads=['ps%d' % bank, 'xT', 'modc'], writes=['xT'])


def phase_ffn(K, layer):
    P, d = K.P, K.d
    HALF = S // 2
    _prev_stack = P.stack
    with ExitStack() as st:
        P.stack = st
        actT = P.sb('actT', [128, NFF, HALF], BF16)
        NWU = 4
        wup = K.pre_wup
        NWD = 3
        wd = [P.sb('wd%d' % i, [128, NFF, 128], BF16) for i in range(NWD)]
        ub = [[P.sb('ub%d_%d' % (gv, i), [128, CH + 2], F32) for i in range(2)] for gv in range(2)]
        tt = [[P.sb('tt%d_%d' % (gv, i), [128, CH], F32) for i in range(2)] for gv in range(2)]
        sg = [P.sb('sgt%d' % i, [128, CH], F32) for i in range(1)]
        P.op('pool', lambda e: e.memset(K.halo[:].rearrange("p a b -> p (a b)"), 0.0), writes=['halo'])

        def ldu(fc):
            s = fc % NWU
            ldflat(K, wup[s][:].rearrange("p k a n -> p (k a n)"), d['wup_l'][layer, fc], ('wup', s))

        def ldd(f):
            ldflat(K, wd[f % NWD][:].rearrange("p k n -> p (k n)"), d['wdn_l'][layer, f], ('wd', f % NWD))
        it = 0
        tail = [None]
        for hf in range(2):
            if hf > 0:
                for f_ in range(NWU - 1):
                    ldu(f_)
            for fc in range(NFF):
                if fc + NWU - 1 < NFF:
                    ldu(fc + NWU - 1)
                if fc == NFF - 4:
                    ldd(0)
                if fc == NFF - 2:
                    ldd(1)
                s = fc % NWU
                for cc in range(2):
                    c = 2 * hf + cc
                    tk = slice(CH * c, CH * c + CH)
                    j = it % 2
                    it += 1
                    for gv in range(2):
                        bank = (0, 2, 6)[(it - 1) % 3] + gv
                        ch = fc + NFF * gv
                        for kc in range(KC):
                            mm(P, K.pb[bank][:, :], wup[s][:, kc, gv, :], K.hT[:, kc, tk], kc == 0, kc == KC - 1,
                               reads=[('wup', s), 'hT'], writes=['ps%d' % bank])
                        u = ub[gv][j]
                        un = ('ub', gv, j)
                        t = tt[gv][j]
                        tn = ('tt', gv, j)
                        P.op('act', lambda e, u=u, bank=bank: e.copy(out=u[:, 2:CH + 2], in_=K.pb[bank][:, :]), reads=['ps%d' % bank], writes=[un])
                        P.op('pool', lambda e, u=u, ch=ch: e.tensor_copy(out=u[:, 0:2], in_=K.halo[:, ch, :]), reads=['halo'], writes=[un])
                        P.op('pool', lambda e, u=u, ch=ch: e.tensor_copy(out=K.halo[:, ch, :], in_=u[:, CH:CH + 2]), reads=[un], writes=['halo'])
                        cw = K.cvw[:, layer, :, ch:ch + 1]
                        P.op('act', lambda e, t=t, cw=cw, bank=bank: e.activation(out=t[:], in_=K.pb[bank][:, :], func=AF.Identity,
                                                                                  scale=cw[:, 2, :], bias=cw[:, 3, :]),
                             reads=['ps%d' % bank, 'cvw'], writes=[tn])
                        P.op('dve', lambda e, u=u, t=t, cw=cw: e.scalar_tensor_tensor(out=t[:], in0=u[:, 1:CH + 1], scalar=cw[:, 1, :], in1=t[:],
                                                                                      op0=ALU.mult, op1=ALU.add), reads=[un, tn, 'cvw'], writes=[tn])
                        P.op('dve', lambda e, u=u, t=t, cw=cw: e.scalar_tensor_tensor(out=t[:], in0=u[:, 0:CH], scalar=cw[:, 0, :], in1=t[:],
                                                                                      op0=ALU.mult, op1=ALU.add), reads=[un, tn, 'cvw'], writes=[tn])
                    if tail[0] is not None:
                        tail[0]()

                    def _tail(j=j, fc=fc, cc=cc):
                        P.op('act', lambda e: e.activation(out=sg[0][:], in_=tt[0][j][:], func=AF.Silu), reads=[('tt', 0, j)], writes=[('sgt', 0)])
                        P.op('pool', lambda e: e.tensor_tensor(out=actT[:, fc, CH * cc:CH * cc + CH], in0=sg[0][:], in1=tt[1][j][:], op=ALU.mult),
                             reads=[('sgt', 0), ('tt', 1, j)], writes=['actT'])
                    tail[0] = _tail
            tail[0]()
            tail[0] = None
            for f in range(KC):
                if f + 2 < KC:
                    ldd(f + 2)
                s = f % NWD
                for cc in range(2):
                    c = 2 * hf + cc
                    bank = 4 + cc
                    tk = slice(CH * c, CH * c + CH)
                    for k2 in range(NFF):
                        mm(P, K.pb[bank][:, :], wd[s][:, k2, :], actT[:, k2, CH * cc:CH * cc + CH], k2 == 0, k2 == NFF - 1,
                           reads=[('wd', s), 'actT'], writes=['ps%d' % bank])
                    P.op('dve', lambda e, bank=bank, f=f, tk=tk: e.scalar_tensor_tensor(
                        out=K.xT[:, f, tk], in0=K.pb[bank][:, :], scalar=K.modc[:, layer, 40 + f:41 + f], in1=K.xT[:, f, tk],
                        op0=ALU.mult, op1=ALU.add), reads=['ps%d' % bank, 'xT', 'modc'], writes=['xT'])
        P.report('ffn')
        P.barrier()
        P.emit()
    P.stack = _prev_stack


def phase_final(K):
    P = K.P
    identf = K.k['k_identf']
    _prev_stack = P.stack
    with ExitStack() as st:
        P.stack = st
        sq = [P.sb('sq%d' % i, [128, CH], F32) for i in range(4)]
        rstd = [P.sb('rstd%d' % i, [128, CH], F32) for i in range(2)]
        tmp = [P.sb('ntmp%d' % i, [128, CH], F32) for i in range(4)]
        K.epsb = P.sb('epsb', [128, 1], F32)
        P.op('pool', lambda e: e.memset(K.epsb[:], 1024.0 * 1e-6), writes=['epsb'])
        ot = [P.sb('ot%d' % i, [128, D], F32) for i in range(4)]

        def make_out_fn(c):
            def out_fn(kc, t_, tname):
                bank = 2 + (kc % 4)
                for j in range(4):
                    P.op('pe', lambda e, j=j: e.transpose(K.pb[bank][:, 128 * j:128 * j + 128], t_[:, 128 * j:128 * j + 128], identf[:]),
                         reads=[tname, 'k_identf'], writes=['ps%d' % bank])
                dstv = [ot[j][:, 128 * kc:128 * kc + 128] for j in range(4)]
                for j in range(4):
                    if kc % 2 == 0:
                        P.op('act', lambda e, j=j: e.copy(out=dstv[j], in_=K.pb[bank][:, 128 * j:128 * j + 128]), reads=['ps%d' % bank], writes=[('ot', j)])
                    else:
                        P.op('dve', lambda e, j=j: e.tensor_copy(out=dstv[j], in_=K.pb[bank][:, 128 * j:128 * j + 128]), reads=['ps%d' % bank], writes=[('ot', j)])
            return out_fn

        def after_chunk(c):
            for j in range(4):
                t = 4 * c + j
                P.dma('sp', K.out[128 * t:128 * t + 128, :], ot[j][:], reads=[('ot', j)], slot='o%d' % j)
        norm_all(K, K.gfin[:], make_out_fn, (sq, rstd, tmp), after_chunk)
        P.barrier()
        P.emit()
    P.stack = _prev_stack


_CACHE = {}


def _prep_inputs(inputs):
    f = lambda a: np.ascontiguousarray(np.asarray(a, dtype=np.float32))
    shared = relayout_weights(inputs)
    for nm in ('norm1_g', 'norm2_g', 'final_g', 'b_mod', 'conv_w', 'conv_b'):
        shared[nm] = f(inputs[nm]).reshape(IN_SHAPES[nm])
    shared.update(make_consts())
    maps = []
    x = f(inputs['x'])
    c = f(inputs['c'])
    for b in range(8):
        m = dict(shared)
        m['x'] = x[b]
        m['c'] = c[b].reshape(8, 128)
        maps.append(m)
    return maps


def kernel(**inputs):
    if 'nc' not in _CACHE:
        _CACHE['nc'] = build(2)[0]
    nc = _CACHE['nc']
    maps = _prep_inputs(inputs)
    res = run_bass_kernel_spmd(nc, maps, core_ids=list(range(8)))
    out = np.stack([np.asarray(res.results[b]['out'], dtype=np.float32) for b in range(8)], axis=0)
    return out
```

```python
import numpy as np
import ml_dtypes
from contextlib import ExitStack
import concourse.bass as bass
import concourse.mybir as mybir
from concourse.bass_utils import run_bass_kernel_spmd

F32 = mybir.dt.float32
BF16 = mybir.dt.bfloat16
ALU = mybir.AluOpType
AF = mybir.ActivationFunctionType
AX = mybir.AxisListType
ENGS = ['pe', 'act', 'dve', 'pool', 'sp']

S, D, KC, NT, NCH, CH = 2048, 1024, 8, 16, 4, 512
DFF = 2816
NFF = 22
NEG = -30000.0
SCALE = 0.125
N_IN = 5656
OFF_AQ, OFF_AK, OFF_AV, OFF_BQ = 0, 768, 1536, 2304
OFF_KC, OFF_VC, OFF_KSL, OFF_VSL, OFF_KWN, OFF_VWN = 2816, 2944, 3072, 3200, 3328, 3456
OFF_BLOG, OFF_GA, OFF_GB = 3584, 3608, 4632


class Prog:
    def __init__(self, nc, stack):
        self.nc = nc
        self.stack = stack
        self.gstack = stack
        self.q = {e: [] for e in ENGS}
        self.cnt = {e: 0 for e in ENGS}
        self.sems = {e: stack.enter_context(nc.semaphore('s_' + e)) for e in ENGS}
        self.waited = {e: {} for e in ENGS}
        self.res = {}
        self.dslots = {}

    def sb(self, name, shape, dtype, glob=False):
        st = self.gstack if glob else self.stack
        self.nalloc = getattr(self, 'nalloc', 0) + 1
        return st.enter_context(self.nc.sbuf_tensor('%s_%d' % (name, self.nalloc), list(shape), dtype))

    def _norm(self, reads, writes):
        excl = [r for r in reads if isinstance(r, str) and r.startswith('ps')]
        if excl:
            reads = [r for r in reads if r not in excl]
            writes = list(writes) + excl
        return reads, writes

    def _deps(self, eng, reads, writes):
        need = {}

        def add(k, v):
            if need.get(k, 0) < v:
                need[k] = v
        reads, writes = self._norm(reads, writes)
        for r in reads:
            st = self.res.get(r)
            if st and st['w']:
                add(*st['w'])
        for w in writes:
            st = self.res.get(w)
            if st:
                if st['w']:
                    add(*st['w'])
                for k, v in st['r'].items():
                    add(k, v)
        out = []
        for k, v in need.items():
            if k == eng and eng == 'pe':
                continue
            if self.waited[eng].get(k, 0) >= v:
                continue
            self.waited[eng][k] = v
            out.append((k, v))
        return out

    def _mark(self, tok, reads, writes):
        reads, writes = self._norm(reads, writes)
        for r in reads:
            st = self.res.setdefault(r, {'w': None, 'r': {}})
            if st['r'].get(tok[0], 0) < tok[1]:
                st['r'][tok[0]] = tok[1]
        for w in writes:
            self.res[w] = {'w': tok, 'r': {}}

    def _sem(self, k):
        return self.sems[k] if k in self.sems else self.dslots[k]['sem']

    def op(self, eng, fn, reads=(), writes=()):
        waits = [(self._sem(k), v) for k, v in self._deps(eng, reads, writes)]
        self.cnt[eng] += 1
        tok = (eng, self.cnt[eng])
        mysem = self.sems[eng]

        def run(e, waits=waits, fn=fn, mysem=mysem):
            for s, v in waits:
                e.wait_ge(s, v)
            fn(e).then_inc(mysem, 1)
        self.q[eng].append(run)
        self._mark(tok, reads, writes)
        return tok

    def dma(self, eng, out, in_, reads=(), writes=(), slot='d0'):
        slot = eng + '_' + slot
        if slot not in self.dslots:
            self.dslots[slot] = {'sem': self.gstack.enter_context(self.nc.semaphore('d_' + slot)), 'n': 0}
        ds = self.dslots[slot]
        deps = self._deps(eng, reads, writes)
        if ds['n'] > 0 and self.waited[eng].get(slot, 0) < 16 * ds['n']:
            self.waited[eng][slot] = 16 * ds['n']
            deps = [d for d in deps if d[0] != slot] + [(slot, 16 * ds['n'])]
        waits = [(self._sem(k), v) for k, v in deps]
        ds['n'] += 1
        tok = (slot, 16 * ds['n'])
        dsem = ds['sem']

        def run(e, waits=waits, out=out, in_=in_, dsem=dsem):
            for s, v in waits:
                e.wait_ge(s, v)
            e.dma_start(out=out, in_=in_).then_inc(dsem, 16)
        self.q[eng].append(run)
        self._mark(tok, reads, writes)
        return tok

    def report(self, tag):
        import os
        if os.environ.get('SBUF_REPORT'):
            print('SBUF', tag, 'remaining', self.nc.sbuf_bytes_remaining, flush=True)

    def barrier(self):
        targets = [(k, 16 * ds['n']) for k, ds in self.dslots.items() if ds['n'] > 0]
        targets += [(e, self.cnt[e]) for e in ENGS if self.cnt[e] > 0]
        for eng in ENGS:
            waits = []
            for k, v in targets:
                if k == eng:
                    continue
                if self.waited[eng].get(k, 0) >= v:
                    continue
                self.waited[eng][k] = v
                waits.append((self._sem(k), v))

            def run(e, waits=waits):
                for s, v in waits:
                    e.wait_ge(s, v)
            self.q[eng].append(run)

    def emit(self):
        q = self.q
        with self.nc.Block() as block:
            @block.tensor
            def _(e):
                for f in q['pe']:
                    f(e)

            @block.scalar
            def _(e):
                for f in q['act']:
                    f(e)

            @block.vector
            def _(e):
                for f in q['dve']:
                    f(e)

            @block.gpsimd
            def _(e):
                for f in q['pool']:
                    f(e)

            @block.sync
            def _(e):
                for f in q['sp']:
                    f(e)
        self.q = {e: [] for e in ENGS}


def make_consts():
    bf = ml_dtypes.bfloat16
    C = {}
    C['k_identb'] = np.eye(128, dtype=np.float32).astype(bf)
    C['k_identf'] = np.eye(128, dtype=np.float32)
    C['k_onesf'] = np.ones((128, 128), np.float32)
    pm = np.zeros((128, 128), np.float32)
    for m in range(128):
        blk, d = divmod(m, 64)
        pm[blk * 64 + (d + 32) % 64, m] = 1.0
    C['k_pm'] = pm.astype(bf)
    inv = (1.0 / (np.float32(10000.0) ** (np.arange(0, 64, 2, dtype=np.float32) / np.float32(64)))).astype(np.float32)
    ang = (np.arange(S, dtype=np.float32)[:, None] * inv[None, :]).astype(np.float32)
    cos, sin = np.cos(ang).astype(np.float32), np.sin(ang).astype(np.float32)
    cosT = np.zeros((128, S), np.float32)
    sinT = np.zeros((128, S), np.float32)
    for row in range(128):
        d = row % 64
        cosT[row] = cos[:, d % 32]
        sinT[row] = (-1.0 if d < 32 else 1.0) * sin[:, d % 32]
    C['k_cos'] = cosT
    C['k_sin'] = sinT
    kk = np.arange(128)[:, None]
    qq = np.arange(128)[None, :]
    mc = np.where(kk > qq, NEG, 0.0).astype(np.float32)
    ml = np.where(kk < qq, NEG, 0.0).astype(np.float32)
    mf = np.where(kk <= qq, NEG, 0.0).astype(np.float32)
    C['k_mpair'] = np.concatenate([mc, ml], 1).astype(bf)
    C['k_mc4'] = np.tile(mc, (1, 4)).astype(bf)
    C['k_mfar'] = mf.astype(bf)
    n = np.arange(128)[:, None, None]
    c = np.arange(4)[None, :, None]
    q = np.arange(512)[None, None, :]
    cneg = np.where((16 * n + 31 > 512 * c + q) | (n == 127), NEG, 0.0).astype(np.float32)
    C['k_cneg'] = cneg.reshape(128, 2048).astype(bf)
    p = np.arange(128)[:, None]
    u = np.arange(256)[None, :]
    C['k_mwide'] = np.where(16 * (u - 120) + 31 > p, NEG, 0.0).astype(np.float32).astype(bf)
    E = np.zeros((128, 2048), np.float32)
    for j in range(32):
        E[j, 64 * j:64 * j + 64] = 1.0
    C['k_E'] = E.astype(bf)
    Fa = np.full((128, 8, 32), -1e30, np.float32)
    for i in range(8, 16):
        for pp in range(128):
            cur = 2 * i + (1 if pp >= 64 else 0)
            for j in (0, cur, cur - 1):
                Fa[pp, i - 8, j] = 1e9
    C['k_F'] = Fa.reshape(128, 256)
    sg = np.zeros((24, 12, 128), np.float32)
    for h in range(8):
        for b in range(3):
            pr, par = divmod(h, 2)
            sg[h * 3 + b, pr * 3 + b, par * 64:(par + 1) * 64] = 1.0
    C['k_selg'] = sg.reshape(24, 1536).astype(bf)
    ol = np.zeros((128, 128), np.float32)
    ol[:, 0:64] = 1.0
    oh = np.zeros((128, 128), np.float32)
    oh[:, 64:128] = 1.0
    C['k_oneslo'] = ol.astype(bf)
    C['k_oneshi'] = oh.astype(bf)
    C['k_zeros'] = np.zeros((128, 128), np.float32).astype(bf)
    C['k_one1'] = np.ones((1, 1), np.float32)
    return C


CONST_SHAPES = {
    'k_identb': ([128, 128], BF16), 'k_identf': ([128, 128], F32), 'k_onesf': ([128, 128], F32),
    'k_pm': ([128, 128], BF16), 'k_cos': ([128, S], F32), 'k_sin': ([128, S], F32),
    'k_mpair': ([128, 256], BF16), 'k_mc4': ([128, 512], BF16), 'k_mfar': ([128, 128], BF16),
    'k_cneg': ([128, 2048], BF16), 'k_mwide': ([128, 256], BF16), 'k_E': ([128, 2048], BF16),
    'k_F': ([128, 256], F32), 'k_selg': ([24, 1536], BF16), 'k_oneslo': ([128, 128], BF16),
    'k_oneshi': ([128, 128], BF16), 'k_zeros': ([128, 128], BF16), 'k_one1': ([1, 1], F32),
}

IN_SHAPES = {
    'x': [S, D], 'c': [8, 128], 'norm1_g': [2, 8, 128], 'norm2_g': [2, 8, 128], 'final_g': [8, 128],
    'b_mod': [2, 48, 128], 'conv_w': [2, 3, 44, 128], 'conv_b': [2, 44, 128],
    'wmod_l': [2, 12, 128, KC * 512], 'wina_l': [2, 3, 3, 128, KC * 256], 'wnsa_l': [2, 128, 12480],
    'wmrg_l': [2, 8, 128, 22 * 128], 'wout_l': [2, 8, 128, KC * 128], 'wup_l': [2, NFF, 128, KC * 256],
    'wdn_l': [2, 8, 128, NFF * 128], 'w1d_l': [2, 2, 128, 32 * 256], 'w2d_l': [2, 128, 256], 'w2v_l': [2, 128, 128],
    'pe_l': [2, 2, 128, 32],
}
NSA_OFF = {'wkv': (0, 2048), 'wbq': (2048, 6144), 'wkd': (6144, 10240), 'wvv': (10240, 12288), 'wbl': (12288, 12480)}


def relayout_weights(inp):
    f = lambda a: np.asarray(a, dtype=np.float32)

    def blk(w, c0, n):
        k = w.shape[0] // 128
        return w[:, c0:c0 + n].reshape(k, 128, n).transpose(1, 0, 2)
    o = {}
    w_mod, w_in = f(inp['w_mod']), f(inp['w_in'])
    o['wmod_l'] = np.stack([np.stack([blk(w_mod[l], 512 * j, 512).reshape(128, -1) for j in range(12)]) for l in range(2)])
    o['wina_l'] = np.stack([np.stack([np.stack([blk(w_in[l], off + 256 * g, 256).reshape(128, -1) for off in (OFF_AQ, OFF_AK, OFF_AV)])
                                      for g in range(3)]) for l in range(2)])
    nsa = []
    for l in range(2):
        wkv = blk(w_in[l], OFF_KC, 256).reshape(128, -1)
        wbq = blk(w_in[l], OFF_BQ, 512).reshape(128, -1)
        wkd = np.zeros((128, KC, 2, 2, 2, 64), np.float32)
        for w_, off in ((0, OFF_KSL), (1, OFF_KWN)):
            for g in range(2):
                b = blk(w_in[l], off + 64 * g, 64)
                wkd[:, :, w_, g, 0, :] = b
                wkd[:, :, w_, g, 1, :] = b
        wvv = np.stack([blk(w_in[l], OFF_VSL, 128), blk(w_in[l], OFF_VWN, 128)], axis=2).reshape(128, -1)
        wbl = blk(w_in[l], OFF_BLOG, 24).reshape(128, -1)
        nsa.append(np.concatenate([wkv, wbq, wkd.reshape(128, -1), wvv, wbl], axis=1))
    o['wnsa_l'] = np.stack(nsa)
    wbra, wbrb, wout = f(inp['w_br_a']), f(inp['w_br_b']), f(inp['w_out'])
    o['wmrg_l'] = np.stack([np.stack([np.concatenate([blk(w_in[l], OFF_GA + 128 * fi, 128), blk(w_in[l], OFF_GB + 128 * fi, 128),
                                                      blk(wbra[l], 128 * fi, 128), blk(wbrb[l], 128 * fi, 128)], axis=1).reshape(128, -1)
                                      for fi in range(8)]) for l in range(2)])
    o['wout_l'] = np.stack([np.stack([blk(wout[l], 128 * fi, 128).reshape(128, -1) for fi in range(8)]) for l in range(2)])
    wup, wdn = f(inp['w_up']), f(inp['w_down'])
    o['wup_l'] = np.stack([np.stack([np.stack([blk(wup[l], 128 * fc, 128), blk(wup[l], DFF + 128 * fc, 128)], axis=2).reshape(128, -1)
                                     for fc in range(NFF)]) for l in range(2)])
    o['wdn_l'] = np.stack([np.stack([blk(wdn[l], 128 * fi, 128).reshape(128, -1) for fi in range(8)]) for l in range(2)])
    w1 = [f(inp['cmp_w1_k']), f(inp['cmp_w1_v'])]
    o['w1d_l'] = np.stack([np.stack([np.tile(w1[kv][l].reshape(32, 64, 256).transpose(1, 0, 2), (2, 1, 1)).reshape(128, -1)
                                     for kv in range(2)]) for l in range(2)])
    w2k, w2v = f(inp['cmp_w2_k']), f(inp['cmp_w2_v'])
    o['w2d_l'] = np.stack([np.tile(blk(w2k[l], 0, 64), (1, 1, 2)).reshape(128, -1) for l in range(2)])
    o['w2v_l'] = np.stack([blk(w2v[l], 0, 64).reshape(128, -1) for l in range(2)])
    pe = [f(inp['cmp_pe_k']), f(inp['cmp_pe_v'])]
    o['pe_l'] = np.stack([np.stack([np.tile(pe[kv][l].T, (2, 1)) for kv in range(2)]) for l in range(2)])
    return {k_: np.ascontiguousarray(v) for k_, v in o.items()}


SCOPED_CONSTS = ('k_cos', 'k_sin', 'k_cneg', 'k_E')


class Ctx:
    pass


def load_scoped_consts(K, names):
    P = K.P
    for nm in names:
        shp, dt = CONST_SHAPES[nm]
        if nm in ('k_cos', 'k_sin'):
            K.k[nm] = P.sb('c' + nm, shp, BF16)
            K.wslot += 1
            P.dma('pool', K.k[nm][:], K.d[nm], writes=[nm], slot='w%d' % (K.wslot % 6))
        else:
            K.k[nm] = P.sb('c' + nm, shp, dt)
            P.dma('sp', K.k[nm][:], K.d[nm], writes=[nm], slot='c%d' % (len(nm) % 4))


def build(nlayers=2, taps=(), upto='all'):
    nc = bass.Bass("TRN2", target_bir_lowering=False)
    K = Ctx()
    K.nc = nc
    K.d = {}
    for nm, shp in IN_SHAPES.items():
        K.d[nm] = nc.dram_tensor(nm, list(shp), F32, kind="ExternalInput").ap()
    for nm, (shp, dt) in CONST_SHAPES.items():
        K.d[nm] = nc.dram_tensor(nm, list(shp), dt, kind="ExternalInput").ap()
    K.out = nc.dram_tensor("out", [S, D], F32, kind="ExternalOutput").ap()
    K.xpark = nc.dram_tensor("xpark", [128, KC * S], F32).ap()
    K.taps = {}
    K.tapset = set(taps)

    with ExitStack() as gst:
        P = Prog(nc, gst)
        K.P = P
        K.pb = [gst.enter_context(nc.psum_tensor('pb%d' % i, [128, 512], F32)) for i in range(8)]
        K.arena = P.sb('arena', [128, KC * S], F32, glob=True)
        K.xT = K.arena[:].rearrange("p (k s) -> p k s", k=KC)
        K.hT = P.sb('hT', [128, KC, S], BF16, glob=True)
        K.k = {}
        for nm, (shp, dt) in CONST_SHAPES.items():
            if nm in SCOPED_CONSTS:
                continue
            K.k[nm] = P.sb('c' + nm, shp, dt, glob=True)
        K.modc = P.sb('modc', [128, 2, 48], F32, glob=True)
        K.gs = P.sb('gs', [128, 2, 2, 8], F32, glob=True)
        K.gfin = P.sb('gfin', [128, 8], F32, glob=True)
        K.cvw = P.sb('cvw', [128, 2, 4, 44], F32, glob=True)
        K.halo = P.sb('halo', [128, 44, 2], F32, glob=True)
        K.wslot = 0

        phase_start(K)
        order = ['start', 'norm', 'dil', 'nsa', 'merge', 'ffn', 'all']
        lvl = order.index(upto)
        for layer in range(nlayers):
            with ExitStack() as lst:
                P.stack = lst
                K.yaT = P.sb('yaT', [128, 2, S], BF16)
                K.ybT = P.sb('ybT', [128, 4, S], BF16)
                stop_early = False
                with ExitStack() as dsc:
                    P.stack = dsc
                    K.pre_dil = [P.sb('pwq', [128, KC, 256], BF16), P.sb('pwk', [128, KC, 256], BF16), P.sb('pwv', [128, KC, 256], BF16)]
                    for w_i, nm_ in enumerate(('wq', 'wk', 'wv')):
                        ldflat(K, K.pre_dil[w_i][:].rearrange("p k n -> p (k n)"), K.d['wina_l'][layer, 0, w_i], (nm_, 0))
                    phase_norm(K, layer, 0)
                    tap_h(K, 'h1_%d' % layer)
                    if lvl <= 1:
                        P.barrier()
                        P.emit()
                        stop_early = True
                    else:
                        park_x(K)
                        phase_dilated(K, layer)
                P.stack = lst
                if stop_early:
                    break
                if lvl >= 3:
                    phase_nsa(K, layer)
                if lvl >= 4:
                    unpark_x(K)
                    phase_merge(K, layer)
                P.barrier()
                P.emit()
            P.stack = gst
            if lvl <= 3:
                break
            tap_x(K, 'xm%d' % layer)
            if lvl <= 4:
                break
            with ExitStack() as fst:
                P.stack = fst
                K.pre_wup = [P.sb('pwup%d' % i, [128, KC, 2, 128], BF16) for i in range(4)]
                for fc_ in range(3):
                    ldflat(K, K.pre_wup[fc_][:].rearrange("p k a n -> p (k a n)"), K.d['wup_l'][layer, fc_], ('wup', fc_))
                phase_norm(K, layer, 1)
                phase_ffn(K, layer)
            P.stack = gst
            tap_x(K, 'xf%d' % layer)
        if lvl >= 6:
            phase_final(K)
        P.barrier()
        P.emit()
    return nc, K


def tap_x(K, name):
    if name not in K.tapset:
        return
    P = K.P
    t = K.nc.dram_tensor("tap_" + name, [128, KC * S], F32, kind="ExternalOutput").ap()
    P.dma('sp', t, K.arena[:], reads=['xT'], slot='tap')
    K.taps[name] = t


def tap_h(K, name):
    if name not in K.tapset:
        return
    P = K.P
    t = K.nc.dram_tensor("tap_" + name, [128, KC * S], BF16, kind="ExternalOutput").ap()
    P.dma('sp', t, K.hT[:].rearrange("p k s -> p (k s)"), reads=['hT'], slot='tap')
    K.taps[name] = t


def tap_any(K, name, ap, reads, shape, dt):
    if name not in K.tapset:
        return
    t = K.nc.dram_tensor("tap_" + name, list(shape), dt, kind="ExternalOutput").ap()
    K.P.dma('sp', t, ap, reads=reads, slot='tap')
    K.taps[name] = t


def mm(P, out, lhsT, rhs, start, stop, reads, writes):
    return P.op('pe', lambda e: e.matmul(out, lhsT=lhsT, rhs=rhs, start=start, stop=stop, skip_group_check=True),
                reads=reads, writes=writes)


def load_w(K, dst, src2d, wname, eng='pool'):
    K.wslot += 1
    return K.P.dma(eng, dst, src2d.rearrange("(k p) n -> p k n", p=128), writes=[wname], slot='w%d' % (K.wslot % 6))


def ldflat(K, dst_flat, src, wname, eng='pool'):
    K.wslot += 1
    return K.P.dma(eng, dst_flat, src, writes=[wname], slot='w%d' % (K.wslot % 6))


def proj_fm(K, ps, psname, wt, wname, cols, tok0, ntok, M=128):
    P = K.P
    for kc in range(KC):
        mm(P, ps[0:M, 0:ntok], wt[:, kc, cols], K.hT[:, kc, tok0:tok0 + ntok], kc == 0, kc == KC - 1,
           reads=[wname, 'hT'], writes=[psname])


def phase_start(K):
    P, nc, d = K.P, K.nc, K.d
    _prev_stack = P.stack
    with ExitStack() as st:
        P.stack = st
        for i, (nm, (shp, dt)) in enumerate(CONST_SHAPES.items()):
            if nm in SCOPED_CONSTS:
                continue
            P.dma('sp', K.k[nm][:], d[nm], writes=[nm], slot='c%d' % (i % 4))
        identf = K.k['k_identf']
        rows = P.sb('rows', [128, 128], F32)
        rows2 = P.sb('rows2', [128, 128], F32)
        cols = P.sb('cols', [128, 512], F32)
        P.op('pool', lambda e: e.memset(rows[:], 0.0), writes=['rows'])
        P.op('pool', lambda e: e.memset(rows2[:], 0.0), writes=['rows2'])
        off = 0
        entries = {}
        rnames = []
        for nm, ap, n in [('c', d['c'], 8), ('g10', d['norm1_g'][0], 8), ('g11', d['norm1_g'][1], 8),
                          ('g20', d['norm2_g'][0], 8), ('g21', d['norm2_g'][1], 8), ('gf', d['final_g'], 8),
                          ('bm0', d['b_mod'][0], 48)]:
            entries[nm] = (off, n)
            P.dma('sp', rows[off:off + n, :], ap, reads=['rows'], writes=[('rows', nm)], slot='r%d' % (len(rnames) % 4))
            rnames.append(('rows', nm))
            off += n
        P.dma('sp', rows2[0:48, :], d['b_mod'][1], reads=['rows2'], writes=[('rows2', 0)], slot='r0')
        P.op('pe', lambda e: e.transpose(K.pb[0][:, 0:128], rows[:, :], identf[:]), reads=rnames + ['rows', 'k_identf'], writes=['ps0'])
        P.op('dve', lambda e: e.tensor_copy(out=cols[:, 0:128], in_=K.pb[0][:, 0:128]), reads=['ps0'], writes=['colsA'])
        P.op('pe', lambda e: e.transpose(K.pb[1][:, 0:128], rows2[:, :], identf[:]), reads=[('rows2', 0), 'rows2', 'k_identf'], writes=['ps1'])
        P.op('dve', lambda e: e.tensor_copy(out=cols[:, 128:256], in_=K.pb[1][:, 0:128]), reads=['ps1'], writes=['colsB'])

        def colof(nm):
            o, n = entries[nm]
            return cols[:, o:o + n]
        for l in range(2):
            rc = P.sb('rc%d' % l, [128, 128], F32)
            P.op('pool', lambda e, rc=rc: e.memset(rc[:], 0.0), writes=[('rc', l)])
            for j in range(2):
                P.dma('sp', rc[44 * j:44 * j + 44, :], d['conv_w'][l, j], reads=[('rc', l)], writes=[('rc', l, j)], slot='r2')
            rd = P.sb('rd%d' % l, [128, 128], F32)
            P.op('pool', lambda e, rd=rd: e.memset(rd[:], 0.0), writes=[('rd', l)])
            P.dma('sp', rd[0:44, :], d['conv_w'][l, 2], reads=[('rd', l)], writes=[('rd', l, 0)], slot='r3')
            P.dma('sp', rd[44:88, :], d['conv_b'][l], reads=[('rd', l)], writes=[('rd', l, 1)], slot='r3')
            P.op('pe', lambda e, rc=rc: e.transpose(K.pb[2][:, 0:128], rc[:, :], identf[:]),
                 reads=[('rc', l), ('rc', l, 0), ('rc', l, 1), 'k_identf'], writes=['ps2'])
            P.op('dve', lambda e, l=l: e.tensor_copy(out=K.cvw[:, l, 0:2, :], in_=K.pb[2][:, 0:88].rearrange("p (a b) -> p a b", a=2)),
                 reads=['ps2'], writes=['cvw'])
            P.op('pe', lambda e, rd=rd: e.transpose(K.pb[3][:, 0:128], rd[:, :], identf[:]),
                 reads=[('rd', l), ('rd', l, 0), ('rd', l, 1), 'k_identf'], writes=['ps3'])
            P.op('dve', lambda e, l=l: e.tensor_copy(out=K.cvw[:, l, 2:4, :], in_=K.pb[3][:, 0:88].rearrange("p (a b) -> p a b", a=2)),
                 reads=['ps3'], writes=['cvw'])

        sc = P.sb('sc', [128, 8], F32)
        P.op('act', lambda e: e.activation(out=sc[:], in_=colof('c'), func=AF.Silu), reads=['colsA'], writes=['sc'])
        mrow = [P.sb('mrow%d' % i, [1, 512], F32) for i in range(2)]
        wm = [P.sb('wm%d' % i, [128, KC, 512], F32) for i in range(2)]
        pieces = [(l, j) for l in range(2) for j in range(12)]
        one1 = K.k['k_one1']

        def ld(i):
            l, j = pieces[i]
            ldflat(K, wm[i % 2][:].rearrange("p k n -> p (k n)"), d['wmod_l'][l, j], ('wm', i % 2), eng='sp')
        ld(0)
        for i, (l, j) in enumerate(pieces):
            if i + 1 < len(pieces):
                ld(i + 1)
            ps = K.pb[4 + (i % 2)]
            psn = 'ps%d' % (4 + (i % 2))
            for kc in range(KC):
                mm(P, ps[0:1, :], sc[:, kc:kc + 1], wm[i % 2][:, kc, :], kc == 0, kc == KC - 1,
                   reads=['sc', ('wm', i % 2)], writes=[psn])
            mr = mrow[i % 2]
            P.op('act', lambda e, ps=ps, mr=mr: e.copy(out=mr[0:1, :], in_=ps[0:1, :]), reads=[psn], writes=[('mrow', i % 2)])
            for q_ in range(4):
                col = l * 48 + 4 * j + q_
                mm(P, K.pb[6][:, col:col + 1], mr[0:1, 128 * q_:128 * q_ + 128], one1[0:1, 0:1], True, True,
                   reads=[('mrow', i % 2), 'k_one1'], writes=['ps6'])
        bmc = [colof('bm0'), cols[:, 128:176]]
        for l in range(2):
            P.op('dve', lambda e, l=l: e.tensor_tensor(out=K.modc[:, l, :], in0=K.pb[6][:, l * 48:l * 48 + 48], in1=bmc[l], op=ALU.add),
                 reads=['ps6', 'colsA', 'colsB'], writes=['modc'])
        for l in range(2):
            for w, gname, sidx in ((0, 'g1%d' % l, 1), (1, 'g2%d' % l, 4)):
                P.op('dve', lambda e, l=l, w=w, gname=gname, sidx=sidx: e.scalar_tensor_tensor(
                    out=K.gs[:, l, w, :], in0=K.modc[:, l, 8 * sidx:8 * sidx + 8], scalar=1.0, in1=colof(gname),
                    op0=ALU.add, op1=ALU.mult), reads=['modc', 'colsA'], writes=['gs'])
                P.op('dve', lambda e, l=l, w=w: e.tensor_scalar(out=K.gs[:, l, w, :], in0=K.gs[:, l, w, :], scalar1=32.0, scalar2=None,
                                                             op0=ALU.mult), reads=['gs'], writes=['gs'])
        P.op('dve', lambda e: e.tensor_scalar(out=K.gfin[:], in0=colof('gf'), scalar1=32.0, scalar2=None, op0=ALU.mult),
             reads=['colsA'], writes=['gfin'])
        tap_any(K, 'modc', K.modc[:].rearrange("p a b -> p (a b)"), ['modc'], [128, 96], F32)

        xin = [P.sb('xin%d' % i, [128, D], F32) for i in range(2)]
        for t in range(NT):
            xi = xin[t % 2]
            P.dma('sp', xi[:], d['x'][128 * t:128 * t + 128, :], writes=[('xin', t % 2)], slot='x%d' % (t % 2))
            for half in range(2):
                ps = K.pb[(2 * t + half) % 4]
                psn = 'ps%d' % ((2 * t + half) % 4)
                for j in range(4):
                    kc = half * 4 + j
                    P.op('pe', lambda e, ps=ps, xi=xi, kc=kc, j=j: e.transpose(ps[:, 128 * j:128 * j + 128], xi[:, 128 * kc:128 * kc + 128], identf[:]),
                         reads=[('xin', t % 2), 'k_identf'], writes=[psn])
                eng = 'act' if half == 0 else 'dve'
                if eng == 'act':
                    P.op('act', lambda e, ps=ps, half=half, t=t: e.copy(out=K.xT[:, half * 4:half * 4 + 4, 128 * t:128 * t + 128],
                                                                         in_=ps[:, :].rearrange("p (a b) -> p a b", a=4)),
                         reads=[psn], writes=['xT'])
                else:
                    P.op('dve', lambda e, ps=ps, half=half, t=t: e.tensor_copy(out=K.xT[:, half * 4:half * 4 + 4, 128 * t:128 * t + 128],
                                                                                in_=ps[:, :].rearrange("p (a b) -> p a b", a=4)),
                         reads=[psn], writes=['xT'])
        P.barrier()
        P.emit()
    P.stack = _prev_stack


def norm_A(K, c, st_tiles):
    P = K.P
    sq, rstds, tmp = st_tiles
    rstd = rstds[c % 2]
    ps = K.pb[c % 2]
    psn = 'ps%d' % (c % 2)
    onesf = K.k['k_onesf']
    for kc in range(KC):
        s_ = sq[kc % len(sq)]
        if kc % 2 == 0:
            P.op('act', lambda e, s_=s_, kc=kc: e.activation(out=s_[:], in_=K.xT[:, kc, CH * c:CH * c + CH], func=AF.Square),
                 reads=['xT'], writes=[('sq', kc % len(sq))])
        else:
            P.op('pool', lambda e, s_=s_, kc=kc: e.tensor_tensor(out=s_[:], in0=K.xT[:, kc, CH * c:CH * c + CH],
                                                                 in1=K.xT[:, kc, CH * c:CH * c + CH], op=ALU.mult),
                 reads=['xT'], writes=[('sq', kc % len(sq))])
        mm(P, ps[:, :], onesf[:], s_[:], kc == 0, kc == KC - 1, reads=['k_onesf', ('sq', kc % len(sq))], writes=[psn])
    P.op('act', lambda e: e.activation(out=rstd[:], in_=ps[:, :], func=AF.Sqrt, bias=K.epsb[:, 0:1], scale=1.0),
         reads=[psn, 'epsb'], writes=[('rstd', c % 2)])
    P.op('dve', lambda e: e.reciprocal(out=rstd[:], in_=rstd[:]), reads=[('rstd', c % 2)], writes=[('rstd', c % 2)])


def norm_B(K, c, gcol, out_fn, st_tiles):
    P = K.P
    sq, rstds, tmp = st_tiles
    rstd = rstds[c % 2]
    for kc in range(KC):
        t_ = tmp[kc % len(tmp)]
        P.op('dve', lambda e, t_=t_, kc=kc: e.scalar_tensor_tensor(out=t_[:], in0=K.xT[:, kc, CH * c:CH * c + CH], scalar=gcol[:, kc:kc + 1],
                                                                 in1=rstd[:], op0=ALU.mult, op1=ALU.mult),
             reads=['xT', ('rstd', c % 2), 'gs', 'gfin'], writes=[('ntmp', kc % len(tmp))])
        out_fn(kc, t_, ('ntmp', kc % len(tmp)))


def norm_all(K, gcol, make_out_fn, st_tiles, after_chunk=None):
    norm_A(K, 0, st_tiles)
    for c in range(NCH):
        if c + 1 < NCH:
            norm_A(K, c + 1, st_tiles)
        norm_B(K, c, gcol, make_out_fn(c), st_tiles)
        if after_chunk:
            after_chunk(c)


def phase_norm(K, layer, which):
    P = K.P
    _prev_stack = P.stack
    with ExitStack() as st:
        P.stack = st
        sq = [P.sb('sq%d' % i, [128, CH], F32) for i in range(4)]
        rstd = [P.sb('rstd%d' % i, [128, CH], F32) for i in range(2)]
        tmp = [P.sb('ntmp%d' % i, [128, CH], F32) for i in range(4)]
        K.epsb = P.sb('epsb', [128, 1], F32)
        P.op('pool', lambda e: e.memset(K.epsb[:], 1024.0 * 1e-6), writes=['epsb'])
        gcol = K.gs[:, layer, which, :]
        sh = K.modc[:, layer, (0 if which == 0 else 24):(8 if which == 0 else 32)]

        def make_out_fn(c):
            def out_fn(kc, t_, tname):
                P.op('act', lambda e: e.activation(out=K.hT[:, kc, CH * c:CH * c + CH], in_=t_[:], func=AF.Identity,
                                                   bias=sh[:, kc:kc + 1], scale=1.0),
                     reads=[tname, 'modc'], writes=['hT'])
            return out_fn
        norm_all(K, gcol, make_out_fn, (sq, rstd, tmp))
        P.barrier()
        P.emit()
    P.stack = _prev_stack


def park_x(K):
    P = K.P
    P.dma('sp', K.xpark, K.arena[:], reads=['xT'], writes=['xpark'], slot='park')
    P.barrier()


def unpark_x(K):
    P = K.P
    P.barrier()
    P.dma('sp', K.arena[:], K.xpark, reads=['xpark'], writes=['xT'], slot='park')


def rope_tile(K, ps, psn, tok0, ntok, out_ap, out_name, W, nope_ap=None, nope_name=None, i=0):
    P = K.P
    raw, t1, t2 = W['raw'][i % 2], W['t1'][i % 2], W['t2'][i % 2]
    rn, t1n, t2n = ('raw', i % 2), ('t1', i % 2), ('t2', i % 2)
    swb = W.get('swbanks', (7,))[i % len(W.get('swbanks', (7,)))]
    psw = K.pb[swb]
    if nope_ap is not None:
        P.op('act', lambda e: e.copy(out=nope_ap, in_=ps[:, 0:ntok]), reads=[psn], writes=[nope_name])
        rawap, rawname = nope_ap, nope_name
    else:
        P.op('act', lambda e: e.copy(out=raw[:, 0:ntok], in_=ps[:, 0:ntok]), reads=[psn], writes=[rn])
        rawap, rawname = raw[:, 0:ntok], rn
    P.op('dve', lambda e: e.tensor_tensor(out=t1[:, 0:ntok], in0=ps[:, 0:ntok], in1=K.k['k_cos'][:, tok0:tok0 + ntok], op=ALU.mult),
         reads=[psn, 'k_cos'], writes=[t1n])
    mm(P, psw[:, 0:ntok], K.k['k_pm'][:], rawap, True, True, reads=['k_pm', rawname], writes=['ps%d' % swb])
    P.op('dve', lambda e: e.tensor_tensor(out=t2[:, 0:ntok], in0=psw[:, 0:ntok], in1=K.k['k_sin'][:, tok0:tok0 + ntok], op=ALU.mult),
         reads=['ps%d' % swb, 'k_sin'], writes=[t2n])
    P.op('pool', lambda e: e.tensor_tensor(out=out_ap, in0=t1[:, 0:ntok].rearrange(W['inre'], **W['inkw']) if W.get('inre') else t1[:, 0:ntok],
                                           in1=t2[:, 0:ntok].rearrange(W['inre'], **W['inkw']) if W.get('inre') else t2[:, 0:ntok], op=ALU.add),
         reads=[t1n, t2n], writes=[out_name])


def acc_init(K, bank):
    P = K.P
    mm(P, K.pb[bank][:, :], K.k['k_zeros'][:], K.k['k_mc4'][:], True, False, reads=['k_zeros', 'k_mc4'], writes=['ps%d' % bank])


class Pipe:
    def __init__(self, K, depth=2):
        self.K, self.depth, self.pending = K, depth, []

    def unit(self, st_bank, score_mms, nq, pt, ptname, pv_list, pre=None, post=None, first=False):
        K = self.K
        P = K.P
        ps = K.pb[st_bank]
        psn = 'ps%d' % st_bank
        for idx, (lhsT, rhs, c0, n, rd) in enumerate(score_mms):
            mm(P, ps[0:lhsT.shape[1], c0:c0 + n], lhsT, rhs, idx == 0, idx == len(score_mms) - 1, reads=rd, writes=[psn])
        P.op('act', lambda e: e.activation(out=pt[:, 0:nq], in_=ps[:, 0:nq], func=AF.Exp, scale=SCALE), reads=[psn], writes=[ptname])
        self.pending.append((pv_list, pt, ptname, pre, post, first))
        if len(self.pending) > self.depth:
            self._pop()

    def pair(self, ua, ub):
        K = self.K
        P = K.P
        for u in (ua, ub):
            ps, psn = K.pb[u['bank']], 'ps%d' % u['bank']
            for idx in range(u['nqk']):
                lhsT, rhs, c0, n, rd = u['sm'][idx]
                mm(P, ps[0:lhsT.shape[1], c0:c0 + n], lhsT, rhs, idx == 0, False, reads=rd, writes=[psn])
        for u in (ua, ub):
            ps, psn = K.pb[u['bank']], 'ps%d' % u['bank']
            nsm = len(u['sm'])
            for idx in range(u['nqk'], nsm):
                lhsT, rhs, c0, n, rd = u['sm'][idx]
                mm(P, ps[0:lhsT.shape[1], c0:c0 + n], lhsT, rhs, False, idx == nsm - 1, reads=rd, writes=[psn])
        for u in (ua, ub):
            ps, psn = K.pb[u['bank']], 'ps%d' % u['bank']
            pt, nq = u['pt'], u['nq']
            P.op('act', lambda e, pt=pt, ps=ps, nq=nq: e.activation(out=pt[:, 0:nq], in_=ps[:, 0:nq], func=AF.Exp, scale=SCALE),
                 reads=[psn], writes=[u['ptname']])
            self.pending.append((u['pv'], pt, u['ptname'], None, u.get('post'), u.get('first', False)))
        while len(self.pending) > self.depth:
            self._pop()

    def _pop(self):
        K = self.K
        P = K.P
        pv_list, pt, ptname, pre, post, first = self.pending.pop(0)
        if pre:
            pre()
        started = set()
        for (ab, lhsT, lrd, c0, n, a0) in pv_list:
            st_ = first and ab not in started
            started.add(ab)
            mm(P, K.pb[ab][:, a0:a0 + n], lhsT, pt[:, c0:c0 + n], st_, False, reads=lrd + [ptname], writes=['ps%d' % ab])
        if post:
            post()

    def flush(self):
        while self.pending:
            self._pop()


def phase_dilated(K, layer):
    P, d = K.P, K.d
    ab = K.arena[:].bitcast(BF16)
    QT = ab[:, 0:4096].rearrange("p (a s) -> p a s", a=2)
    KT = ab[:, 4096:8192].rearrange("p (a s) -> p a s", a=2)
    Vp = ab[:, 8192:16384].rearrange("p (t h c) -> p t h c", t=NT, h=4)
    af = K.arena[:]
    numacc = af[:, 8192:12288].rearrange("p (a s) -> p a s", a=2)
    denacc = af[:, 12288:16384].rearrange("p (a s) -> p a s", a=2)
    ident = K.k['k_identb']
    _prev_stack = P.stack
    with ExitStack() as st:
        P.stack = st
        wq = [K.pre_dil[0], P.sb('wq1', [128, KC, 256], BF16)]
        wk = [K.pre_dil[1], P.sb('wk1', [128, KC, 256], BF16)]
        wv = [K.pre_dil[2], P.sb('wv1', [128, KC, 256], BF16)]
        W = {'raw': [P.sb('raw%d' % i, [128, CH], BF16) for i in range(2)],
             't1': [P.sb('t1_%d' % i, [128, CH], F32) for i in range(2)],
             't2': [P.sb('t2_%d' % i, [128, CH], F32) for i in range(2)], 'swbanks': (7, 6)}
        pts = [P.sb('pt%d' % i, [128, CH], BF16) for i in range(4)]
        rden = P.sb('rden', [128, S], F32)
        load_scoped_consts(K, ['k_cos', 'k_sin'])
        P.op('pool', lambda e: e.memset(ab[:, 8192:16384], 0.0), writes=['Vp'])

        def ldw(g):
            ldflat(K, wq[g % 2][:].rearrange("p k n -> p (k n)"), d['wina_l'][layer, g, 0], ('wq', g % 2))
            ldflat(K, wk[g % 2][:].rearrange("p k n -> p (k n)"), d['wina_l'][layer, g, 1], ('wk', g % 2))
            ldflat(K, wv[g % 2][:].rearrange("p k n -> p (k n)"), d['wina_l'][layer, g, 2], ('wv', g % 2))
        rt = 0
        for g in range(3):
            if g + 1 < 3:
                ldw(g + 1)
            dil = (1, 4, 16)[g]
            for which, wt, wn, dst, dname in ((0, wq[g % 2], ('wq', g % 2), QT, 'QT'), (1, wk[g % 2], ('wk', g % 2), KT, 'KT')):
                for pr in range(2):
                    for c in range(NCH):
                        bank = rt % 2
                        proj_fm(K, K.pb[bank], 'ps%d' % bank, wt, wn, slice(128 * pr, 128 * pr + 128), CH * c, CH)
                        if dil == 1:
                            out_ap = dst[:, pr, CH * c:CH * c + CH]
                            W['inre'] = None
                        else:
                            per = CH // dil
                            out_ap = dst[:, pr, :].rearrange("p (r m) -> p m r", r=dil)[:, per * c:per * c + per, :]
                            W['inre'] = "p (m r) -> p m r"
                            W['inkw'] = {'r': dil}
                        rope_tile(K, K.pb[bank], 'ps%d' % bank, CH * c, CH, out_ap, dname, W, i=rt)
                        rt += 1
            for t in range(NT):
                bank = 2 + (t % 2)
                ps = K.pb[bank]
                if dil == 1:
                    tok = lambda kc: K.hT[:, kc, 128 * t:128 * t + 128]
                elif dil == 4:
                    r, j = divmod(t, 4)
                    tok = lambda kc, r=r, j=j: K.hT[:, kc, :].rearrange("p (m r) -> p r m", r=4)[:, r, 128 * j:128 * j + 128]
                else:
                    tok = lambda kc, r=t: K.hT[:, kc, :].rearrange("p (m r) -> p r m", r=16)[:, r, :]
                for kc in range(KC):
                    mm(P, ps[:, 0:256], tok(kc), wv[g % 2][:, kc, :], kc == 0, kc == KC - 1, reads=['hT', ('wv', g % 2)], writes=['ps%d' % bank])
                psv = ps[:, 0:256].rearrange("p (a b c) -> p a b c", a=2, b=2)
                P.op('act', lambda e, t=t, psv=psv: e.copy(out=Vp[:, t, 0:4:2, 0:64], in_=psv[:, :, 0, :]), reads=['ps%d' % bank], writes=['Vp'])
                P.op('dve', lambda e, t=t, psv=psv: e.tensor_copy(out=Vp[:, t, 1:4:2, 64:128], in_=psv[:, :, 1, :]), reads=['ps%d' % bank], writes=['Vp'])
            it = 0
            pipe = Pipe(K, depth=2)
            for pr in range(2):
                for c in range(NCH):
                    nb, db = (3, 4) if (pr * NCH + c) % 2 == 0 else (5, 6)
                    units = []
                    for par in range(2):
                        rows = slice(64 * par, 64 * par + 64)
                        h = 2 * pr + par
                        ones = K.k['k_oneslo'] if par == 0 else K.k['k_oneshi']
                        onesn = 'k_oneslo' if par == 0 else 'k_oneshi'
                        if dil == 16:
                            sm = []
                            pv = []
                            for j in range(4):
                                t = 4 * c + j
                                sm.append((KT[rows, pr, 128 * t:128 * t + 128], QT[rows, pr, 128 * t:128 * t + 128], 128 * j, 128, ['KT', 'QT']))
                                pv.append((nb, Vp[:, t, h, :], ['Vp'], 128 * j, 128, 128 * j))
                                pv.append((db, ones[:], [onesn], 128 * j, 128, 128 * j))
                            sm.append((ident[:], K.k['k_mc4'][:], 0, 512, ['k_identb', 'k_mc4']))
                            units.append((sm, 512, pv, par, 4))
                        else:
                            first_t = 0 if dil == 1 else 4 * c
                            for kt in range(max(4 * c - 1, first_t), 4 * c + 4):
                                q0 = max(128 * kt, CH * c)
                                q1 = min(128 * kt + 256, CH * c + CH)
                                nq = q1 - q0
                                sm = [(KT[rows, pr, 128 * kt:128 * kt + 128], QT[rows, pr, q0:q1], 0, nq, ['KT', 'QT'])]
                                m0 = 0 if q0 == 128 * kt else 128
                                sm.append((ident[:], K.k['k_mpair'][:, m0:m0 + nq], 0, nq, ['k_identb', 'k_mpair']))
                                pv = [(nb, Vp[:, kt, h, :], ['Vp'], 0, nq, q0 - CH * c),
                                      (db, ones[:], [onesn], 0, nq, q0 - CH * c)]
                                units.append((sm, nq, pv, par, 1))

                    def pre(nb=nb, db=db):
                        acc_init(K, nb)
                        acc_init(K, db)

                    def post(nb=nb, db=db, pr=pr, c=c):
                        for bank, acc, an in ((nb, numacc, 'numacc'), (db, denacc, 'denacc')):
                            ps = K.pb[bank]
                            if dil == 1:
                                P.op('act', lambda e, ps=ps, acc=acc: e.copy(out=acc[:, pr, CH * c:CH * c + CH], in_=ps[:, :]),
                                     reads=['ps%d' % bank], writes=[an])
                            elif dil == 4:
                                view = acc[:, pr, :].rearrange("p (m r) -> p r m", r=4)[:, c, :]
                                P.op('dve', lambda e, ps=ps, view=view: e.tensor_tensor(out=view, in0=view, in1=ps[:, :], op=ALU.add),
                                     reads=['ps%d' % bank, an], writes=[an])
                            else:
                                view = acc[:, pr, :].rearrange("p (n r) -> p r n", r=16)[:, 4 * c:4 * c + 4, :]
                                P.op('dve', lambda e, ps=ps, view=view: e.tensor_tensor(out=view, in0=view, in1=ps[:, :].rearrange("p (r n) -> p r n", r=4), op=ALU.add),
                                     reads=['ps%d' % bank, an], writes=[an])
                    ua_ = [u for u in units if u[3] == 0]
                    ub_ = [u for u in units if u[3] == 1]
                    assert len(ua_) == len(ub_)
                    for ui, (a_, b_) in enumerate(zip(ua_, ub_)):
                        da = {'bank': it % 3, 'sm': a_[0], 'nq': a_[1], 'pv': a_[2], 'nqk': a_[4], 'pt': pts[it % 4], 'ptname': ('pt', it % 4),
                              'first': ui == 0}
                        it += 1
                        db_ = {'bank': it % 3, 'sm': b_[0], 'nq': b_[1], 'pv': b_[2], 'nqk': b_[4], 'pt': pts[it % 4], 'ptname': ('pt', it % 4),
                               'post': post if ui == len(ua_) - 1 else None}
                        it += 1
                        pipe.pair(da, db_)
            pipe.flush()
        for pr in range(2):
            P.op('dve', lambda e, pr=pr: e.reciprocal(out=rden[:], in_=denacc[:, pr, :]), reads=['denacc'], writes=['rden'])
            P.op('dve', lambda e, pr=pr: e.tensor_tensor(out=K.yaT[:, pr, :], in0=numacc[:, pr, :], in1=rden[:], op=ALU.mult),
                 reads=['numacc', 'rden'], writes=['yaT'])
        P.report('dil')
        tap_any(K, 'ya%d' % layer, K.yaT[:].rearrange("p a s -> p (a s)"), ['yaT'], [128, 2 * S], BF16)
        P.barrier()
        P.emit()
    P.stack = _prev_stack


def phase_nsa(K, layer):
    P, d = K.P, K.d
    wn = d['wnsa_l'][layer]
    ab = K.arena[:].bitcast(BF16)
    QR = ab[:, 0:8192].rearrange("p (a s) -> p a s", a=4)
    QN = ab[:, 8192:16384].rearrange("p (a s) -> p a s", a=4)
    KS = ab[:, 16384:20480].rearrange("p (g s) -> p g s", g=2)
    KW = ab[:, 20480:24576].rearrange("p (g s) -> p g s", g=2)
    VS = ab[:, 24576:32768].rearrange("p (t g v c) -> p t g v c", t=NT, g=2, v=2)
    ident = K.k['k_identb']
    identf = K.k['k_identf']
    _prev_stack = P.stack
    with ExitStack() as st:
        P.stack = st
        VW = P.sb('VW', [128, NT, 2, 2, 128], BF16)
        gateT = P.sb('gateT', [24, S], BF16)
        negT = P.sb('negT', [128, 2, S], BF16)
        kcT2 = P.sb('kcT2', [128, 2, 128], BF16)
        vcp = P.sb('vcp', [128, 2, 2, 128], BF16)
        P.op('pool', lambda e: e.memset(VW[:].rearrange("p t g v c -> p (t g v c)"), 0.0), writes=['VW'])
        P.op('pool', lambda e: e.memset(vcp[:].rearrange("p g v c -> p (g v c)"), 0.0), writes=['vcp'])
        P.op('pool', lambda e: e.memset(kcT2[:].rearrange("p g c -> p (g c)"), 0.0), writes=['kcT2'])
        P.op('pool', lambda e: e.memset(negT[:].rearrange("p g s -> p (g s)"), 0.0), writes=['negT'])

        with ExitStack() as cst:
            P.stack = cst
            wkv = P.sb('wkv', [128, KC, 256], BF16)
            ldflat(K, wkv[:].rearrange("p k n -> p (k n)"), wn[:, 0:2048], 'wkv')
            cmpT = K.arena[:, 0:4096].rearrange("p (a s) -> p a s", a=2)
            for kv in range(2):
                for c in range(NCH):
                    bank = (kv * NCH + c) % 2
                    proj_fm(K, K.pb[bank], 'ps%d' % bank, wkv, 'wkv', slice(128 * kv, 128 * kv + 128), CH * c, CH)
                    P.op('act', lambda e, bank=bank, kv=kv, c=c: e.copy(out=cmpT[:, kv, CH * c:CH * c + CH], in_=K.pb[bank][:, :]),
                         reads=['ps%d' % bank], writes=['cmpT'])
            w1ds = [ab[:, 8192:16384].rearrange("p (l j) -> p l j", l=32), ab[:, 24576:32768].rearrange("p (l j) -> p l j", l=32)]
            kpes = [ab[:, 16384:20480].rearrange("p (l n) -> p l n", l=32), ab[:, 20480:24576].rearrange("p (l n) -> p l n", l=32)]
            w2d = P.sb('w2d', [128, 2, 128], BF16)
            w2v = P.sb('w2v', [128, 2, 64], BF16)
            peTs = [P.sb('peT%d' % i_, [128, 32], F32) for i_ in range(2)]
            hid = P.sb('hid', [128, 4, 128], BF16)
            for kv in range(2):
                ldflat(K, w1ds[kv].rearrange("p l j -> p (l j)"), d['w1d_l'][layer, kv], ('w1d', kv))
                P.dma('sp', peTs[kv][:], d['pe_l'][layer, kv], writes=[('peT', kv)], slot='r%d' % kv)
            ldflat(K, w2d[:].rearrange("p a b -> p (a b)"), d['w2d_l'][layer], ('w2d', 0))
            ldflat(K, w2v[:].rearrange("p a b -> p (a b)"), d['w2v_l'][layer], 'w2v')
            for kv in range(2):
                w1d, kpe, peT = w1ds[kv], kpes[kv], peTs[kv]
                for l in range(32):
                    P.op('dve', lambda e, l=l, kv=kv, kpe=kpe, peT=peT: e.tensor_scalar(
                        out=kpe[:, l, 0:127], in0=cmpT[:, kv, :].rearrange("p (n r) -> p r n", r=16)[:, l % 16, (l // 16):(l // 16) + 127],
                        scalar1=peT[:, l:l + 1], scalar2=None, op0=ALU.add), reads=['cmpT', ('peT', kv)], writes=[('kpe', kv)])
                for jc in range(2):
                    bks = (3, 4) if jc == 0 else (6, 7)
                    for l in range(32):
                        for g in range(2):
                            rows = slice(64 * g, 64 * g + 64)
                            mm(P, K.pb[bks[g]][:, 0:127], w1d[rows, l, 128 * jc:128 * jc + 128], kpe[rows, l, 0:127], l == 0, l == 31,
                               reads=[('w1d', kv), ('kpe', kv)], writes=['ps%d' % bks[g]])
                    for g in range(2):
                        P.op('act', lambda e, bank=bks[g], g=g, jc=jc: e.activation(out=hid[:, g * 2 + jc, 0:127], in_=K.pb[bank][:, 0:127], func=AF.Silu),
                             reads=['ps%d' % bks[g]], writes=['hid'])
                for g in range(2):
                    if kv == 0:
                        for jc in range(2):
                            mm(P, K.pb[5][:, 0:127], w2d[:, jc, :], hid[:, g * 2 + jc, 0:127], jc == 0, jc == 1,
                               reads=[('w2d', 0), 'hid'], writes=['ps5'])
                        P.op('act', lambda e, g=g: e.copy(out=kcT2[:, g, 0:127], in_=K.pb[5][:, 0:127]), reads=['ps5'], writes=['kcT2'])
                    else:
                        for jc in range(2):
                            mm(P, K.pb[5][0:127, 0:64], hid[:, g * 2 + jc, 0:127], w2v[:, jc, :], jc == 0, jc == 1,
                               reads=['w2v', 'hid'], writes=['ps5'])
                        P.op('act', lambda e, g=g: e.copy(out=vcp[0:127, g, 0, 0:64], in_=K.pb[5][0:127, 0:64]), reads=['ps5'], writes=['vcp'])
                        P.op('dve', lambda e, g=g: e.tensor_copy(out=vcp[0:127, g, 1, 64:128], in_=K.pb[5][0:127, 0:64]), reads=['ps5'], writes=['vcp'])
            P.barrier()
            P.emit()
        P.stack = st
        P.op('pool', lambda e: e.memset(ab[:, 24576:32768], 0.0), writes=['VS'])
        import os as _os
        _stop = int(_os.environ.get('NSA_STOP', '9'))
        if _stop <= 1:
            P.barrier(); P.emit(); P.stack = _prev_stack
            return

        esb = P.sb('esb', [128, 4, 128], F32)
        rs = P.sb('rs', [128, 8], F32)
        ps4 = P.sb('ps4', [128, 128], F32)
        imp = P.sb('imp', [128, 32], F32)
        imp2 = P.sb('imp2', [128, 32], F32)
        m8 = P.sb('m8', [128, 16], F32)
        nsels = [P.sb('nsel%d' % i_, [128, 32], BF16) for i_ in range(3)]

        def selA(k_, i, g):
            nsel, nsn = nsels[k_ % 3], ('nsel', k_ % 3)
            ps = K.pb[6]
            psn = 'ps6'
            first = True
            for par_ in range(2):
                for r in (par_, par_ + 2):
                    h = 4 * g + r
                    pr, par = divmod(h, 2)
                    rows = slice(64 * par, 64 * par + 64)
                    mm(P, ps[:, 128 * r:128 * r + 128], QN[rows, pr, 128 * i:128 * i + 128], kcT2[rows, g, :], first, False,
                       reads=['QN', 'kcT2'], writes=[psn])
                    first = False
                for r in (par_, par_ + 2):
                    mm(P, ps[:, 128 * r:128 * r + 128], ident[:], K.k['k_mwide'][:, 120 - 8 * i:120 - 8 * i + 128], False, (par_ == 1 and r == 3),
                       reads=['k_identb', 'k_mwide'], writes=[psn])
            P.op('act', lambda e: e.activation(out=esb[:].rearrange("p a b -> p (a b)"), in_=ps[:, :], func=AF.Exp, scale=SCALE),
                 reads=[psn], writes=['esb'])
            P.op('dve', lambda e: e.tensor_reduce(out=rs[:, 0:4], in_=esb[:], axis=AX.X, op=ALU.add), reads=['esb'], writes=['rs'])
            P.op('dve', lambda e: e.reciprocal(out=rs[:, 4:8], in_=rs[:, 0:4]), reads=['rs'], writes=['rs'])
            P.op('dve', lambda e: e.tensor_tensor(out=esb[:], in0=esb[:], in1=rs[:, 4:8].unsqueeze(2).to_broadcast([128, 4, 128]), op=ALU.mult),
                 reads=['esb', 'rs'], writes=['esb'])
            P.op('dve', lambda e: e.tensor_reduce(out=ps4[:], in_=esb[:].rearrange("p h n -> p n h"), axis=AX.X, op=ALU.add),
                 reads=['esb'], writes=['ps4'])
            P.op('dve', lambda e: e.tensor_reduce(out=imp[:], in_=ps4[:].rearrange("p (j f) -> p j f", f=4), axis=AX.X, op=ALU.add),
                 reads=['ps4'], writes=['imp'])
            P.op('dve', lambda e: e.tensor_tensor(out=imp[:, 1:32], in0=imp[:, 1:32], in1=ps4[:, 3:124:4], op=ALU.add),
                 reads=['imp', 'ps4'], writes=['imp'])
            P.op('dve', lambda e: e.tensor_tensor(out=imp[:], in0=imp[:], in1=K.k['k_F'][:, 32 * (i - 8):32 * (i - 8) + 32], op=ALU.max),
                 reads=['imp', 'k_F'], writes=['imp'])
            P.op('dve', lambda e: e.max(out=m8[:, 0:8], in_=imp[:]), reads=['imp'], writes=['m8'])
            P.op('dve', lambda e: e.match_replace(out=imp2[:], in_to_replace=m8[:, 0:8], in_values=imp[:], imm_value=-1e30),
                 reads=['imp', 'm8'], writes=['imp2'])
            P.op('dve', lambda e: e.max(out=m8[:, 8:16], in_=imp2[:]), reads=['imp2'], writes=['m8'])
            P.op('dve', lambda e: e.tensor_scalar(out=nsel[:], in0=imp[:], scalar1=m8[:, 15:16], scalar2=NEG, op0=ALU.is_lt, op1=ALU.mult),
                 reads=['imp', 'm8'], writes=[nsn])

        def selB(k_, i, g):
            nsel, nsn = nsels[k_ % 3], ('nsel', k_ % 3)
            pst = K.pb[4][:].bitcast(BF16)
            P.op('pe', lambda e: e.transpose(pst[0:32, 0:128], nsel[:, :], ident[:]), reads=[nsn, 'k_identb'], writes=['ps4'])
            P.op('act', lambda e: e.copy(out=negT[0:32, g, 128 * i:128 * i + 128], in_=pst[0:32, 0:128]), reads=['ps4'], writes=['negT'])

        sel_list = [(i, g) for i in range(8, NT) for g in range(2)]
        sel_pos = [0]

        def sel_step():
            k_ = sel_pos[0]
            sel_pos[0] += 1
            if 0 <= k_ - 2 < len(sel_list):
                selB(k_ - 2, *sel_list[k_ - 2])
            if k_ < len(sel_list):
                selA(k_, *sel_list[k_])

        sel_ticks = [0]

        def sel_tick():
            sel_ticks[0] += 1
            sel_step()

        with ExitStack() as pst:
            P.stack = pst
            W = {'raw': [P.sb('raw%d' % i, [128, CH], BF16) for i in range(2)],
                 't1': [P.sb('t1_%d' % i, [128, CH], F32) for i in range(2)],
                 't2': [P.sb('t2_%d' % i, [128, CH], F32) for i in range(2)], 'inre': None, 'swbanks': (7, 5)}
            load_scoped_consts(K, ['k_cos', 'k_sin'])
            wbq = P.sb('wbq', [128, KC, 512], BF16)
            wkd = P.sb('wkd', [128, KC, 2, 2, 2, 64], BF16)
            wvv = P.sb('wvv', [128, KC, 2, 128], BF16)
            wbl = P.sb('wbl', [128, KC, 24], BF16)
            ldflat(K, wbq[:].rearrange("p k n -> p (k n)"), wn[:, 2048:6144], 'wbq')
            ldflat(K, wkd[:].rearrange("p k a b c d -> p (k a b c d)"), wn[:, 6144:10240], 'wkd')
            ldflat(K, wvv[:].rearrange("p k a n -> p (k a n)"), wn[:, 10240:12288], 'wvv')
            ldflat(K, wbl[:].rearrange("p k n -> p (k n)"), wn[:, 12288:12480], 'wbl')
            P.report('nsa-proj')
            rt = 0
            for pr in range(4):
                for c in range(NCH):
                    bank = rt % 2
                    proj_fm(K, K.pb[bank], 'ps%d' % bank, wbq, 'wbq', slice(128 * pr, 128 * pr + 128), CH * c, CH)
                    rope_tile(K, K.pb[bank], 'ps%d' % bank, CH * c, CH, QR[:, pr, CH * c:CH * c + CH], 'QR', W,
                              nope_ap=QN[:, pr, CH * c:CH * c + CH], nope_name='QN', i=rt)
                    rt += 1
            for w_, dst, dn in ((0, KS, 'KS'), (1, KW, 'KW')):
                for g in range(2):
                    for c in range(NCH):
                        bank = rt % 2
                        for kc in range(KC):
                            mm(P, K.pb[bank][:, :], wkd[:, kc, w_, g, :, :].rearrange("p a b -> p (a b)"), K.hT[:, kc, CH * c:CH * c + CH],
                               kc == 0, kc == KC - 1, reads=['wkd', 'hT'], writes=['ps%d' % bank])
                        rope_tile(K, K.pb[bank], 'ps%d' % bank, CH * c, CH, dst[:, g, CH * c:CH * c + CH], dn, W, i=rt)
                        rt += 1
                        sel_tick()
            for w_, dst, dn in ((0, VS, 'VS'), (1, VW[:], 'VW')):
                for t in range(NT):
                    bank = 2 + (t % 2)
                    for kc in range(KC):
                        mm(P, K.pb[bank][:, 0:128], K.hT[:, kc, 128 * t:128 * t + 128], wvv[:, kc, w_, :], kc == 0, kc == KC - 1,
                           reads=['hT', 'wvv'], writes=['ps%d' % bank])
                    psv = K.pb[bank][:, 0:128].rearrange("p (g dd) -> p g dd", g=2)
                    P.op('act', lambda e, dst=dst, t=t, psv=psv: e.copy(out=dst[:, t, :, 0, 0:64], in_=psv), reads=['ps%d' % bank], writes=[dn])
                    P.op('dve', lambda e, dst=dst, t=t, psv=psv: e.tensor_copy(out=dst[:, t, :, 1, 64:128], in_=psv), reads=['ps%d' % bank], writes=[dn])
                    sel_tick()
            for c in range(NCH):
                bank = 4 + (c % 2)
                for kc in range(KC):
                    mm(P, K.pb[bank][0:24, :], wbl[:, kc, :], K.hT[:, kc, CH * c:CH * c + CH], kc == 0, kc == KC - 1,
                       reads=['wbl', 'hT'], writes=['ps%d' % bank])
                P.op('act', lambda e, bank=bank, c=c: e.activation(out=gateT[:, CH * c:CH * c + CH], in_=K.pb[bank][0:24, :], func=AF.Sigmoid),
                     reads=['ps%d' % bank], writes=['gateT'])
            while sel_pos[0] < len(sel_list) + 2:
                sel_step()
            P.barrier()
            P.emit()
        P.stack = st

        P.report('nsa-afterproj')
        if _stop <= 2:
            P.barrier(); P.emit(); P.stack = _prev_stack
            return
        if _stop <= 3:
            P.barrier(); P.emit(); P.stack = _prev_stack
            return
        load_scoped_consts(K, ['k_cneg', 'k_E'])
        pts = [P.sb('pt%d' % i, [128, CH], BF16) for i in range(4)]
        rden = P.sb('rden', [128, CH], F32)
        wgt = P.sb('wgt', [128, CH], F32)
        yacc = P.sb('yacc', [128, CH], F32)
        ytmp = P.sb('ytmp', [128, CH], F32)
        E = K.k['k_E']
        it = 0
        pc = 0
        pipe = Pipe(K, depth=2)
        for pr in range(4):
            g = pr // 2
            for c in range(NCH):
                for br in range(3):
                    nb, db = (3, 4) if pc % 2 == 0 else (5, 6)
                    pc += 1
                    units = []
                    for par in range(2):
                        rows = slice(64 * par, 64 * par + 64)
                        ones = K.k['k_oneslo'] if par == 0 else K.k['k_oneshi']
                        onesn = 'k_oneslo' if par == 0 else 'k_oneshi'
                        if br == 0:
                            sm = [(kcT2[rows, g, :], QN[rows, pr, CH * c:CH * c + CH], 0, CH, ['kcT2', 'QN']),
                                  (ident[:], K.k['k_cneg'][:, CH * c:CH * c + CH], 0, CH, ['k_identb', 'k_cneg'])]
                            pv = [(nb, vcp[:, g, par, :], ['vcp'], 0, CH, 0), (db, ones[:], [onesn], 0, CH, 0)]
                            units.append((sm, CH, pv, par, 1))
                        else:
                            Kt = KS if br == 1 else KW
                            Kn = 'KS' if br == 1 else 'KW'
                            Vt = VS if br == 1 else VW
                            Vn = 'VS' if br == 1 else 'VW'
                            kt0 = 0 if br == 1 else max(0, 4 * c - 4)
                            for kt in range(kt0, 4 * c + 4):
                                q0 = max(128 * kt, CH * c)
                                q1 = CH * c + CH if br == 1 else min(128 * kt + 640, CH * c + CH)
                                nq = q1 - q0
                                sm = [(Kt[rows, g, 128 * kt:128 * kt + 128], QR[rows, pr, q0:q1], 0, nq, [Kn, 'QR'])]
                                if br == 1 and c >= 2:
                                    sm.append((E[:, 128 * kt:128 * kt + 128], negT[:, g, q0:q1], 0, nq, ['k_E', 'negT']))
                                if q0 == 128 * kt:
                                    sm.append((ident[:], K.k['k_mpair'][:, 0:128], 0, 128, ['k_identb', 'k_mpair']))
                                if br == 2 and q1 == 128 * kt + 640:
                                    sm.append((ident[:], K.k['k_mfar'][:], nq - 128, 128, ['k_identb', 'k_mfar']))
                                pv = [(nb, Vt[:, kt, g, par, :], [Vn], 0, nq, q0 - CH * c), (db, ones[:], [onesn], 0, nq, q0 - CH * c)]
                                units.append((sm, nq, pv, par, 1))

                    def pre(nb=nb, db=db):
                        acc_init(K, nb)
                        acc_init(K, db)

                    def post(nb=nb, db=db, pr=pr, c=c, br=br):
                        mm(P, K.pb[7][:, :], K.k['k_selg'][:, 128 * (pr * 3 + br):128 * (pr * 3 + br) + 128], gateT[:, CH * c:CH * c + CH], True, True,
                           reads=['k_selg', 'gateT'], writes=['ps7'])
                        if br == 0:
                            P.op('dve', lambda e: e.tensor_scalar(out=rden[:], in0=K.pb[db][:, :], scalar1=1e-30, scalar2=None, op0=ALU.max),
                                 reads=['ps%d' % db], writes=['rden'])
                            P.op('dve', lambda e: e.reciprocal(out=rden[:], in_=rden[:]), reads=['rden'], writes=['rden'])
                        else:
                            P.op('dve', lambda e: e.reciprocal(out=rden[:], in_=K.pb[db][:, :]), reads=['ps%d' % db], writes=['rden'])
                        P.op('dve', lambda e: e.tensor_tensor(out=wgt[:], in0=K.pb[7][:, :], in1=rden[:], op=ALU.mult), reads=['ps7', 'rden'], writes=['wgt'])
                        if br == 0:
                            P.op('dve', lambda e: e.tensor_tensor(out=yacc[:], in0=K.pb[nb][:, :], in1=wgt[:], op=ALU.mult),
                                 reads=['ps%d' % nb, 'wgt'], writes=['yacc'])
                        elif br == 1:
                            P.op('dve', lambda e: e.tensor_tensor(out=ytmp[:], in0=K.pb[nb][:, :], in1=wgt[:], op=ALU.mult),
                                 reads=['ps%d' % nb, 'wgt'], writes=['ytmp'])
                            P.op('pool', lambda e: e.tensor_tensor(out=yacc[:], in0=yacc[:], in1=ytmp[:], op=ALU.add), reads=['yacc', 'ytmp'], writes=['yacc'])
                        else:
                            P.op('dve', lambda e: e.tensor_tensor(out=ytmp[:], in0=K.pb[nb][:, :], in1=wgt[:], op=ALU.mult),
                                 reads=['ps%d' % nb, 'wgt'], writes=['ytmp'])
                            P.op('pool', lambda e: e.tensor_tensor(out=K.ybT[:, pr, CH * c:CH * c + CH], in0=yacc[:], in1=ytmp[:], op=ALU.add),
                                 reads=['yacc', 'ytmp'], writes=['ybT'])
                    ua_ = [u for u in units if u[3] == 0]
                    ub_ = [u for u in units if u[3] == 1]
                    assert len(ua_) == len(ub_)
                    for ui, (a_, b_) in enumerate(zip(ua_, ub_)):
                        da = {'bank': it % 3, 'sm': a_[0], 'nq': a_[1], 'pv': a_[2], 'nqk': a_[4], 'pt': pts[it % 4], 'ptname': ('pt', it % 4),
                              'first': ui == 0}
                        it += 1
                        db_ = {'bank': it % 3, 'sm': b_[0], 'nq': b_[1], 'pv': b_[2], 'nqk': b_[4], 'pt': pts[it % 4], 'ptname': ('pt', it % 4),
                               'post': post if ui == len(ua_) - 1 else None}
                        it += 1
                        pipe.pair(da, db_)
        pipe.flush()
        P.report('nsa-attn')
        tap_any(K, 'yb%d' % layer, K.ybT[:].rearrange("p a s -> p (a s)"), ['ybT'], [128, 4 * S], BF16)
        P.barrier()
        P.emit()
    P.stack = _prev_stack


def phase_merge(K, layer):
    P, d = K.P, K.d
    _prev_stack = P.stack
    with ExitStack() as st:
        P.stack = st
        mT = P.sb('mT', [128, KC, S], BF16)
        wmg = [P.sb('wmg%d' % i, [128, 22, 128], BF16) for i in range(2)]
        wo = [P.sb('wo%d' % i, [128, KC, 128], BF16) for i in range(2)]
        sga = [P.sb('sga%d' % i, [128, CH], F32) for i in range(1)]
        sgb = [P.sb('sgb%d' % i, [128, CH], F32) for i in range(1)]
        t1 = [P.sb('mt1_%d' % i, [128, CH], F32) for i in range(1)]
        t2 = [P.sb('mt2_%d' % i, [128, CH], F32) for i in range(1)]

        def ldf(f):
            ldflat(K, wmg[f % 2][:].rearrange("p k n -> p (k n)"), d['wmrg_l'][layer, f], ('wmg', f % 2))
        ldf(0)
        it = 0
        for f in range(KC):
            if f + 1 < KC:
                ldf(f + 1)
            s = f % 2
            for c in range(NCH):
                j = 0
                tk = slice(CH * c, CH * c + CH)
                for kc in range(KC):
                    mm(P, K.pb[0][:, :], wmg[s][:, kc, :], K.hT[:, kc, tk], kc == 0, kc == KC - 1, reads=[('wmg', s), 'hT'], writes=['ps0'])
                P.op('act', lambda e, j=j: e.activation(out=sga[j][:], in_=K.pb[0][:, :], func=AF.Sigmoid), reads=['ps0'], writes=[('sga', j)])
                for kc in range(KC):
                    mm(P, K.pb[1][:, :], wmg[s][:, 8 + kc, :], K.hT[:, kc, tk], kc == 0, kc == KC - 1, reads=[('wmg', s), 'hT'], writes=['ps1'])
                P.op('act', lambda e, j=j: e.activation(out=sgb[j][:], in_=K.pb[1][:, :], func=AF.Sigmoid), reads=['ps1'], writes=[('sgb', j)])
                for kc in range(2):
                    mm(P, K.pb[2][:, :], wmg[s][:, 16 + kc, :], K.yaT[:, kc, tk], kc == 0, kc == 1, reads=[('wmg', s), 'yaT'], writes=['ps2'])
                P.op('dve', lambda e, j=j: e.tensor_tensor(out=t1[j][:], in0=K.pb[2][:, :], in1=sga[j][:], op=ALU.mult),
                     reads=['ps2', ('sga', j)], writes=[('mt1', j)])
                for kc in range(4):
                    mm(P, K.pb[3][:, :], wmg[s][:, 18 + kc, :], K.ybT[:, kc, tk], kc == 0, kc == 3, reads=[('wmg', s), 'ybT'], writes=['ps3'])
                P.op('dve', lambda e, j=j: e.tensor_tensor(out=t2[j][:], in0=K.pb[3][:, :], in1=sgb[j][:], op=ALU.mult),
                     reads=['ps3', ('sgb', j)], writes=[('mt2', j)])
                P.op('pool', lambda e, j=j, f=f, tk=tk: e.tensor_tensor(out=mT[:, f, tk], in0=t1[j][:], in1=t2[j][:], op=ALU.add),
                     reads=[('mt1', j), ('mt2', j)], writes=['mT'])
                it += 1
        P.report('merge')
        ldflat(K, wo[0][:].rearrange("p k n -> p (k n)"), d['wout_l'][layer, 0], ('wo', 0))
        for f in range(KC):
            if f + 1 < KC:
                ldflat(K, wo[(f + 1) % 2][:].rearrange("p k n -> p (k n)"), d['wout_l'][layer, f + 1], ('wo', (f + 1) % 2))
            s = f % 2
            for c in range(NCH):
                bank = 4 + (c % 2)
                tk = slice(CH * c, CH * c + CH)
                for kc in range(KC):
                    mm(P, K.pb[bank][:, :], wo[s][:, kc, :], mT[:, kc, tk], kc == 0, kc == KC - 1, reads=[('wo', s), 'mT'], writes=['ps%d' % bank])
                P.op('dve', lambda e, bank=bank, f=f, tk=tk: e.scalar_tensor_tensor(
                    out=K.xT[:, f, tk], in0=K.pb[bank][:, :], scalar=K.modc[:, layer, 16 + f:17 + f], in1=K.xT[:, f, tk],
                    op0=ALU.mult, op1=ALU.add), reads=['ps%d' % bank, 'xT', 'modc'], writes=['xT'])


def phase_ffn(K, layer):
    P, d = K.P, K.d
    HALF = S // 2
    _prev_stack = P.stack
    with ExitStack() as st:
        P.stack = st
        actT = P.sb('actT', [128, NFF, HALF], BF16)
        NWU = 4
        wup = K.pre_wup
        NWD = 3
        wd = [P.sb('wd%d' % i, [128, NFF, 128], BF16) for i in range(NWD)]
        ub = [[P.sb('ub%d_%d' % (gv, i), [128, CH + 2], F32) for i in range(2)] for gv in range(2)]
        tt = [[P.sb('tt%d_%d' % (gv, i), [128, CH], F32) for i in range(2)] for gv in range(2)]
        sg = [P.sb('sgt%d' % i, [128, CH], F32) for i in range(1)]
        P.op('pool', lambda e: e.memset(K.halo[:].rearrange("p a b -> p (a b)"), 0.0), writes=['halo'])

        def ldu(fc):
            s = fc % NWU
            ldflat(K, wup[s][:].rearrange("p k a n -> p (k a n)"), d['wup_l'][layer, fc], ('wup', s))

        def ldd(f):
            ldflat(K, wd[f % NWD][:].rearrange("p k n -> p (k n)"), d['wdn_l'][layer, f], ('wd', f % NWD))
        it = 0
        tail = [None]
        for hf in range(2):
            if hf > 0:
                for f_ in range(NWU - 1):
                    ldu(f_)
            for fc in range(NFF):
                if fc + NWU - 1 < NFF:
                    ldu(fc + NWU - 1)
                if fc == NFF - 4:
                    ldd(0)
                if fc == NFF - 2:
                    ldd(1)
                s = fc % NWU
                for cc in range(2):
                    c = 2 * hf + cc
                    tk = slice(CH * c, CH * c + CH)
                    j = it % 2
                    it += 1
                    for gv in range(2):
                        bank = (0, 2, 6)[(it - 1) % 3] + gv
                        ch = fc + NFF * gv
                        for kc in range(KC):
                            mm(P, K.pb[bank][:, :], wup[s][:, kc, gv, :], K.hT[:, kc, tk], kc == 0, kc == KC - 1,
                               reads=[('wup', s), 'hT'], writes=['ps%d' % bank])
                        u = ub[gv][j]
                        un = ('ub', gv, j)
                        t = tt[gv][j]
                        tn = ('tt', gv, j)
                        P.op('act', lambda e, u=u, bank=bank: e.copy(out=u[:, 2:CH + 2], in_=K.pb[bank][:, :]), reads=['ps%d' % bank], writes=[un])
                        P.op('pool', lambda e, u=u, ch=ch: e.tensor_copy(out=u[:, 0:2], in_=K.halo[:, ch, :]), reads=['halo'], writes=[un])
                        P.op('pool', lambda e, u=u, ch=ch: e.tensor_copy(out=K.halo[:, ch, :], in_=u[:, CH:CH + 2]), reads=[un], writes=['halo'])
                        cw = K.cvw[:, layer, :, ch:ch + 1]
                        P.op('act', lambda e, t=t, cw=cw, bank=bank: e.activation(out=t[:], in_=K.pb[bank][:, :], func=AF.Identity,
                                                                                  scale=cw[:, 2, :], bias=cw[:, 3, :]),
                             reads=['ps%d' % bank, 'cvw'], writes=[tn])
                        P.op('dve', lambda e, u=u, t=t, cw=cw: e.scalar_tensor_tensor(out=t[:], in0=u[:, 1:CH + 1], scalar=cw[:, 1, :], in1=t[:],
                                                                                      op0=ALU.mult, op1=ALU.add), reads=[un, tn, 'cvw'], writes=[tn])
                        P.op('dve', lambda e, u=u, t=t, cw=cw: e.scalar_tensor_tensor(out=t[:], in0=u[:, 0:CH], scalar=cw[:, 0, :], in1=t[:],
                                                                                      op0=ALU.mult, op1=ALU.add), reads=[un, tn, 'cvw'], writes=[tn])
                    if tail[0] is not None:
                        tail[0]()

                    def _tail(j=j, fc=fc, cc=cc):
                        P.op('act', lambda e: e.activation(out=sg[0][:], in_=tt[0][j][:], func=AF.Silu), reads=[('tt', 0, j)], writes=[('sgt', 0)])
                        P.op('pool', lambda e: e.tensor_tensor(out=actT[:, fc, CH * cc:CH * cc + CH], in0=sg[0][:], in1=tt[1][j][:], op=ALU.mult),
                             reads=[('sgt', 0), ('tt', 1, j)], writes=['actT'])
                    tail[0] = _tail
            tail[0]()
            tail[0] = None
            for f in range(KC):
                if f + 2 < KC:
                    ldd(f + 2)
                s = f % NWD
                for cc in range(2):
                    c = 2 * hf + cc
                    bank = 4 + cc
                    tk = slice(CH * c, CH * c + CH)
                    for k2 in range(NFF):
                        mm(P, K.pb[bank][:, :], wd[s][:, k2, :], actT[:, k2, CH * cc:CH * cc + CH], k2 == 0, k2 == NFF - 1,
                           reads=[('wd', s), 'actT'], writes=['ps%d' % bank])
                    P.op('dve', lambda e, bank=bank, f=f, tk=tk: e.scalar_tensor_tensor(
                        out=K.xT[:, f, tk], in0=K.pb[bank][:, :], scalar=K.modc[:, layer, 40 + f:41 + f], in1=K.xT[:, f, tk],
                        op0=ALU.mult, op1=ALU.add), reads=['ps%d' % bank, 'xT', 'modc'], writes=['xT'])
        P.report('ffn')
        P.barrier()
        P.emit()
    P.stack = _prev_stack


def phase_final(K):
    P = K.P
    identf = K.k['k_identf']
    _prev_stack = P.stack
    with ExitStack() as st:
        P.stack = st
        sq = [P.sb('sq%d' % i, [128, CH], F32) for i in range(4)]
        rstd = [P.sb('rstd%d' % i, [128, CH], F32) for i in range(2)]
        tmp = [P.sb('ntmp%d' % i, [128, CH], F32) for i in range(4)]
        K.epsb = P.sb('epsb', [128, 1], F32)
        P.op('pool', lambda e: e.memset(K.epsb[:], 1024.0 * 1e-6), writes=['epsb'])
        ot = [P.sb('ot%d' % i, [128, D], F32) for i in range(4)]

        def make_out_fn(c):
            def out_fn(kc, t_, tname):
                bank = 2 + (kc % 4)
                for j in range(4):
                    P.op('pe', lambda e, j=j: e.transpose(K.pb[bank][:, 128 * j:128 * j + 128], t_[:, 128 * j:128 * j + 128], identf[:]),
                         reads=[tname, 'k_identf'], writes=['ps%d' % bank])
                dstv = [ot[j][:, 128 * kc:128 * kc + 128] for j in range(4)]
                for j in range(4):
                    if kc % 2 == 0:
                        P.op('act', lambda e, j=j: e.copy(out=dstv[j], in_=K.pb[bank][:, 128 * j:128 * j + 128]), reads=['ps%d' % bank], writes=[('ot', j)])
                    else:
                        P.op('dve', lambda e, j=j: e.tensor_copy(out=dstv[j], in_=K.pb[bank][:, 128 * j:128 * j + 128]), reads=['ps%d' % bank], writes=[('ot', j)])
            return out_fn

        def after_chunk(c):
            for j in range(4):
                t = 4 * c + j
                P.dma('sp', K.out[128 * t:128 * t + 128, :], ot[j][:], reads=[('ot', j)], slot='o%d' % j)
        norm_all(K, K.gfin[:], make_out_fn, (sq, rstd, tmp), after_chunk)
        P.barrier()
        P.emit()
    P.stack = _prev_stack


_CACHE = {}


def _prep_inputs(inputs):
    f = lambda a: np.ascontiguousarray(np.asarray(a, dtype=np.float32))
    shared = relayout_weights(inputs)
    for nm in ('norm1_g', 'norm2_g', 'final_g', 'b_mod', 'conv_w', 'conv_b'):
        shared[nm] = f(inputs[nm]).reshape(IN_SHAPES[nm])
    shared.update(make_consts())
    maps = []
    x = f(inputs['x'])
    c = f(inputs['c'])
    for b in range(8):
        m = dict(shared)
        m['x'] = x[b]
        m['c'] = c[b].reshape(8, 128)
        maps.append(m)
    return maps


def kernel(**inputs):
    if 'nc' not in _CACHE:
        _CACHE['nc'] = build(2)[0]
    nc = _CACHE['nc']
    maps = _prep_inputs(inputs)
    res = run_bass_kernel_spmd(nc, maps, core_ids=list(range(8)))
    out = np.stack([np.asarray(res.results[b]['out'], dtype=np.float32) for b in range(8)], axis=0)
    return out
```

```python
import numpy as np
import ml_dtypes
from contextlib import ExitStack
import concourse.bass as bass
import concourse.mybir as mybir
from concourse.bass_utils import run_bass_kernel_spmd

F32 = mybir.dt.float32
BF16 = mybir.dt.bfloat16
ALU = mybir.AluOpType
AF = mybir.ActivationFunctionType
AX = mybir.AxisListType
ENGS = ['pe', 'act', 'dve', 'pool', 'sp']

S, D, KC, NT, NCH, CH = 2048, 1024, 8, 16, 4, 512
DFF = 2816
NFF = 22
NEG = -30000.0
SCALE = 0.125
N_IN = 5656
OFF_AQ, OFF_AK, OFF_AV, OFF_BQ = 0, 768, 1536, 2304
OFF_KC, OFF_VC, OFF_KSL, OFF_VSL, OFF_KWN, OFF_VWN = 2816, 2944, 3072, 3200, 3328, 3456
OFF_BLOG, OFF_GA, OFF_GB = 3584, 3608, 4632


class Prog:
    def __init__(self, nc, stack):
        self.nc = nc
        self.stack = stack
        self.gstack = stack
        self.q = {e: [] for e in ENGS}
        self.cnt = {e: 0 for e in ENGS}
        self.sems = {e: stack.enter_context(nc.semaphore('s_' + e)) for e in ENGS}
        self.waited = {e: {} for e in ENGS}
        self.res = {}
        self.dslots = {}

    def sb(self, name, shape, dtype, glob=False):
        st = self.gstack if glob else self.stack
        self.nalloc = getattr(self, 'nalloc', 0) + 1
        return st.enter_context(self.nc.sbuf_tensor('%s_%d' % (name, self.nalloc), list(shape), dtype))

    def _norm(self, reads, writes):
        excl = [r for r in reads if isinstance(r, str) and r.startswith('ps')]
        if excl:
            reads = [r for r in reads if r not in excl]
            writes = list(writes) + excl
        return reads, writes

    def _deps(self, eng, reads, writes):
        need = {}

        def add(k, v):
            if need.get(k, 0) < v:
                need[k] = v
        reads, writes = self._norm(reads, writes)
        for r in reads:
            st = self.res.get(r)
            if st and st['w']:
                add(*st['w'])
        for w in writes:
            st = self.res.get(w)
            if st:
                if st['w']:
                    add(*st['w'])
                for k, v in st['r'].items():
                    add(k, v)
        out = []
        for k, v in need.items():
            if k == eng and eng == 'pe':
                continue
            if self.waited[eng].get(k, 0) >= v:
                continue
            self.waited[eng][k] = v
            out.append((k, v))
        return out

    def _mark(self, tok, reads, writes):
        reads, writes = self._norm(reads, writes)
        for r in reads:
            st = self.res.setdefault(r, {'w': None, 'r': {}})
            if st['r'].get(tok[0], 0) < tok[1]:
                st['r'][tok[0]] = tok[1]
        for w in writes:
            self.res[w] = {'w': tok, 'r': {}}

    def _sem(self, k):
        return self.sems[k] if k in self.sems else self.dslots[k]['sem']

    def op(self, eng, fn, reads=(), writes=()):
        waits = [(self._sem(k), v) for k, v in self._deps(eng, reads, writes)]
        self.cnt[eng] += 1
        tok = (eng, self.cnt[eng])
        mysem = self.sems[eng]

        def run(e, waits=waits, fn=fn, mysem=mysem):
            for s, v in waits:
                e.wait_ge(s, v)
            fn(e).then_inc(mysem, 1)
        self.q[eng].append(run)
        self._mark(tok, reads, writes)
        return tok

    def dma(self, eng, out, in_, reads=(), writes=(), slot='d0'):
        slot = eng + '_' + slot
        if slot not in self.dslots:
            self.dslots[slot] = {'sem': self.gstack.enter_context(self.nc.semaphore('d_' + slot)), 'n': 0}
        ds = self.dslots[slot]
        deps = self._deps(eng, reads, writes)
        if ds['n'] > 0 and self.waited[eng].get(slot, 0) < 16 * ds['n']:
            self.waited[eng][slot] = 16 * ds['n']
            deps = [d for d in deps if d[0] != slot] + [(slot, 16 * ds['n'])]
        waits = [(self._sem(k), v) for k, v in deps]
        ds['n'] += 1
        tok = (slot, 16 * ds['n'])
        dsem = ds['sem']

        def run(e, waits=waits, out=out, in_=in_, dsem=dsem):
            for s, v in waits:
                e.wait_ge(s, v)
            e.dma_start(out=out, in_=in_).then_inc(dsem, 16)
        self.q[eng].append(run)
        self._mark(tok, reads, writes)
        return tok

    def report(self, tag):
        import os
        if os.environ.get('SBUF_REPORT'):
            print('SBUF', tag, 'remaining', self.nc.sbuf_bytes_remaining, flush=True)

    def barrier(self):
        targets = [(k, 16 * ds['n']) for k, ds in self.dslots.items() if ds['n'] > 0]
        targets += [(e, self.cnt[e]) for e in ENGS if self.cnt[e] > 0]
        for eng in ENGS:
            waits = []
            for k, v in targets:
                if k == eng:
                    continue
                if self.waited[eng].get(k, 0) >= v:
                    continue
                self.waited[eng][k] = v
                waits.append((self._sem(k), v))

            def run(e, waits=waits):
                for s, v in waits:
                    e.wait_ge(s, v)
            self.q[eng].append(run)

    def emit(self):
        q = self.q
        with self.nc.Block() as block:
            @block.tensor
            def _(e):
                for f in q['pe']:
                    f(e)

            @block.scalar
            def _(e):
                for f in q['act']:
                    f(e)

            @block.vector
            def _(e):
                for f in q['dve']:
                    f(e)

            @block.gpsimd
            def _(e):
                for f in q['pool']:
                    f(e)

            @block.sync
            def _(e):
                for f in q['sp']:
                    f(e)
        self.q = {e: [] for e in ENGS}


def make_consts():
    bf = ml_dtypes.bfloat16
    C = {}
    C['k_identb'] = np.eye(128, dtype=np.float32).astype(bf)
    C['k_identf'] = np.eye(128, dtype=np.float32)
    C['k_onesf'] = np.ones((128, 128), np.float32)
    pm = np.zeros((128, 128), np.float32)
    for m in range(128):
        blk, d = divmod(m, 64)
        pm[blk * 64 + (d + 32) % 64, m] = 1.0
    C['k_pm'] = pm.astype(bf)
    inv = (1.0 / (np.float32(10000.0) ** (np.arange(0, 64, 2, dtype=np.float32) / np.float32(64)))).astype(np.float32)
    ang = (np.arange(S, dtype=np.float32)[:, None] * inv[None, :]).astype(np.float32)
    cos, sin = np.cos(ang).astype(np.float32), np.sin(ang).astype(np.float32)
    cosT = np.zeros((128, S), np.float32)
    sinT = np.zeros((128, S), np.float32)
    for row in range(128):
        d = row % 64
        cosT[row] = cos[:, d % 32]
        sinT[row] = (-1.0 if d < 32 else 1.0) * sin[:, d % 32]
    C['k_cos'] = cosT
    C['k_sin'] = sinT
    kk = np.arange(128)[:, None]
    qq = np.arange(128)[None, :]
    mc = np.where(kk > qq, NEG, 0.0).astype(np.float32)
    ml = np.where(kk < qq, NEG, 0.0).astype(np.float32)
    mf = np.where(kk <= qq, NEG, 0.0).astype(np.float32)
    C['k_mpair'] = np.concatenate([mc, ml], 1).astype(bf)
    C['k_mc4'] = np.tile(mc, (1, 4)).astype(bf)
    C['k_mfar'] = mf.astype(bf)
    n = np.arange(128)[:, None, None]
    c = np.arange(4)[None, :, None]
    q = np.arange(512)[None, None, :]
    cneg = np.where((16 * n + 31 > 512 * c + q) | (n == 127), NEG, 0.0).astype(np.float32)
    C['k_cneg'] = cneg.reshape(128, 2048).astype(bf)
    p = np.arange(128)[:, None]
    u = np.arange(256)[None, :]
    C['k_mwide'] = np.where(16 * (u - 120) + 31 > p, NEG, 0.0).astype(np.float32).astype(bf)
    E = np.zeros((128, 2048), np.float32)
    for j in range(32):
        E[j, 64 * j:64 * j + 64] = 1.0
    C['k_E'] = E.astype(bf)
    Fa = np.full((128, 8, 32), -1e30, np.float32)
    for i in range(8, 16):
        for pp in range(128):
            cur = 2 * i + (1 if pp >= 64 else 0)
            for j in (0, cur, cur - 1):
                Fa[pp, i - 8, j] = 1e9
    C['k_F'] = Fa.reshape(128, 256)
    sg = np.zeros((24, 12, 128), np.float32)
    for h in range(8):
        for b in range(3):
            pr, par = divmod(h, 2)
            sg[h * 3 + b, pr * 3 + b, par * 64:(par + 1) * 64] = 1.0
    C['k_selg'] = sg.reshape(24, 1536).astype(bf)
    ol = np.zeros((128, 128), np.float32)
    ol[:, 0:64] = 1.0
    oh = np.zeros((128, 128), np.float32)
    oh[:, 64:128] = 1.0
    C['k_oneslo'] = ol.astype(bf)
    C['k_oneshi'] = oh.astype(bf)
    C['k_zeros'] = np.zeros((128, 128), np.float32).astype(bf)
    C['k_one1'] = np.ones((1, 1), np.float32)
    return C


CONST_SHAPES = {
    'k_identb': ([128, 128], BF16), 'k_identf': ([128, 128], F32), 'k_onesf': ([128, 128], F32),
    'k_pm': ([128, 128], BF16), 'k_cos': ([128, S], F32), 'k_sin': ([128, S], F32),
    'k_mpair': ([128, 256], BF16), 'k_mc4': ([128, 512], BF16), 'k_mfar': ([128, 128], BF16),
    'k_cneg': ([128, 2048], BF16), 'k_mwide': ([128, 256], BF16), 'k_E': ([128, 2048], BF16),
    'k_F': ([128, 256], F32), 'k_selg': ([24, 1536], BF16), 'k_oneslo': ([128, 128], BF16),
    'k_oneshi': ([128, 128], BF16), 'k_zeros': ([128, 128], BF16), 'k_one1': ([1, 1], F32),
}

IN_SHAPES = {
    'x': [S, D], 'c': [8, 128], 'norm1_g': [2, 8, 128], 'norm2_g': [2, 8, 128], 'final_g': [8, 128],
    'b_mod': [2, 48, 128], 'conv_w': [2, 3, 44, 128], 'conv_b': [2, 44, 128],
    'wmod_l': [2, 12, 128, KC * 512], 'wina_l': [2, 3, 3, 128, KC * 256], 'wnsa_l': [2, 128, 12480],
    'wmrg_l': [2, 8, 128, 22 * 128], 'wout_l': [2, 8, 128, KC * 128], 'wup_l': [2, NFF, 128, KC * 256],
    'wdn_l': [2, 8, 128, NFF * 128], 'w1d_l': [2, 2, 128, 32 * 256], 'w2d_l': [2, 128, 256], 'w2v_l': [2, 128, 128],
    'pe_l': [2, 2, 128, 32],
}
NSA_OFF = {'wkv': (0, 2048), 'wbq': (2048, 6144), 'wkd': (6144, 10240), 'wvv': (10240, 12288), 'wbl': (12288, 12480)}


def relayout_weights(inp):
    f = lambda a: np.asarray(a, dtype=np.float32)

    def blk(w, c0, n):
        k = w.shape[0] // 128
        return w[:, c0:c0 + n].reshape(k, 128, n).transpose(1, 0, 2)
    o = {}
    w_mod, w_in = f(inp['w_mod']), f(inp['w_in'])
    o['wmod_l'] = np.stack([np.stack([blk(w_mod[l], 512 * j, 512).reshape(128, -1) for j in range(12)]) for l in range(2)])
    o['wina_l'] = np.stack([np.stack([np.stack([blk(w_in[l], off + 256 * g, 256).reshape(128, -1) for off in (OFF_AQ, OFF_AK, OFF_AV)])
                                      for g in range(3)]) for l in range(2)])
    nsa = []
    for l in range(2):
        wkv = blk(w_in[l], OFF_KC, 256).reshape(128, -1)
        wbq = blk(w_in[l], OFF_BQ, 512).reshape(128, -1)
        wkd = np.zeros((128, KC, 2, 2, 2, 64), np.float32)
        for w_, off in ((0, OFF_KSL), (1, OFF_KWN)):
            for g in range(2):
                b = blk(w_in[l], off + 64 * g, 64)
                wkd[:, :, w_, g, 0, :] = b
                wkd[:, :, w_, g, 1, :] = b
        wvv = np.stack([blk(w_in[l], OFF_VSL, 128), blk(w_in[l], OFF_VWN, 128)], axis=2).reshape(128, -1)
        wbl = blk(w_in[l], OFF_BLOG, 24).reshape(128, -1)
        nsa.append(np.concatenate([wkv, wbq, wkd.reshape(128, -1), wvv, wbl], axis=1))
    o['wnsa_l'] = np.stack(nsa)
    wbra, wbrb, wout = f(inp['w_br_a']), f(inp['w_br_b']), f(inp['w_out'])
    o['wmrg_l'] = np.stack([np.stack([np.concatenate([blk(w_in[l], OFF_GA + 128 * fi, 128), blk(w_in[l], OFF_GB + 128 * fi, 128),
                                                      blk(wbra[l], 128 * fi, 128), blk(wbrb[l], 128 * fi, 128)], axis=1).reshape(128, -1)
                                      for fi in range(8)]) for l in range(2)])
    o['wout_l'] = np.stack([np.stack([blk(wout[l], 128 * fi, 128).reshape(128, -1) for fi in range(8)]) for l in range(2)])
    wup, wdn = f(inp['w_up']), f(inp['w_down'])
    o['wup_l'] = np.stack([np.stack([np.stack([blk(wup[l], 128 * fc, 128), blk(wup[l], DFF + 128 * fc, 128)], axis=2).reshape(128, -1)
                                     for fc in range(NFF)]) for l in range(2)])
    o['wdn_l'] = np.stack([np.stack([blk(wdn[l], 128 * fi, 128).reshape(128, -1) for fi in range(8)]) for l in range(2)])
    w1 = [f(inp['cmp_w1_k']), f(inp['cmp_w1_v'])]
    o['w1d_l'] = np.stack([np.stack([np.tile(w1[kv][l].reshape(32, 64, 256).transpose(1, 0, 2), (2, 1, 1)).reshape(128, -1)
                                     for kv in range(2)]) for l in range(2)])
    w2k, w2v = f(inp['cmp_w2_k']), f(inp['cmp_w2_v'])
    o['w2d_l'] = np.stack([np.tile(blk(w2k[l], 0, 64), (1, 1, 2)).reshape(128, -1) for l in range(2)])
    o['w2v_l'] = np.stack([blk(w2v[l], 0, 64).reshape(128, -1) for l in range(2)])
    pe = [f(inp['cmp_pe_k']), f(inp['cmp_pe_v'])]
    o['pe_l'] = np.stack([np.stack([np.tile(pe[kv][l].T, (2, 1)) for kv in range(2)]) for l in range(2)])
    return {k_: np.ascontiguousarray(v) for k_, v in o.items()}


SCOPED_CONSTS = ('k_cos', 'k_sin', 'k_cneg', 'k_E')


class Ctx:
    pass


def load_scoped_consts(K, names):
    P = K.P
    for nm in names:
        shp, dt = CONST_SHAPES[nm]
        if nm in ('k_cos', 'k_sin'):
            K.k[nm] = P.sb('c' + nm, shp, BF16)
            K.wslot += 1
            P.dma('pool', K.k[nm][:], K.d[nm], writes=[nm], slot='w%d' % (K.wslot % 6))
        else:
            K.k[nm] = P.sb('c' + nm, shp, dt)
            P.dma('sp', K.k[nm][:], K.d[nm], writes=[nm], slot='c%d' % (len(nm) % 4))


def build(nlayers=2, taps=(), upto='all'):
    nc = bass.Bass("TRN2", target_bir_lowering=False)
    K = Ctx()
    K.nc = nc
    K.d = {}
    for nm, shp in IN_SHAPES.items():
        K.d[nm] = nc.dram_tensor(nm, list(shp), F32, kind="ExternalInput").ap()
    for nm, (shp, dt) in CONST_SHAPES.items():
        K.d[nm] = nc.dram_tensor(nm, list(shp), dt, kind="ExternalInput").ap()
    K.out = nc.dram_tensor("out", [S, D], F32, kind="ExternalOutput").ap()
    K.xpark = nc.dram_tensor("xpark", [128, KC * S], F32).ap()
    K.taps = {}
    K.tapset = set(taps)

    with ExitStack() as gst:
        P = Prog(nc, gst)
        K.P = P
        K.pb = [gst.enter_context(nc.psum_tensor('pb%d' % i, [128, 512], F32)) for i in range(8)]
        K.arena = P.sb('arena', [128, KC * S], F32, glob=True)
        K.xT = K.arena[:].rearrange("p (k s) -> p k s", k=KC)
        K.hT = P.sb('hT', [128, KC, S], BF16, glob=True)
        K.k = {}
        for nm, (shp, dt) in CONST_SHAPES.items():
            if nm in SCOPED_CONSTS:
                continue
            K.k[nm] = P.sb('c' + nm, shp, dt, glob=True)
        K.modc = P.sb('modc', [128, 2, 48], F32, glob=True)
        K.gs = P.sb('gs', [128, 2, 2, 8], F32, glob=True)
        K.gfin = P.sb('gfin', [128, 8], F32, glob=True)
        K.cvw = P.sb('cvw', [128, 2, 4, 44], F32, glob=True)
        K.halo = P.sb('halo', [128, 44, 2], F32, glob=True)
        K.wslot = 0

        phase_start(K)
        order = ['start', 'norm', 'dil', 'nsa', 'merge', 'ffn', 'all']
        lvl = order.index(upto)
        for layer in range(nlayers):
            with ExitStack() as lst:
                P.stack = lst
                K.yaT = P.sb('yaT', [128, 2, S], BF16)
                K.ybT = P.sb('ybT', [128, 4, S], BF16)
                stop_early = False
                with ExitStack() as dsc:
                    P.stack = dsc
                    K.pre_dil = [P.sb('pwq', [128, KC, 256], BF16), P.sb('pwk', [128, KC, 256], BF16), P.sb('pwv', [128, KC, 256], BF16)]
                    for w_i, nm_ in enumerate(('wq', 'wk', 'wv')):
                        ldflat(K, K.pre_dil[w_i][:].rearrange("p k n -> p (k n)"), K.d['wina_l'][layer, 0, w_i], (nm_, 0))
                    phase_norm(K, layer, 0)
                    tap_h(K, 'h1_%d' % layer)
                    if lvl <= 1:
                        P.barrier()
                        P.emit()
                        stop_early = True
                    else:
                        park_x(K)
                        phase_dilated(K, layer)
                P.stack = lst
                if stop_early:
                    break
                if lvl >= 3:
                    phase_nsa(K, layer)
                if lvl >= 4:
                    unpark_x(K)
                    phase_merge(K, layer)
                P.barrier()
                P.emit()
            P.stack = gst
            if lvl <= 3:
                break
            tap_x(K, 'xm%d' % layer)
            if lvl <= 4:
                break
            with ExitStack() as fst:
                P.stack = fst
                K.pre_wup = [P.sb('pwup%d' % i, [128, KC, 2, 128], BF16) for i in range(4)]
                for fc_ in range(3):
                    ldflat(K, K.pre_wup[fc_][:].rearrange("p k a n -> p (k a n)"), K.d['wup_l'][layer, fc_], ('wup', fc_))
                phase_norm(K, layer, 1)
                phase_ffn(K, layer)
            P.stack = gst
            tap_x(K, 'xf%d' % layer)
        if lvl >= 6:
            phase_final(K)
        P.barrier()
        P.emit()
    return nc, K


def tap_x(K, name):
    if name not in K.tapset:
        return
    P = K.P
    t = K.nc.dram_tensor("tap_" + name, [128, KC * S], F32, kind="ExternalOutput").ap()
    P.dma('sp', t, K.arena[:], reads=['xT'], slot='tap')
    K.taps[name] = t


def tap_h(K, name):
    if name not in K.tapset:
        return
    P = K.P
    t = K.nc.dram_tensor("tap_" + name, [128, KC * S], BF16, kind="ExternalOutput").ap()
    P.dma('sp', t, K.hT[:].rearrange("p k s -> p (k s)"), reads=['hT'], slot='tap')
    K.taps[name] = t


def tap_any(K, name, ap, reads, shape, dt):
    if name not in K.tapset:
        return
    t = K.nc.dram_tensor("tap_" + name, list(shape), dt, kind="ExternalOutput").ap()
    K.P.dma('sp', t, ap, reads=reads, slot='tap')
    K.taps[name] = t


def mm(P, out, lhsT, rhs, start, stop, reads, writes):
    return P.op('pe', lambda e: e.matmul(out, lhsT=lhsT, rhs=rhs, start=start, stop=stop, skip_group_check=True),
                reads=reads, writes=writes)


def load_w(K, dst, src2d, wname, eng='pool'):
    K.wslot += 1
    return K.P.dma(eng, dst, src2d.rearrange("(k p) n -> p k n", p=128), writes=[wname], slot='w%d' % (K.wslot % 6))


def ldflat(K, dst_flat, src, wname, eng='pool'):
    K.wslot += 1
    return K.P.dma(eng, dst_flat, src, writes=[wname], slot='w%d' % (K.wslot % 6))


def proj_fm(K, ps, psname, wt, wname, cols, tok0, ntok, M=128):
    P = K.P
    for kc in range(KC):
        mm(P, ps[0:M, 0:ntok], wt[:, kc, cols], K.hT[:, kc, tok0:tok0 + ntok], kc == 0, kc == KC - 1,
           reads=[wname, 'hT'], writes=[psname])


def phase_start(K):
    P, nc, d = K.P, K.nc, K.d
    _prev_stack = P.stack
    with ExitStack() as st:
        P.stack = st
        for i, (nm, (shp, dt)) in enumerate(CONST_SHAPES.items()):
            if nm in SCOPED_CONSTS:
                continue
            P.dma('sp', K.k[nm][:], d[nm], writes=[nm], slot='c%d' % (i % 4))
        identf = K.k['k_identf']
        rows = P.sb('rows', [128, 128], F32)
        rows2 = P.sb('rows2', [128, 128], F32)
        cols = P.sb('cols', [128, 512], F32)
        P.op('pool', lambda e: e.memset(rows[:], 0.0), writes=['rows'])
        P.op('pool', lambda e: e.memset(rows2[:], 0.0), writes=['rows2'])
        off = 0
        entries = {}
        rnames = []
        for nm, ap, n in [('c', d['c'], 8), ('g10', d['norm1_g'][0], 8), ('g11', d['norm1_g'][1], 8),
                          ('g20', d['norm2_g'][0], 8), ('g21', d['norm2_g'][1], 8), ('gf', d['final_g'], 8),
                          ('bm0', d['b_mod'][0], 48)]:
            entries[nm] = (off, n)
            P.dma('sp', rows[off:off + n, :], ap, reads=['rows'], writes=[('rows', nm)], slot='r%d' % (len(rnames) % 4))
            rnames.append(('rows', nm))
            off += n
        P.dma('sp', rows2[0:48, :], d['b_mod'][1], reads=['rows2'], writes=[('rows2', 0)], slot='r0')
        P.op('pe', lambda e: e.transpose(K.pb[0][:, 0:128], rows[:, :], identf[:]), reads=rnames + ['rows', 'k_identf'], writes=['ps0'])
        P.op('dve', lambda e: e.tensor_copy(out=cols[:, 0:128], in_=K.pb[0][:, 0:128]), reads=['ps0'], writes=['colsA'])
        P.op('pe', lambda e: e.transpose(K.pb[1][:, 0:128], rows2[:, :], identf[:]), reads=[('rows2', 0), 'rows2', 'k_identf'], writes=['ps1'])
        P.op('dve', lambda e: e.tensor_copy(out=cols[:, 128:256], in_=K.pb[1][:, 0:128]), reads=['ps1'], writes=['colsB'])

        def colof(nm):
            o, n = entries[nm]
            return cols[:, o:o + n]
        for l in range(2):
            rc = P.sb('rc%d' % l, [128, 128], F32)
            P.op('pool', lambda e, rc=rc: e.memset(rc[:], 0.0), writes=[('rc', l)])
            for j in range(2):
                P.dma('sp', rc[44 * j:44 * j + 44, :], d['conv_w'][l, j], reads=[('rc', l)], writes=[('rc', l, j)], slot='r2')
            rd = P.sb('rd%d' % l, [128, 128], F32)
            P.op('pool', lambda e, rd=rd: e.memset(rd[:], 0.0), writes=[('rd', l)])
            P.dma('sp', rd[0:44, :], d['conv_w'][l, 2], reads=[('rd', l)], writes=[('rd', l, 0)], slot='r3')
            P.dma('sp', rd[44:88, :], d['conv_b'][l], reads=[('rd', l)], writes=[('rd', l, 1)], slot='r3')
            P.op('pe', lambda e, rc=rc: e.transpose(K.pb[2][:, 0:128], rc[:, :], identf[:]),
                 reads=[('rc', l), ('rc', l, 0), ('rc', l, 1), 'k_identf'], writes=['ps2'])
            P.op('dve', lambda e, l=l: e.tensor_copy(out=K.cvw[:, l, 0:2, :], in_=K.pb[2][:, 0:88].rearrange("p (a b) -> p a b", a=2)),
                 reads=['ps2'], writes=['cvw'])
            P.op('pe', lambda e, rd=rd: e.transpose(K.pb[3][:, 0:128], rd[:, :], identf[:]),
                 reads=[('rd', l), ('rd', l, 0), ('rd', l, 1), 'k_identf'], writes=['ps3'])
            P.op('dve', lambda e, l=l: e.tensor_copy(out=K.cvw[:, l, 2:4, :], in_=K.pb[3][:, 0:88].rearrange("p (a b) -> p a b", a=2)),
                 reads=['ps3'], writes=['cvw'])

        sc = P.sb('sc', [128, 8], F32)
        P.op('act', lambda e: e.activation(out=sc[:], in_=colof('c'), func=AF.Silu), reads=['colsA'], writes=['sc'])
        mrow = [P.sb('mrow%d' % i, [1, 512], F32) for i in range(2)]
        wm = [P.sb('wm%d' % i, [128, KC, 512], F32) for i in range(2)]
        pieces = [(l, j) for l in range(2) for j in range(12)]
        one1 = K.k['k_one1']

        def ld(i):
            l, j = pieces[i]
            ldflat(K, wm[i % 2][:].rearrange("p k n -> p (k n)"), d['wmod_l'][l, j], ('wm', i % 2), eng='sp')
        ld(0)
        for i, (l, j) in enumerate(pieces):
            if i + 1 < len(pieces):
                ld(i + 1)
            ps = K.pb[4 + (i % 2)]
            psn = 'ps%d' % (4 + (i % 2))
            for kc in range(KC):
                mm(P, ps[0:1, :], sc[:, kc:kc + 1], wm[i % 2][:, kc, :], kc == 0, kc == KC - 1,
                   reads=['sc', ('wm', i % 2)], writes=[psn])
            mr = mrow[i % 2]
            P.op('act', lambda e, ps=ps, mr=mr: e.copy(out=mr[0:1, :], in_=ps[0:1, :]), reads=[psn], writes=[('mrow', i % 2)])
            for q_ in range(4):
                col = l * 48 + 4 * j + q_
                mm(P, K.pb[6][:, col:col + 1], mr[0:1, 128 * q_:128 * q_ + 128], one1[0:1, 0:1], True, True,
                   reads=[('mrow', i % 2), 'k_one1'], writes=['ps6'])
        bmc = [colof('bm0'), cols[:, 128:176]]
        for l in range(2):
            P.op('dve', lambda e, l=l: e.tensor_tensor(out=K.modc[:, l, :], in0=K.pb[6][:, l * 48:l * 48 + 48], in1=bmc[l], op=ALU.add),
                 reads=['ps6', 'colsA', 'colsB'], writes=['modc'])
        for l in range(2):
            for w, gname, sidx in ((0, 'g1%d' % l, 1), (1, 'g2%d' % l, 4)):
                P.op('dve', lambda e, l=l, w=w, gname=gname, sidx=sidx: e.scalar_tensor_tensor(
                    out=K.gs[:, l, w, :], in0=K.modc[:, l, 8 * sidx:8 * sidx + 8], scalar=1.0, in1=colof(gname),
                    op0=ALU.add, op1=ALU.mult), reads=['modc', 'colsA'], writes=['gs'])
                P.op('dve', lambda e, l=l, w=w: e.tensor_scalar(out=K.gs[:, l, w, :], in0=K.gs[:, l, w, :], scalar1=32.0, scalar2=None,
                                                             op0=ALU.mult), reads=['gs'], writes=['gs'])
        P.op('dve', lambda e: e.tensor_scalar(out=K.gfin[:], in0=colof('gf'), scalar1=32.0, scalar2=None, op0=ALU.mult),
             reads=['colsA'], writes=['gfin'])
        tap_any(K, 'modc', K.modc[:].rearrange("p a b -> p (a b)"), ['modc'], [128, 96], F32)

        xin = [P.sb('xin%d' % i, [128, D], F32) for i in range(2)]
        for t in range(NT):
            xi = xin[t % 2]
            P.dma('sp', xi[:], d['x'][128 * t:128 * t + 128, :], writes=[('xin', t % 2)], slot='x%d' % (t % 2))
            for half in range(2):
                ps = K.pb[(2 * t + half) % 4]
                psn = 'ps%d' % ((2 * t + half) % 4)
                for j in range(4):
                    kc = half * 4 + j
                    P.op('pe', lambda e, ps=ps, xi=xi, kc=kc, j=j: e.transpose(ps[:, 128 * j:128 * j + 128], xi[:, 128 * kc:128 * kc + 128], identf[:]),
                         reads=[('xin', t % 2), 'k_identf'], writes=[psn])
                eng = 'act' if half == 0 else 'dve'
                if eng == 'act':
                    P.op('act', lambda e, ps=ps, half=half, t=t: e.copy(out=K.xT[:, half * 4:half * 4 + 4, 128 * t:128 * t + 128],
                                                                         in_=ps[:, :].rearrange("p (a b) -> p a b", a=4)),
                         reads=[psn], writes=['xT'])
                else:
                    P.op('dve', lambda e, ps=ps, half=half, t=t: e.tensor_copy(out=K.xT[:, half * 4:half * 4 + 4, 128 * t:128 * t + 128],
                                                                                in_=ps[:, :].rearrange("p (a b) -> p a b", a=4)),
                         reads=[psn], writes=['xT'])
        P.barrier()
        P.emit()
    P.stack = _prev_stack


def norm_A(K, c, st_tiles):
    P = K.P
    sq, rstds, tmp = st_tiles
    rstd = rstds[c % 2]
    ps = K.pb[c % 2]
    psn = 'ps%d' % (c % 2)
    onesf = K.k['k_onesf']
    for kc in range(KC):
        s_ = sq[kc % len(sq)]
        if kc % 2 == 0:
            P.op('act', lambda e, s_=s_, kc=kc: e.activation(out=s_[:], in_=K.xT[:, kc, CH * c:CH * c + CH], func=AF.Square),
                 reads=['xT'], writes=[('sq', kc % len(sq))])
        else:
            P.op('pool', lambda e, s_=s_, kc=kc: e.tensor_tensor(out=s_[:], in0=K.xT[:, kc, CH * c:CH * c + CH],
                                                                 in1=K.xT[:, kc, CH * c:CH * c + CH], op=ALU.mult),
                 reads=['xT'], writes=[('sq', kc % len(sq))])
        mm(P, ps[:, :], onesf[:], s_[:], kc == 0, kc == KC - 1, reads=['k_onesf', ('sq', kc % len(sq))], writes=[psn])
    P.op('act', lambda e: e.activation(out=rstd[:], in_=ps[:, :], func=AF.Sqrt, bias=K.epsb[:, 0:1], scale=1.0),
         reads=[psn, 'epsb'], writes=[('rstd', c % 2)])
    P.op('dve', lambda e: e.reciprocal(out=rstd[:], in_=rstd[:]), reads=[('rstd', c % 2)], writes=[('rstd', c % 2)])


def norm_B(K, c, gcol, out_fn, st_tiles):
    P = K.P
    sq, rstds, tmp = st_tiles
    rstd = rstds[c % 2]
    for kc in range(KC):
        t_ = tmp[kc % len(tmp)]
        P.op('dve', lambda e, t_=t_, kc=kc: e.scalar_tensor_tensor(out=t_[:], in0=K.xT[:, kc, CH * c:CH * c + CH], scalar=gcol[:, kc:kc + 1],
                                                                 in1=rstd[:], op0=ALU.mult, op1=ALU.mult),
             reads=['xT', ('rstd', c % 2), 'gs', 'gfin'], writes=[('ntmp', kc % len(tmp))])
        out_fn(kc, t_, ('ntmp', kc % len(tmp)))


def norm_all(K, gcol, make_out_fn, st_tiles, after_chunk=None):
    norm_A(K, 0, st_tiles)
    for c in range(NCH):
        if c + 1 < NCH:
            norm_A(K, c + 1, st_tiles)
        norm_B(K, c, gcol, make_out_fn(c), st_tiles)
        if after_chunk:
            after_chunk(c)


def phase_norm(K, layer, which):
    P = K.P
    _prev_stack = P.stack
    with ExitStack() as st:
        P.stack = st
        sq = [P.sb('sq%d' % i, [128, CH], F32) for i in range(4)]
        rstd = [P.sb('rstd%d' % i, [128, CH], F32) for i in range(2)]
        tmp = [P.sb('ntmp%d' % i, [128, CH], F32) for i in range(4)]
        K.epsb = P.sb('epsb', [128, 1], F32)
        P.op('pool', lambda e: e.memset(K.epsb[:], 1024.0 * 1e-6), writes=['epsb'])
        gcol = K.gs[:, layer, which, :]
        sh = K.modc[:, layer, (0 if which == 0 else 24):(8 if which == 0 else 32)]

        def make_out_fn(c):
            def out_fn(kc, t_, tname):
                P.op('act', lambda e: e.activation(out=K.hT[:, kc, CH * c:CH * c + CH], in_=t_[:], func=AF.Identity,
                                                   bias=sh[:, kc:kc + 1], scale=1.0),
                     reads=[tname, 'modc'], writes=['hT'])
            return out_fn
        norm_all(K, gcol, make_out_fn, (sq, rstd, tmp))
        P.barrier()
        P.emit()
    P.stack = _prev_stack


def park_x(K):
    P = K.P
    P.dma('sp', K.xpark, K.arena[:], reads=['xT'], writes=['xpark'], slot='park')
    P.barrier()


def unpark_x(K):
    P = K.P
    P.barrier()
    P.dma('sp', K.arena[:], K.xpark, reads=['xpark'], writes=['xT'], slot='park')


def rope_tile(K, ps, psn, tok0, ntok, out_ap, out_name, W, nope_ap=None, nope_name=None, i=0):
    P = K.P
    raw, t1, t2 = W['raw'][i % 2], W['t1'][i % 2], W['t2'][i % 2]
    rn, t1n, t2n = ('raw', i % 2), ('t1', i % 2), ('t2', i % 2)
    swb = W.get('swbanks', (7,))[i % len(W.get('swbanks', (7,)))]
    psw = K.pb[swb]
    if nope_ap is not None:
        P.op('act', lambda e: e.copy(out=nope_ap, in_=ps[:, 0:ntok]), reads=[psn], writes=[nope_name])
        rawap, rawname = nope_ap, nope_name
    else:
        P.op('act', lambda e: e.copy(out=raw[:, 0:ntok], in_=ps[:, 0:ntok]), reads=[psn], writes=[rn])
        rawap, rawname = raw[:, 0:ntok], rn
    P.op('dve', lambda e: e.tensor_tensor(out=t1[:, 0:ntok], in0=ps[:, 0:ntok], in1=K.k['k_cos'][:, tok0:tok0 + ntok], op=ALU.mult),
         reads=[psn, 'k_cos'], writes=[t1n])
    mm(P, psw[:, 0:ntok], K.k['k_pm'][:], rawap, True, True, reads=['k_pm', rawname], writes=['ps%d' % swb])
    P.op('dve', lambda e: e.tensor_tensor(out=t2[:, 0:ntok], in0=psw[:, 0:ntok], in1=K.k['k_sin'][:, tok0:tok0 + ntok], op=ALU.mult),
         reads=['ps%d' % swb, 'k_sin'], writes=[t2n])
    P.op('pool', lambda e: e.tensor_tensor(out=out_ap, in0=t1[:, 0:ntok].rearrange(W['inre'], **W['inkw']) if W.get('inre') else t1[:, 0:ntok],
                                           in1=t2[:, 0:ntok].rearrange(W['inre'], **W['inkw']) if W.get('inre') else t2[:, 0:ntok], op=ALU.add),
         reads=[t1n, t2n], writes=[out_name])


def acc_init(K, bank):
    P = K.P
    mm(P, K.pb[bank][:, :], K.k['k_zeros'][:], K.k['k_mc4'][:], True, False, reads=['k_zeros', 'k_mc4'], writes=['ps%d' % bank])


class Pipe:
    def __init__(self, K, depth=2):
        self.K, self.depth, self.pending = K, depth, []

    def unit(self, st_bank, score_mms, nq, pt, ptname, pv_list, pre=None, post=None, first=False):
        K = self.K
        P = K.P
        ps = K.pb[st_bank]
        psn = 'ps%d' % st_bank
        for idx, (lhsT, rhs, c0, n, rd) in enumerate(score_mms):
            mm(P, ps[0:lhsT.shape[1], c0:c0 + n], lhsT, rhs, idx == 0, idx == len(score_mms) - 1, reads=rd, writes=[psn])
        P.op('act', lambda e: e.activation(out=pt[:, 0:nq], in_=ps[:, 0:nq], func=AF.Exp, scale=SCALE), reads=[psn], writes=[ptname])
        self.pending.append((pv_list, pt, ptname, pre, post, first))
        if len(self.pending) > self.depth:
            self._pop()

    def pair(self, ua, ub):
        K = self.K
        P = K.P
        for u in (ua, ub):
            ps, psn = K.pb[u['bank']], 'ps%d' % u['bank']
            for idx in range(u['nqk']):
                lhsT, rhs, c0, n, rd = u['sm'][idx]
                mm(P, ps[0:lhsT.shape[1], c0:c0 + n], lhsT, rhs, idx == 0, False, reads=rd, writes=[psn])
        for u in (ua, ub):
            ps, psn = K.pb[u['bank']], 'ps%d' % u['bank']
            nsm = len(u['sm'])
            for idx in range(u['nqk'], nsm):
                lhsT, rhs, c0, n, rd = u['sm'][idx]
                mm(P, ps[0:lhsT.shape[1], c0:c0 + n], lhsT, rhs, False, idx == nsm - 1, reads=rd, writes=[psn])
        for u in (ua, ub):
            ps, psn = K.pb[u['bank']], 'ps%d' % u['bank']
            pt, nq = u['pt'], u['nq']
            P.op('act', lambda e, pt=pt, ps=ps, nq=nq: e.activation(out=pt[:, 0:nq], in_=ps[:, 0:nq], func=AF.Exp, scale=SCALE),
                 reads=[psn], writes=[u['ptname']])
            self.pending.append((u['pv'], pt, u['ptname'], None, u.get('post'), u.get('first', False)))
        while len(self.pending) > self.depth:
            self._pop()

    def _pop(self):
        K = self.K
        P = K.P
        pv_list, pt, ptname, pre, post, first = self.pending.pop(0)
        if pre:
            pre()
        started = set()
        for (ab, lhsT, lrd, c0, n, a0) in pv_list:
            st_ = first and ab not in started
            started.add(ab)
            mm(P, K.pb[ab][:, a0:a0 + n], lhsT, pt[:, c0:c0 + n], st_, False, reads=lrd + [ptname], writes=['ps%d' % ab])
        if post:
            post()

    def flush(self):
        while self.pending:
            self._pop()


def phase_dilated(K, layer):
    P, d = K.P, K.d
    ab = K.arena[:].bitcast(BF16)
    QT = ab[:, 0:4096].rearrange("p (a s) -> p a s", a=2)
    KT = ab[:, 4096:8192].rearrange("p (a s) -> p a s", a=2)
    Vp = ab[:, 8192:16384].rearrange("p (t h c) -> p t h c", t=NT, h=4)
    af = K.arena[:]
    numacc = af[:, 8192:12288].rearrange("p (a s) -> p a s", a=2)
    denacc = af[:, 12288:16384].rearrange("p (a s) -> p a s", a=2)
    ident = K.k['k_identb']
    _prev_stack = P.stack
    with ExitStack() as st:
        P.stack = st
        wq = [K.pre_dil[0], P.sb('wq1', [128, KC, 256], BF16)]
        wk = [K.pre_dil[1], P.sb('wk1', [128, KC, 256], BF16)]
        wv = [K.pre_dil[2], P.sb('wv1', [128, KC, 256], BF16)]
        W = {'raw': [P.sb('raw%d' % i, [128, CH], BF16) for i in range(2)],
             't1': [P.sb('t1_%d' % i, [128, CH], F32) for i in range(2)],
             't2': [P.sb('t2_%d' % i, [128, CH], F32) for i in range(2)], 'swbanks': (7, 6)}
        pts = [P.sb('pt%d' % i, [128, CH], BF16) for i in range(4)]
        rden = P.sb('rden', [128, S], F32)
        load_scoped_consts(K, ['k_cos', 'k_sin'])
        P.op('pool', lambda e: e.memset(ab[:, 8192:16384], 0.0), writes=['Vp'])

        def ldw(g):
            ldflat(K, wq[g % 2][:].rearrange("p k n -> p (k n)"), d['wina_l'][layer, g, 0], ('wq', g % 2))
            ldflat(K, wk[g % 2][:].rearrange("p k n -> p (k n)"), d['wina_l'][layer, g, 1], ('wk', g % 2))
            ldflat(K, wv[g % 2][:].rearrange("p k n -> p (k n)"), d['wina_l'][layer, g, 2], ('wv', g % 2))
        rt = 0
        for g in range(3):
            if g + 1 < 3:
                ldw(g + 1)
            dil = (1, 4, 16)[g]
            for which, wt, wn, dst, dname in ((0, wq[g % 2], ('wq', g % 2), QT, 'QT'), (1, wk[g % 2], ('wk', g % 2), KT, 'KT')):
                for pr in range(2):
                    for c in range(NCH):
                        bank = rt % 2
                        proj_fm(K, K.pb[bank], 'ps%d' % bank, wt, wn, slice(128 * pr, 128 * pr + 128), CH * c, CH)
                        if dil == 1:
                            out_ap = dst[:, pr, CH * c:CH * c + CH]
                            W['inre'] = None
                        else:
                            per = CH // dil
                            out_ap = dst[:, pr, :].rearrange("p (r m) -> p m r", r=dil)[:, per * c:per * c + per, :]
                            W['inre'] = "p (m r) -> p m r"
                            W['inkw'] = {'r': dil}
                        rope_tile(K, K.pb[bank], 'ps%d' % bank, CH * c, CH, out_ap, dname, W, i=rt)
                        rt += 1
            for t in range(NT):
                bank = 2 + (t % 2)
                ps = K.pb[bank]
                if dil == 1:
                    tok = lambda kc: K.hT[:, kc, 128 * t:128 * t + 128]
                elif dil == 4:
                    r, j = divmod(t, 4)
                    tok = lambda kc, r=r, j=j: K.hT[:, kc, :].rearrange("p (m r) -> p r m", r=4)[:, r, 128 * j:128 * j + 128]
                else:
                    tok = lambda kc, r=t: K.hT[:, kc, :].rearrange("p (m r) -> p r m", r=16)[:, r, :]
                for kc in range(KC):
                    mm(P, ps[:, 0:256], tok(kc), wv[g % 2][:, kc, :], kc == 0, kc == KC - 1, reads=['hT', ('wv', g % 2)], writes=['ps%d' % bank])
                psv = ps[:, 0:256].rearrange("p (a b c) -> p a b c", a=2, b=2)
                P.op('act', lambda e, t=t, psv=psv: e.copy(out=Vp[:, t, 0:4:2, 0:64], in_=psv[:, :, 0, :]), reads=['ps%d' % bank], writes=['Vp'])
                P.op('dve', lambda e, t=t, psv=psv: e.tensor_copy(out=Vp[:, t, 1:4:2, 64:128], in_=psv[:, :, 1, :]), reads=['ps%d' % bank], writes=['Vp'])
            it = 0
            pipe = Pipe(K, depth=2)
            for pr in range(2):
                for c in range(NCH):
                    nb, db = (3, 4) if (pr * NCH + c) % 2 == 0 else (5, 6)
                    units = []
                    for par in range(2):
                        rows = slice(64 * par, 64 * par + 64)
                        h = 2 * pr + par
                        ones = K.k['k_oneslo'] if par == 0 else K.k['k_oneshi']
                        onesn = 'k_oneslo' if par == 0 else 'k_oneshi'
                        if dil == 16:
                            sm = []
                            pv = []
                            for j in range(4):
                                t = 4 * c + j
                                sm.append((KT[rows, pr, 128 * t:128 * t + 128], QT[rows, pr, 128 * t:128 * t + 128], 128 * j, 128, ['KT', 'QT']))
                                pv.append((nb, Vp[:, t, h, :], ['Vp'], 128 * j, 128, 128 * j))
                                pv.append((db, ones[:], [onesn], 128 * j, 128, 128 * j))
                            sm.append((ident[:], K.k['k_mc4'][:], 0, 512, ['k_identb', 'k_mc4']))
                            units.append((sm, 512, pv, par, 4))
                        else:
                            first_t = 0 if dil == 1 else 4 * c
                            for kt in range(max(4 * c - 1, first_t), 4 * c + 4):
                                q0 = max(128 * kt, CH * c)
                                q1 = min(128 * kt + 256, CH * c + CH)
                                nq = q1 - q0
                                sm = [(KT[rows, pr, 128 * kt:128 * kt + 128], QT[rows, pr, q0:q1], 0, nq, ['KT', 'QT'])]
                                m0 = 0 if q0 == 128 * kt else 128
                                sm.append((ident[:], K.k['k_mpair'][:, m0:m0 + nq], 0, nq, ['k_identb', 'k_mpair']))
                                pv = [(nb, Vp[:, kt, h, :], ['Vp'], 0, nq, q0 - CH * c),
                                      (db, ones[:], [onesn], 0, nq, q0 - CH * c)]
                                units.append((sm, nq, pv, par, 1))

                    def pre(nb=nb, db=db):
                        acc_init(K, nb)
                        acc_init(K, db)

                    def post(nb=nb, db=db, pr=pr, c=c):
                        for bank, acc, an in ((nb, numacc, 'numacc'), (db, denacc, 'denacc')):
                            ps = K.pb[bank]
                            if dil == 1:
                                P.op('act', lambda e, ps=ps, acc=acc: e.copy(out=acc[:, pr, CH * c:CH * c + CH], in_=ps[:, :]),
                                     reads=['ps%d' % bank], writes=[an])
                            elif dil == 4:
                                view = acc[:, pr, :].rearrange("p (m r) -> p r m", r=4)[:, c, :]
                                P.op('dve', lambda e, ps=ps, view=view: e.tensor_tensor(out=view, in0=view, in1=ps[:, :], op=ALU.add),
                                     reads=['ps%d' % bank, an], writes=[an])
                            else:
                                view = acc[:, pr, :].rearrange("p (n r) -> p r n", r=16)[:, 4 * c:4 * c + 4, :]
                                P.op('dve', lambda e, ps=ps, view=view: e.tensor_tensor(out=view, in0=view, in1=ps[:, :].rearrange("p (r n) -> p r n", r=4), op=ALU.add),
                                     reads=['ps%d' % bank, an], writes=[an])
                    ua_ = [u for u in units if u[3] == 0]
                    ub_ = [u for u in units if u[3] == 1]
                    assert len(ua_) == len(ub_)
                    for ui, (a_, b_) in enumerate(zip(ua_, ub_)):
                        da = {'bank': it % 3, 'sm': a_[0], 'nq': a_[1], 'pv': a_[2], 'nqk': a_[4], 'pt': pts[it % 4], 'ptname': ('pt', it % 4),
                              'first': ui == 0}
                        it += 1
                        db_ = {'bank': it % 3, 'sm': b_[0], 'nq': b_[1], 'pv': b_[2], 'nqk': b_[4], 'pt': pts[it % 4], 'ptname': ('pt', it % 4),
                               'post': post if ui == len(ua_) - 1 else None}
                        it += 1
                        pipe.pair(da, db_)
            pipe.flush()
        for pr in range(2):
            P.op('dve', lambda e, pr=pr: e.reciprocal(out=rden[:], in_=denacc[:, pr, :]), reads=['denacc'], writes=['rden'])
            P.op('dve', lambda e, pr=pr: e.tensor_tensor(out=K.yaT[:, pr, :], in0=numacc[:, pr, :], in1=rden[:], op=ALU.mult),
                 reads=['numacc', 'rden'], writes=['yaT'])
        P.report('dil')
        tap_any(K, 'ya%d' % layer, K.yaT[:].rearrange("p a s -> p (a s)"), ['yaT'], [128, 2 * S], BF16)
        P.barrier()
        P.emit()
    P.stack = _prev_stack


def phase_nsa(K, layer):
    P, d = K.P, K.d
    wn = d['wnsa_l'][layer]
    ab = K.arena[:].bitcast(BF16)
    QR = ab[:, 0:8192].rearrange("p (a s) -> p a s", a=4)
    QN = ab[:, 8192:16384].rearrange("p (a s) -> p a s", a=4)
    KS = ab[:, 16384:20480].rearrange("p (g s) -> p g s", g=2)
    KW = ab[:, 20480:24576].rearrange("p (g s) -> p g s", g=2)
    VS = ab[:, 24576:32768].rearrange("p (t g v c) -> p t g v c", t=NT, g=2, v=2)
    ident = K.k['k_identb']
    identf = K.k['k_identf']
    _prev_stack = P.stack
    with ExitStack() as st:
        P.stack = st
        VW = P.sb('VW', [128, NT, 2, 2, 128], BF16)
        gateT = P.sb('gateT', [24, S], BF16)
        negT = P.sb('negT', [128, 2, S], BF16)
        kcT2 = P.sb('kcT2', [128, 2, 128], BF16)
        vcp = P.sb('vcp', [128, 2, 2, 128], BF16)
        P.op('pool', lambda e: e.memset(VW[:].rearrange("p t g v c -> p (t g v c)"), 0.0), writes=['VW'])
        P.op('pool', lambda e: e.memset(vcp[:].rearrange("p g v c -> p (g v c)"), 0.0), writes=['vcp'])
        P.op('pool', lambda e: e.memset(kcT2[:].rearrange("p g c -> p (g c)"), 0.0), writes=['kcT2'])
        P.op('pool', lambda e: e.memset(negT[:].rearrange("p g s -> p (g s)"), 0.0), writes=['negT'])

        wbq = P.sb('wbq', [128, KC, 512], BF16)
        ldflat(K, wbq[:].rearrange("p k n -> p (k n)"), wn[:, 2048:6144], 'wbq')
        with ExitStack() as cst:
            P.stack = cst
            wkv = P.sb('wkv', [128, KC, 256], BF16)
            ldflat(K, wkv[:].rearrange("p k n -> p (k n)"), wn[:, 0:2048], 'wkv')
            cmpT = K.arena[:, 0:4096].rearrange("p (a s) -> p a s", a=2)
            for kv in range(2):
                for c in range(NCH):
                    bank = (kv * NCH + c) % 2
                    proj_fm(K, K.pb[bank], 'ps%d' % bank, wkv, 'wkv', slice(128 * kv, 128 * kv + 128), CH * c, CH)
                    P.op('act', lambda e, bank=bank, kv=kv, c=c: e.copy(out=cmpT[:, kv, CH * c:CH * c + CH], in_=K.pb[bank][:, :]),
                         reads=['ps%d' % bank], writes=['cmpT'])
            w1ds = [ab[:, 8192:16384].rearrange("p (l j) -> p l j", l=32), ab[:, 24576:32768].rearrange("p (l j) -> p l j", l=32)]
            kpes = [ab[:, 16384:20480].rearrange("p (l n) -> p l n", l=32), ab[:, 20480:24576].rearrange("p (l n) -> p l n", l=32)]
            w2d = P.sb('w2d', [128, 2, 128], BF16)
            w2v = P.sb('w2v', [128, 2, 64], BF16)
            peTs = [P.sb('peT%d' % i_, [128, 32], F32) for i_ in range(2)]
            hid = P.sb('hid', [128, 4, 128], BF16)
            for kv in range(2):
                ldflat(K, w1ds[kv].rearrange("p l j -> p (l j)"), d['w1d_l'][layer, kv], ('w1d', kv))
                P.dma('sp', peTs[kv][:], d['pe_l'][layer, kv], writes=[('peT', kv)], slot='r%d' % kv)
            ldflat(K, w2d[:].rearrange("p a b -> p (a b)"), d['w2d_l'][layer], ('w2d', 0))
            ldflat(K, w2v[:].rearrange("p a b -> p (a b)"), d['w2v_l'][layer], 'w2v')
            for kv in range(2):
                w1d, kpe, peT = w1ds[kv], kpes[kv], peTs[kv]
                for l in range(32):
                    P.op('dve', lambda e, l=l, kv=kv, kpe=kpe, peT=peT: e.tensor_scalar(
                        out=kpe[:, l, 0:127], in0=cmpT[:, kv, :].rearrange("p (n r) -> p r n", r=16)[:, l % 16, (l // 16):(l // 16) + 127],
                        scalar1=peT[:, l:l + 1], scalar2=None, op0=ALU.add), reads=['cmpT', ('peT', kv)], writes=[('kpe', kv)])
                for jc in range(2):
                    bks = (3, 4) if jc == 0 else (6, 7)
                    for l in range(32):
                        for g in range(2):
                            rows = slice(64 * g, 64 * g + 64)
                            mm(P, K.pb[bks[g]][:, 0:127], w1d[rows, l, 128 * jc:128 * jc + 128], kpe[rows, l, 0:127], l == 0, l == 31,
                               reads=[('w1d', kv), ('kpe', kv)], writes=['ps%d' % bks[g]])
                    for g in range(2):
                        P.op('act', lambda e, bank=bks[g], g=g, jc=jc: e.activation(out=hid[:, g * 2 + jc, 0:127], in_=K.pb[bank][:, 0:127], func=AF.Silu),
                             reads=['ps%d' % bks[g]], writes=['hid'])
                for g in range(2):
                    if kv == 0:
                        for jc in range(2):
                            mm(P, K.pb[5][:, 0:127], w2d[:, jc, :], hid[:, g * 2 + jc, 0:127], jc == 0, jc == 1,
                               reads=[('w2d', 0), 'hid'], writes=['ps5'])
                        P.op('act', lambda e, g=g: e.copy(out=kcT2[:, g, 0:127], in_=K.pb[5][:, 0:127]), reads=['ps5'], writes=['kcT2'])
                    else:
                        for jc in range(2):
                            mm(P, K.pb[5][0:127, 0:64], hid[:, g * 2 + jc, 0:127], w2v[:, jc, :], jc == 0, jc == 1,
                               reads=['w2v', 'hid'], writes=['ps5'])
                        P.op('act', lambda e, g=g: e.copy(out=vcp[0:127, g, 0, 0:64], in_=K.pb[5][0:127, 0:64]), reads=['ps5'], writes=['vcp'])
                        P.op('dve', lambda e, g=g: e.tensor_copy(out=vcp[0:127, g, 1, 64:128], in_=K.pb[5][0:127, 0:64]), reads=['ps5'], writes=['vcp'])
            P.barrier()
            P.emit()
        P.stack = st
        P.op('pool', lambda e: e.memset(ab[:, 24576:32768], 0.0), writes=['VS'])
        import os as _os
        _stop = int(_os.environ.get('NSA_STOP', '9'))
        if _stop <= 1:
            P.barrier(); P.emit(); P.stack = _prev_stack
            return

        esb = P.sb('esb', [128, 4, 128], F32)
        rs = P.sb('rs', [128, 8], F32)
        ps4 = P.sb('ps4', [128, 128], F32)
        imp = P.sb('imp', [128, 32], F32)
        imp2 = P.sb('imp2', [128, 32], F32)
        m8 = P.sb('m8', [128, 16], F32)
        nsels = [P.sb('nsel%d' % i_, [128, 32], BF16) for i_ in range(3)]

        def selA(k_, i, g):
            nsel, nsn = nsels[k_ % 3], ('nsel', k_ % 3)
            ps = K.pb[6]
            psn = 'ps6'
            first = True
            for par_ in range(2):
                for r in (par_, par_ + 2):
                    h = 4 * g + r
                    pr, par = divmod(h, 2)
                    rows = slice(64 * par, 64 * par + 64)
                    mm(P, ps[:, 128 * r:128 * r + 128], QN[rows, pr, 128 * i:128 * i + 128], kcT2[rows, g, :], first, False,
                       reads=['QN', 'kcT2'], writes=[psn])
                    first = False
                for r in (par_, par_ + 2):
                    mm(P, ps[:, 128 * r:128 * r + 128], ident[:], K.k['k_mwide'][:, 120 - 8 * i:120 - 8 * i + 128], False, (par_ == 1 and r == 3),
                       reads=['k_identb', 'k_mwide'], writes=[psn])
            P.op('act', lambda e: e.activation(out=esb[:].rearrange("p a b -> p (a b)"), in_=ps[:, :], func=AF.Exp, scale=SCALE),
                 reads=[psn], writes=['esb'])
            P.op('dve', lambda e: e.tensor_reduce(out=rs[:, 0:4], in_=esb[:], axis=AX.X, op=ALU.add), reads=['esb'], writes=['rs'])
            P.op('dve', lambda e: e.reciprocal(out=rs[:, 4:8], in_=rs[:, 0:4]), reads=['rs'], writes=['rs'])
            P.op('dve', lambda e: e.tensor_tensor(out=esb[:], in0=esb[:], in1=rs[:, 4:8].unsqueeze(2).to_broadcast([128, 4, 128]), op=ALU.mult),
                 reads=['esb', 'rs'], writes=['esb'])
            P.op('dve', lambda e: e.tensor_reduce(out=ps4[:], in_=esb[:].rearrange("p h n -> p n h"), axis=AX.X, op=ALU.add),
                 reads=['esb'], writes=['ps4'])
            P.op('dve', lambda e: e.tensor_reduce(out=imp[:], in_=ps4[:].rearrange("p (j f) -> p j f", f=4), axis=AX.X, op=ALU.add),
                 reads=['ps4'], writes=['imp'])
            P.op('dve', lambda e: e.tensor_tensor(out=imp[:, 1:32], in0=imp[:, 1:32], in1=ps4[:, 3:124:4], op=ALU.add),
                 reads=['imp', 'ps4'], writes=['imp'])
            P.op('dve', lambda e: e.tensor_tensor(out=imp[:], in0=imp[:], in1=K.k['k_F'][:, 32 * (i - 8):32 * (i - 8) + 32], op=ALU.max),
                 reads=['imp', 'k_F'], writes=['imp'])
            P.op('dve', lambda e: e.max(out=m8[:, 0:8], in_=imp[:]), reads=['imp'], writes=['m8'])
            P.op('dve', lambda e: e.match_replace(out=imp2[:], in_to_replace=m8[:, 0:8], in_values=imp[:], imm_value=-1e30),
                 reads=['imp', 'm8'], writes=['imp2'])
            P.op('dve', lambda e: e.max(out=m8[:, 8:16], in_=imp2[:]), reads=['imp2'], writes=['m8'])
            P.op('dve', lambda e: e.tensor_scalar(out=nsel[:], in0=imp[:], scalar1=m8[:, 15:16], scalar2=NEG, op0=ALU.is_lt, op1=ALU.mult),
                 reads=['imp', 'm8'], writes=[nsn])

        def selB(k_, i, g):
            nsel, nsn = nsels[k_ % 3], ('nsel', k_ % 3)
            pst = K.pb[4][:].bitcast(BF16)
            P.op('pe', lambda e: e.transpose(pst[0:32, 0:128], nsel[:, :], ident[:]), reads=[nsn, 'k_identb'], writes=['ps4'])
            P.op('act', lambda e: e.copy(out=negT[0:32, g, 128 * i:128 * i + 128], in_=pst[0:32, 0:128]), reads=['ps4'], writes=['negT'])

        sel_list = [(i, g) for i in range(8, NT) for g in range(2)]
        sel_pos = [0]

        def sel_step():
            k_ = sel_pos[0]
            sel_pos[0] += 1
            if 0 <= k_ - 2 < len(sel_list):
                selB(k_ - 2, *sel_list[k_ - 2])
            if k_ < len(sel_list):
                selA(k_, *sel_list[k_])

        sel_ticks = [0]

        def sel_tick():
            sel_ticks[0] += 1
            sel_step()

        with ExitStack() as pst:
            P.stack = pst
            W = {'raw': [P.sb('raw%d' % i, [128, CH], BF16) for i in range(2)],
                 't1': [P.sb('t1_%d' % i, [128, CH], F32) for i in range(2)],
                 't2': [P.sb('t2_%d' % i, [128, CH], F32) for i in range(2)], 'inre': None, 'swbanks': (7, 5)}
            load_scoped_consts(K, ['k_cos', 'k_sin'])
            wkd = P.sb('wkd', [128, KC, 2, 2, 2, 64], BF16)
            wvv = P.sb('wvv', [128, KC, 2, 128], BF16)
            wbl = P.sb('wbl', [128, KC, 24], BF16)
            ldflat(K, wkd[:].rearrange("p k a b c d -> p (k a b c d)"), wn[:, 6144:10240], 'wkd')
            ldflat(K, wvv[:].rearrange("p k a n -> p (k a n)"), wn[:, 10240:12288], 'wvv')
            ldflat(K, wbl[:].rearrange("p k n -> p (k n)"), wn[:, 12288:12480], 'wbl')
            P.report('nsa-proj')
            rt = 0
            for pr in range(4):
                for c in range(NCH):
                    bank = rt % 2
                    proj_fm(K, K.pb[bank], 'ps%d' % bank, wbq, 'wbq', slice(128 * pr, 128 * pr + 128), CH * c, CH)
                    rope_tile(K, K.pb[bank], 'ps%d' % bank, CH * c, CH, QR[:, pr, CH * c:CH * c + CH], 'QR', W,
                              nope_ap=QN[:, pr, CH * c:CH * c + CH], nope_name='QN', i=rt)
                    rt += 1
            for w_, dst, dn in ((0, KS, 'KS'), (1, KW, 'KW')):
                for g in range(2):
                    for c in range(NCH):
                        bank = rt % 2
                        for kc in range(KC):
                            mm(P, K.pb[bank][:, :], wkd[:, kc, w_, g, :, :].rearrange("p a b -> p (a b)"), K.hT[:, kc, CH * c:CH * c + CH],
                               kc == 0, kc == KC - 1, reads=['wkd', 'hT'], writes=['ps%d' % bank])
                        rope_tile(K, K.pb[bank], 'ps%d' % bank, CH * c, CH, dst[:, g, CH * c:CH * c + CH], dn, W, i=rt)
                        rt += 1
                        sel_tick()
            for w_, dst, dn in ((0, VS, 'VS'), (1, VW[:], 'VW')):
                for t in range(NT):
                    bank = 2 + (t % 2)
                    for kc in range(KC):
                        mm(P, K.pb[bank][:, 0:128], K.hT[:, kc, 128 * t:128 * t + 128], wvv[:, kc, w_, :], kc == 0, kc == KC - 1,
                           reads=['hT', 'wvv'], writes=['ps%d' % bank])
                    psv = K.pb[bank][:, 0:128].rearrange("p (g dd) -> p g dd", g=2)
                    P.op('act', lambda e, dst=dst, t=t, psv=psv: e.copy(out=dst[:, t, :, 0, 0:64], in_=psv), reads=['ps%d' % bank], writes=[dn])
                    P.op('dve', lambda e, dst=dst, t=t, psv=psv: e.tensor_copy(out=dst[:, t, :, 1, 64:128], in_=psv), reads=['ps%d' % bank], writes=[dn])
                    sel_tick()
            for c in range(NCH):
                bank = 4 + (c % 2)
                for kc in range(KC):
                    mm(P, K.pb[bank][0:24, :], wbl[:, kc, :], K.hT[:, kc, CH * c:CH * c + CH], kc == 0, kc == KC - 1,
                       reads=['wbl', 'hT'], writes=['ps%d' % bank])
                P.op('act', lambda e, bank=bank, c=c: e.activation(out=gateT[:, CH * c:CH * c + CH], in_=K.pb[bank][0:24, :], func=AF.Sigmoid),
                     reads=['ps%d' % bank], writes=['gateT'])
            while sel_pos[0] < len(sel_list) + 2:
                sel_step()
            P.barrier()
            P.emit()
        P.stack = st

        P.report('nsa-afterproj')
        if _stop <= 2:
            P.barrier(); P.emit(); P.stack = _prev_stack
            return
        if _stop <= 3:
            P.barrier(); P.emit(); P.stack = _prev_stack
            return
        load_scoped_consts(K, ['k_cneg', 'k_E'])
        pts = [P.sb('pt%d' % i, [128, CH], BF16) for i in range(4)]
        rden = P.sb('rden', [128, CH], F32)
        wgt = P.sb('wgt', [128, CH], F32)
        yacc = P.sb('yacc', [128, CH], F32)
        ytmp = P.sb('ytmp', [128, CH], F32)
        E = K.k['k_E']
        it = 0
        pc = 0
        pipe = Pipe(K, depth=2)
        for pr in range(4):
            g = pr // 2
            for c in range(NCH):
                for br in range(3):
                    nb, db = (3, 4) if pc % 2 == 0 else (5, 6)
                    pc += 1
                    units = []
                    for par in range(2):
                        rows = slice(64 * par, 64 * par + 64)
                        ones = K.k['k_oneslo'] if par == 0 else K.k['k_oneshi']
                        onesn = 'k_oneslo' if par == 0 else 'k_oneshi'
                        if br == 0:
                            sm = [(kcT2[rows, g, :], QN[rows, pr, CH * c:CH * c + CH], 0, CH, ['kcT2', 'QN']),
                                  (ident[:], K.k['k_cneg'][:, CH * c:CH * c + CH], 0, CH, ['k_identb', 'k_cneg'])]
                            pv = [(nb, vcp[:, g, par, :], ['vcp'], 0, CH, 0), (db, ones[:], [onesn], 0, CH, 0)]
                            units.append((sm, CH, pv, par, 1))
                        else:
                            Kt = KS if br == 1 else KW
                            Kn = 'KS' if br == 1 else 'KW'
                            Vt = VS if br == 1 else VW
                            Vn = 'VS' if br == 1 else 'VW'
                            kt0 = 0 if br == 1 else max(0, 4 * c - 4)
                            for kt in range(kt0, 4 * c + 4):
                                q0 = max(128 * kt, CH * c)
                                q1 = CH * c + CH if br == 1 else min(128 * kt + 640, CH * c + CH)
                                nq = q1 - q0
                                sm = [(Kt[rows, g, 128 * kt:128 * kt + 128], QR[rows, pr, q0:q1], 0, nq, [Kn, 'QR'])]
                                if br == 1 and c >= 2:
                                    sm.append((E[:, 128 * kt:128 * kt + 128], negT[:, g, q0:q1], 0, nq, ['k_E', 'negT']))
                                if q0 == 128 * kt:
                                    sm.append((ident[:], K.k['k_mpair'][:, 0:128], 0, 128, ['k_identb', 'k_mpair']))
                                if br == 2 and q1 == 128 * kt + 640:
                                    sm.append((ident[:], K.k['k_mfar'][:], nq - 128, 128, ['k_identb', 'k_mfar']))
                                pv = [(nb, Vt[:, kt, g, par, :], [Vn], 0, nq, q0 - CH * c), (db, ones[:], [onesn], 0, nq, q0 - CH * c)]
                                units.append((sm, nq, pv, par, 1))

                    def pre(nb=nb, db=db):
                        acc_init(K, nb)
                        acc_init(K, db)

                    def post(nb=nb, db=db, pr=pr, c=c, br=br):
                        mm(P, K.pb[7][:, :], K.k['k_selg'][:, 128 * (pr * 3 + br):128 * (pr * 3 + br) + 128], gateT[:, CH * c:CH * c + CH], True, True,
                           reads=['k_selg', 'gateT'], writes=['ps7'])
                        if br == 0:
                            P.op('dve', lambda e: e.tensor_scalar(out=rden[:], in0=K.pb[db][:, :], scalar1=1e-30, scalar2=None, op0=ALU.max),
                                 reads=['ps%d' % db], writes=['rden'])
                            P.op('dve', lambda e: e.reciprocal(out=rden[:], in_=rden[:]), reads=['rden'], writes=['rden'])
                        else:
                            P.op('dve', lambda e: e.reciprocal(out=rden[:], in_=K.pb[db][:, :]), reads=['ps%d' % db], writes=['rden'])
                        P.op('dve', lambda e: e.tensor_tensor(out=wgt[:], in0=K.pb[7][:, :], in1=rden[:], op=ALU.mult), reads=['ps7', 'rden'], writes=['wgt'])
                        if br == 0:
                            P.op('dve', lambda e: e.tensor_tensor(out=yacc[:], in0=K.pb[nb][:, :], in1=wgt[:], op=ALU.mult),
                                 reads=['ps%d' % nb, 'wgt'], writes=['yacc'])
                        elif br == 1:
                            P.op('dve', lambda e: e.tensor_tensor(out=ytmp[:], in0=K.pb[nb][:, :], in1=wgt[:], op=ALU.mult),
                                 reads=['ps%d' % nb, 'wgt'], writes=['ytmp'])
                            P.op('pool', lambda e: e.tensor_tensor(out=yacc[:], in0=yacc[:], in1=ytmp[:], op=ALU.add), reads=['yacc', 'ytmp'], writes=['yacc'])
                        else:
                            P.op('dve', lambda e: e.tensor_tensor(out=ytmp[:], in0=K.pb[nb][:, :], in1=wgt[:], op=ALU.mult),
                                 reads=['ps%d' % nb, 'wgt'], writes=['ytmp'])
                            P.op('pool', lambda e: e.tensor_tensor(out=K.ybT[:, pr, CH * c:CH * c + CH], in0=yacc[:], in1=ytmp[:], op=ALU.add),
                                 reads=['yacc', 'ytmp'], writes=['ybT'])
                    ua_ = [u for u in units if u[3] == 0]
                    ub_ = [u for u in units if u[3] == 1]
                    assert len(ua_) == len(ub_)
                    for ui, (a_, b_) in enumerate(zip(ua_, ub_)):
                        da = {'bank': it % 3, 'sm': a_[0], 'nq': a_[1], 'pv': a_[2], 'nqk': a_[4], 'pt': pts[it % 4], 'ptname': ('pt', it % 4),
                              'first': ui == 0}
                        it += 1
                        db_ = {'bank': it % 3, 'sm': b_[0], 'nq': b_[1], 'pv': b_[2], 'nqk': b_[4], 'pt': pts[it % 4], 'ptname': ('pt', it % 4),
                               'post': post if ui == len(ua_) - 1 else None}
                        it += 1
                        pipe.pair(da, db_)
        pipe.flush()
        P.report('nsa-attn')
        tap_any(K, 'yb%d' % layer, K.ybT[:].rearrange("p a s -> p (a s)"), ['ybT'], [128, 4 * S], BF16)
        P.barrier()
        P.emit()
    P.stack = _prev_stack


def phase_merge(K, layer):
    P, d = K.P, K.d
    _prev_stack = P.stack
    with ExitStack() as st:
        P.stack = st
        mT = P.sb('mT', [128, KC, S], BF16)
        wmg = [P.sb('wmg%d' % i, [128, 22, 128], BF16) for i in range(2)]
        wo = [P.sb('wo%d' % i, [128, KC, 128], BF16) for i in range(2)]
        sga = [P.sb('sga%d' % i, [128, CH], F32) for i in range(1)]
        sgb = [P.sb('sgb%d' % i, [128, CH], F32) for i in range(1)]
        t1 = [P.sb('mt1_%d' % i, [128, CH], F32) for i in range(1)]
        t2 = [P.sb('mt2_%d' % i, [128, CH], F32) for i in range(1)]

        def ldf(f):
            ldflat(K, wmg[f % 2][:].rearrange("p k n -> p (k n)"), d['wmrg_l'][layer, f], ('wmg', f % 2))
        ldf(0)
        it = 0
        for f in range(KC):
            if f + 1 < KC:
                ldf(f + 1)
            s = f % 2
            for c in range(NCH):
                j = 0
                tk = slice(CH * c, CH * c + CH)
                for kc in range(KC):
                    mm(P, K.pb[0][:, :], wmg[s][:, kc, :], K.hT[:, kc, tk], kc == 0, kc == KC - 1, reads=[('wmg', s), 'hT'], writes=['ps0'])
                P.op('act', lambda e, j=j: e.activation(out=sga[j][:], in_=K.pb[0][:, :], func=AF.Sigmoid), reads=['ps0'], writes=[('sga', j)])
                for kc in range(KC):
                    mm(P, K.pb[1][:, :], wmg[s][:, 8 + kc, :], K.hT[:, kc, tk], kc == 0, kc == KC - 1, reads=[('wmg', s), 'hT'], writes=['ps1'])
                P.op('act', lambda e, j=j: e.activation(out=sgb[j][:], in_=K.pb[1][:, :], func=AF.Sigmoid), reads=['ps1'], writes=[('sgb', j)])
                for kc in range(2):
                    mm(P, K.pb[2][:, :], wmg[s][:, 16 + kc, :], K.yaT[:, kc, tk], kc == 0, kc == 1, reads=[('wmg', s), 'yaT'], writes=['ps2'])
                P.op('dve', lambda e, j=j: e.tensor_tensor(out=t1[j][:], in0=K.pb[2][:, :], in1=sga[j][:], op=ALU.mult),
                     reads=['ps2', ('sga', j)], writes=[('mt1', j)])
                for kc in range(4):
                    mm(P, K.pb[3][:, :], wmg[s][:, 18 + kc, :], K.ybT[:, kc, tk], kc == 0, kc == 3, reads=[('wmg', s), 'ybT'], writes=['ps3'])
                P.op('dve', lambda e, j=j: e.tensor_tensor(out=t2[j][:], in0=K.pb[3][:, :], in1=sgb[j][:], op=ALU.mult),
                     reads=['ps3', ('sgb', j)], writes=[('mt2', j)])
                P.op('pool', lambda e, j=j, f=f, tk=tk: e.tensor_tensor(out=mT[:, f, tk], in0=t1[j][:], in1=t2[j][:], op=ALU.add),
                     reads=[('mt1', j), ('mt2', j)], writes=['mT'])
                it += 1
        P.report('merge')
        ldflat(K, wo[0][:].rearrange("p k n -> p (k n)"), d['wout_l'][layer, 0], ('wo', 0))
        for f in range(KC):
            if f + 1 < KC:
                ldflat(K, wo[(f + 1) % 2][:].rearrange("p k n -> p (k n)"), d['wout_l'][layer, f + 1], ('wo', (f + 1) % 2))
            s = f % 2
            for c in range(NCH):
                bank = 4 + (c % 2)
                tk = slice(CH * c, CH * c + CH)
                for kc in range(KC):
                    mm(P, K.pb[bank][:, :], wo[s][:, kc, :], mT[:, kc, tk], kc == 0, kc == KC - 1, reads=[('wo', s), 'mT'], writes=['ps%d' % bank])
                P.op('dve', lambda e, bank=bank, f=f, tk=tk: e.scalar_tensor_tensor(
                    out=K.xT[:, f, tk], in0=K.pb[bank][:, :], scalar=K.modc[:, layer, 16 + f:17 + f], in1=K.xT[:, f, tk],
                    op0=ALU.mult, op1=ALU.add), reads=['ps%d' % bank, 'xT', 'modc'], writes=['xT'])


def phase_ffn(K, layer):
    P, d = K.P, K.d
    HALF = S // 2
    _prev_stack = P.stack
    with ExitStack() as st:
        P.stack = st
        actT = P.sb('actT', [128, NFF, HALF], BF16)
        NWU = 4
        wup = K.pre_wup
        NWD = 3
        wd = [P.sb('wd%d' % i, [128, NFF, 128], BF16) for i in range(NWD)]
        ub = [[P.sb('ub%d_%d' % (gv, i), [128, CH + 2], F32) for i in range(2)] for gv in range(2)]
        tt = [[P.sb('tt%d_%d' % (gv, i), [128, CH], F32) for i in range(2)] for gv in range(2)]
        sg = [P.sb('sgt%d' % i, [128, CH], F32) for i in range(1)]
        P.op('pool', lambda e: e.memset(K.halo[:].rearrange("p a b -> p (a b)"), 0.0), writes=['halo'])

        def ldu(fc):
            s = fc % NWU
            ldflat(K, wup[s][:].rearrange("p k a n -> p (k a n)"), d['wup_l'][layer, fc], ('wup', s))

        def ldd(f):
            ldflat(K, wd[f % NWD][:].rearrange("p k n -> p (k n)"), d['wdn_l'][layer, f], ('wd', f % NWD))
        it = 0
        tail = [None]
        for hf in range(2):
            if hf > 0:
                for f_ in range(NWU - 1):
                    ldu(f_)
            for fc in range(NFF):
                if fc + NWU - 1 < NFF:
                    ldu(fc + NWU - 1)
                if fc == NFF - 4:
                    ldd(0)
                if fc == NFF - 2:
                    ldd(1)
                s = fc % NWU
                for cc in range(2):
                    c = 2 * hf + cc
                    tk = slice(CH * c, CH * c + CH)
                    j = it % 2
                    it += 1
                    for gv in range(2):
                        bank = (0, 2, 6)[(it - 1) % 3] + gv
                        ch = fc + NFF * gv
                        for kc in range(KC):
                            mm(P, K.pb[bank][:, :], wup[s][:, kc, gv, :], K.hT[:, kc, tk], kc == 0, kc == KC - 1,
                               reads=[('wup', s), 'hT'], writes=['ps%d' % bank])
                        u = ub[gv][j]
                        un = ('ub', gv, j)
                        t = tt[gv][j]
                        tn = ('tt', gv, j)
                        P.op('act', lambda e, u=u, bank=bank: e.copy(out=u[:, 2:CH + 2], in_=K.pb[bank][:, :]), reads=['ps%d' % bank], writes=[un])
                        P.op('pool', lambda e, u=u, ch=ch: e.tensor_copy(out=u[:, 0:2], in_=K.halo[:, ch, :]), reads=['halo'], writes=[un])
                        P.op('pool', lambda e, u=u, ch=ch: e.tensor_copy(out=K.halo[:, ch, :], in_=u[:, CH:CH + 2]), reads=[un], writes=['halo'])
                        cw = K.cvw[:, layer, :, ch:ch + 1]
                        P.op('act', lambda e, t=t, cw=cw, bank=bank: e.activation(out=t[:], in_=K.pb[bank][:, :], func=AF.Identity,
                                                                                  scale=cw[:, 2, :], bias=cw[:, 3, :]),
                             reads=['ps%d' % bank, 'cvw'], writes=[tn])
                        P.op('dve', lambda e, u=u, t=t, cw=cw: e.scalar_tensor_tensor(out=t[:], in0=u[:, 1:CH + 1], scalar=cw[:, 1, :], in1=t[:],
                                                                                      op0=ALU.mult, op1=ALU.add), reads=[un, tn, 'cvw'], writes=[tn])
                        P.op('dve', lambda e, u=u, t=t, cw=cw: e.scalar_tensor_tensor(out=t[:], in0=u[:, 0:CH], scalar=cw[:, 0, :], in1=t[:],
                                                                                      op0=ALU.mult, op1=ALU.add), reads=[un, tn, 'cvw'], writes=[tn])
                    if tail[0] is not None:
                        tail[0]()

                    def _tail(j=j, fc=fc, cc=cc):
                        P.op('act', lambda e: e.activation(out=sg[0][:], in_=tt[0][j][:], func=AF.Silu), reads=[('tt', 0, j)], writes=[('sgt', 0)])
                        P.op('pool', lambda e: e.tensor_tensor(out=actT[:, fc, CH * cc:CH * cc + CH], in0=sg[0][:], in1=tt[1][j][:], op=ALU.mult),
                             reads=[('sgt', 0), ('tt', 1, j)], writes=['actT'])
                    tail[0] = _tail
            tail[0]()
            tail[0] = None
            for f in range(KC):
                if f + 2 < KC:
                    ldd(f + 2)
                s = f % NWD
                for cc in range(2):
                    c = 2 * hf + cc
                    bank = 4 + cc
                    tk = slice(CH * c, CH * c + CH)
                    for k2 in range(NFF):
                        mm(P, K.pb[bank][:, :], wd[s][:, k2, :], actT[:, k2, CH * cc:CH * cc + CH], k2 == 0, k2 == NFF - 1,
                           reads=[('wd', s), 'actT'], writes=['ps%d' % bank])
                    P.op('dve', lambda e, bank=bank, f=f, tk=tk: e.scalar_tensor_tensor(
                        out=K.xT[:, f, tk], in0=K.pb[bank][:, :], scalar=K.modc[:, layer, 40 + f:41 + f], in1=K.xT[:, f, tk],
                        op0=ALU.mult, op1=ALU.add), reads=['ps%d' % bank, 'xT', 'modc'], writes=['xT'])
        P.report('ffn')
        P.barrier()
        P.emit()
    P.stack = _prev_stack


def phase_final(K):
    P = K.P
    identf = K.k['k_identf']
    _prev_stack = P.stack
    with ExitStack() as st:
        P.stack = st
        sq = [P.sb('sq%d' % i, [128, CH], F32) for i in range(4)]
        rstd = [P.sb('rstd%d' % i, [128, CH], F32) for i in range(2)]
        tmp = [P.sb('ntmp%d' % i, [128, CH], F32) for i in range(4)]
        K.epsb = P.sb('epsb', [128, 1], F32)
        P.op('pool', lambda e: e.memset(K.epsb[:], 1024.0 * 1e-6), writes=['epsb'])
        ot = [P.sb('ot%d' % i, [128, D], F32) for i in range(4)]

        def make_out_fn(c):
            def out_fn(kc, t_, tname):
                bank = 2 + (kc % 4)
                for j in range(4):
                    P.op('pe', lambda e, j=j: e.transpose(K.pb[bank][:, 128 * j:128 * j + 128], t_[:, 128 * j:128 * j + 128], identf[:]),
                         reads=[tname, 'k_identf'], writes=['ps%d' % bank])
                dstv = [ot[j][:, 128 * kc:128 * kc + 128] for j in range(4)]
                for j in range(4):
                    if kc % 2 == 0:
                        P.op('act', lambda e, j=j: e.copy(out=dstv[j], in_=K.pb[bank][:, 128 * j:128 * j + 128]), reads=['ps%d' % bank], writes=[('ot', j)])
                    else:
                        P.op('dve', lambda e, j=j: e.tensor_copy(out=dstv[j], in_=K.pb[bank][:, 128 * j:128 * j + 128]), reads=['ps%d' % bank], writes=[('ot', j)])
            return out_fn

        def after_chunk(c):
            for j in range(4):
                t = 4 * c + j
                P.dma('sp', K.out[128 * t:128 * t + 128, :], ot[j][:], reads=[('ot', j)], slot='o%d' % j)
        norm_all(K, K.gfin[:], make_out_fn, (sq, rstd, tmp), after_chunk)
        P.barrier()
        P.emit()
    P.stack = _prev_stack


_CACHE = {}


def _prep_inputs(inputs):
    f = lambda a: np.ascontiguousarray(np.asarray(a, dtype=np.float32))
    shared = relayout_weights(inputs)
    for nm in ('norm1_g', 'norm2_g', 'final_g', 'b_mod', 'conv_w', 'conv_b'):
        shared[nm] = f(inputs[nm]).reshape(IN_SHAPES[nm])
    shared.update(make_consts())
    maps = []
    x = f(inputs['x'])
    c = f(inputs['c'])
    for b in range(8):
        m = dict(shared)
        m['x'] = x[b]
        m['c'] = c[b].reshape(8, 128)
        maps.append(m)
    return maps


def kernel(**inputs):
    if 'nc' not in _CACHE:
        _CACHE['nc'] = build(2)[0]
    nc = _CACHE['nc']
    maps = _prep_inputs(inputs)
    res = run_bass_kernel_spmd(nc, maps, core_ids=list(range(8)))
    out = np.stack([np.asarray(res.results[b]['out'], dtype=np.float32) for b in range(8)], axis=0)
    return out
```

```python
import numpy as np
import ml_dtypes
from contextlib import ExitStack
import concourse.bass as bass
import concourse.mybir as mybir
from concourse.bass_utils import run_bass_kernel_spmd

F32 = mybir.dt.float32
BF16 = mybir.dt.bfloat16
ALU = mybir.AluOpType
AF = mybir.ActivationFunctionType
AX = mybir.AxisListType
ENGS = ['pe', 'act', 'dve', 'pool', 'sp']

S, D, KC, NT, NCH, CH = 2048, 1024, 8, 16, 4, 512
DFF = 2816
NFF = 22
NEG = -30000.0
SCALE = 0.125
N_IN = 5656
OFF_AQ, OFF_AK, OFF_AV, OFF_BQ = 0, 768, 1536, 2304
OFF_KC, OFF_VC, OFF_KSL, OFF_VSL, OFF_KWN, OFF_VWN = 2816, 2944, 3072, 3200, 3328, 3456
OFF_BLOG, OFF_GA, OFF_GB = 3584, 3608, 4632


class Prog:
    def __init__(self, nc, stack):
        self.nc = nc
        self.stack = stack
        self.gstack = stack
        self.q = {e: [] for e in ENGS}
        self.cnt = {e: 0 for e in ENGS}
        self.sems = {e: stack.enter_context(nc.semaphore('s_' + e)) for e in ENGS}
        self.waited = {e: {} for e in ENGS}
        self.res = {}
        self.dslots = {}

    def sb(self, name, shape, dtype, glob=False):
        st = self.gstack if glob else self.stack
        self.nalloc = getattr(self, 'nalloc', 0) + 1
        return st.enter_context(self.nc.sbuf_tensor('%s_%d' % (name, self.nalloc), list(shape), dtype))

    def _norm(self, reads, writes):
        excl = [r for r in reads if isinstance(r, str) and r.startswith('ps')]
        if excl:
            reads = [r for r in reads if r not in excl]
            writes = list(writes) + excl
        return reads, writes

    def _deps(self, eng, reads, writes):
        need = {}

        def add(k, v):
            if need.get(k, 0) < v:
                need[k] = v
        reads, writes = self._norm(reads, writes)
        for r in reads:
            st = self.res.get(r)
            if st and st['w']:
                add(*st['w'])
        for w in writes:
            st = self.res.get(w)
            if st:
                if st['w']:
                    add(*st['w'])
                for k, v in st['r'].items():
                    add(k, v)
        out = []
        for k, v in need.items():
            if k == eng and eng == 'pe':
                continue
            if self.waited[eng].get(k, 0) >= v:
                continue
            self.waited[eng][k] = v
            out.append((k, v))
        return out

    def _mark(self, tok, reads, writes):
        reads, writes = self._norm(reads, writes)
        for r in reads:
            st = self.res.setdefault(r, {'w': None, 'r': {}})
            if st['r'].get(tok[0], 0) < tok[1]:
                st['r'][tok[0]] = tok[1]
        for w in writes:
            self.res[w] = {'w': tok, 'r': {}}

    def _sem(self, k):
        return self.sems[k] if k in self.sems else self.dslots[k]['sem']

    def op(self, eng, fn, reads=(), writes=()):
        waits = [(self._sem(k), v) for k, v in self._deps(eng, reads, writes)]
        self.cnt[eng] += 1
        tok = (eng, self.cnt[eng])
        mysem = self.sems[eng]

        def run(e, waits=waits, fn=fn, mysem=mysem):
            for s, v in waits:
                e.wait_ge(s, v)
            fn(e).then_inc(mysem, 1)
        self.q[eng].append(run)
        self._mark(tok, reads, writes)
        return tok

    def dma(self, eng, out, in_, reads=(), writes=(), slot='d0'):
        slot = eng + '_' + slot
        if slot not in self.dslots:
            self.dslots[slot] = {'sem': self.gstack.enter_context(self.nc.semaphore('d_' + slot)), 'n': 0}
        ds = self.dslots[slot]
        deps = self._deps(eng, reads, writes)
        if ds['n'] > 0 and self.waited[eng].get(slot, 0) < 16 * ds['n']:
            self.waited[eng][slot] = 16 * ds['n']
            deps = [d for d in deps if d[0] != slot] + [(slot, 16 * ds['n'])]
        waits = [(self._sem(k), v) for k, v in deps]
        ds['n'] += 1
        tok = (slot, 16 * ds['n'])
        dsem = ds['sem']

        def run(e, waits=waits, out=out, in_=in_, dsem=dsem):
            for s, v in waits:
                e.wait_ge(s, v)
            e.dma_start(out=out, in_=in_).then_inc(dsem, 16)
        self.q[eng].append(run)
        self._mark(tok, reads, writes)
        return tok

    def report(self, tag):
        import os
        if os.environ.get('SBUF_REPORT'):
            print('SBUF', tag, 'remaining', self.nc.sbuf_bytes_remaining, flush=True)

    def barrier(self):
        targets = [(k, 16 * ds['n']) for k, ds in self.dslots.items() if ds['n'] > 0]
        targets += [(e, self.cnt[e]) for e in ENGS if self.cnt[e] > 0]
        for eng in ENGS:
            waits = []
            for k, v in targets:
                if k == eng:
                    continue
                if self.waited[eng].get(k, 0) >= v:
                    continue
                self.waited[eng][k] = v
                waits.append((self._sem(k), v))

            def run(e, waits=waits):
                for s, v in waits:
                    e.wait_ge(s, v)
            self.q[eng].append(run)

    def emit(self):
        q = self.q
        with self.nc.Block() as block:
            @block.tensor
            def _(e):
                for f in q['pe']:
                    f(e)

            @block.scalar
            def _(e):
                for f in q['act']:
                    f(e)

            @block.vector
            def _(e):
                for f in q['dve']:
                    f(e)

            @block.gpsimd
            def _(e):
                for f in q['pool']:
                    f(e)

            @block.sync
            def _(e):
                for f in q['sp']:
                    f(e)
        self.q = {e: [] for e in ENGS}


def make_consts():
    bf = ml_dtypes.bfloat16
    C = {}
    C['k_identb'] = np.eye(128, dtype=np.float32).astype(bf)
    C['k_identf'] = np.eye(128, dtype=np.float32)
    C['k_onesf'] = np.ones((128, 128), np.float32)
    pm = np.zeros((128, 128), np.float32)
    for m in range(128):
        blk, d = divmod(m, 64)
        pm[blk * 64 + (d + 32) % 64, m] = 1.0
    C['k_pm'] = pm.astype(bf)
    inv = (1.0 / (np.float32(10000.0) ** (np.arange(0, 64, 2, dtype=np.float32) / np.float32(64)))).astype(np.float32)
    ang = (np.arange(S, dtype=np.float32)[:, None] * inv[None, :]).astype(np.float32)
    cos, sin = np.cos(ang).astype(np.float32), np.sin(ang).astype(np.float32)
    cosT = np.zeros((128, S), np.float32)
    sinT = np.zeros((128, S), np.float32)
    for row in range(128):
        d = row % 64
        cosT[row] = cos[:, d % 32]
        sinT[row] = (-1.0 if d < 32 else 1.0) * sin[:, d % 32]
    C['k_cos'] = cosT
    C['k_sin'] = sinT
    kk = np.arange(128)[:, None]
    qq = np.arange(128)[None, :]
    mc = np.where(kk > qq, NEG, 0.0).astype(np.float32)
    ml = np.where(kk < qq, NEG, 0.0).astype(np.float32)
    mf = np.where(kk <= qq, NEG, 0.0).astype(np.float32)
    C['k_mpair'] = np.concatenate([mc, ml], 1).astype(bf)
    C['k_mc4'] = np.tile(mc, (1, 4)).astype(bf)
    C['k_mfar'] = mf.astype(bf)
    n = np.arange(128)[:, None, None]
    c = np.arange(4)[None, :, None]
    q = np.arange(512)[None, None, :]
    cneg = np.where((16 * n + 31 > 512 * c + q) | (n == 127), NEG, 0.0).astype(np.float32)
    C['k_cneg'] = cneg.reshape(128, 2048).astype(bf)
    p = np.arange(128)[:, None]
    u = np.arange(256)[None, :]
    C['k_mwide'] = np.where(16 * (u - 120) + 31 > p, NEG, 0.0).astype(np.float32).astype(bf)
    E = np.zeros((128, 2048), np.float32)
    for j in range(32):
        E[j, 64 * j:64 * j + 64] = 1.0
    C['k_E'] = E.astype(bf)
    Fa = np.full((128, 8, 32), -1e30, np.float32)
    for i in range(8, 16):
        for pp in range(128):
            cur = 2 * i + (1 if pp >= 64 else 0)
            for j in (0, cur, cur - 1):
                Fa[pp, i - 8, j] = 1e9
    C['k_F'] = Fa.reshape(128, 256)
    sg = np.zeros((24, 12, 128), np.float32)
    for h in range(8):
        for b in range(3):
            pr, par = divmod(h, 2)
            sg[h * 3 + b, pr * 3 + b, par * 64:(par + 1) * 64] = 1.0
    C['k_selg'] = sg.reshape(24, 1536).astype(bf)
    ol = np.zeros((128, 128), np.float32)
    ol[:, 0:64] = 1.0
    oh = np.zeros((128, 128), np.float32)
    oh[:, 64:128] = 1.0
    C['k_oneslo'] = ol.astype(bf)
    C['k_oneshi'] = oh.astype(bf)
    C['k_zeros'] = np.zeros((128, 128), np.float32).astype(bf)
    C['k_one1'] = np.ones((1, 1), np.float32)
    return C


CONST_SHAPES = {
    'k_identb': ([128, 128], BF16), 'k_identf': ([128, 128], F32), 'k_onesf': ([128, 128], F32),
    'k_pm': ([128, 128], BF16), 'k_cos': ([128, S], F32), 'k_sin': ([128, S], F32),
    'k_mpair': ([128, 256], BF16), 'k_mc4': ([128, 512], BF16), 'k_mfar': ([128, 128], BF16),
    'k_cneg': ([128, 2048], BF16), 'k_mwide': ([128, 256], BF16), 'k_E': ([128, 2048], BF16),
    'k_F': ([128, 256], F32), 'k_selg': ([24, 1536], BF16), 'k_oneslo': ([128, 128], BF16),
    'k_oneshi': ([128, 128], BF16), 'k_zeros': ([128, 128], BF16), 'k_one1': ([1, 1], F32),
}

IN_SHAPES = {
    'x': [S, D], 'c': [8, 128], 'norm1_g': [2, 8, 128], 'norm2_g': [2, 8, 128], 'final_g': [8, 128],
    'b_mod': [2, 48, 128], 'conv_w': [2, 3, 44, 128], 'conv_b': [2, 44, 128],
    'wmod_l': [2, 12, 128, KC * 512], 'wina_l': [2, 3, 3, 128, KC * 256], 'wnsa_l': [2, 128, 12480],
    'wmrg_l': [2, 8, 128, 22 * 128], 'wout_l': [2, 8, 128, KC * 128], 'wup_l': [2, NFF, 128, KC * 256],
    'wdn_l': [2, 8, 128, NFF * 128], 'w1d_l': [2, 2, 128, 32 * 256], 'w2d_l': [2, 128, 256], 'w2v_l': [2, 128, 128],
    'pe_l': [2, 2, 128, 32],
}
NSA_OFF = {'wkv': (0, 2048), 'wbq': (2048, 6144), 'wkd': (6144, 10240), 'wvv': (10240, 12288), 'wbl': (12288, 12480)}


def relayout_weights(inp):
    f = lambda a: np.asarray(a, dtype=np.float32)

    def blk(w, c0, n):
        k = w.shape[0] // 128
        return w[:, c0:c0 + n].reshape(k, 128, n).transpose(1, 0, 2)
    o = {}
    w_mod, w_in = f(inp['w_mod']), f(inp['w_in'])
    o['wmod_l'] = np.stack([np.stack([blk(w_mod[l], 512 * j, 512).reshape(128, -1) for j in range(12)]) for l in range(2)])
    o['wina_l'] = np.stack([np.stack([np.stack([blk(w_in[l], off + 256 * g, 256).reshape(128, -1) for off in (OFF_AQ, OFF_AK, OFF_AV)])
                                      for g in range(3)]) for l in range(2)])
    nsa = []
    for l in range(2):
        wkv = blk(w_in[l], OFF_KC, 256).reshape(128, -1)
        wbq = blk(w_in[l], OFF_BQ, 512).reshape(128, -1)
        wkd = np.zeros((128, KC, 2, 2, 2, 64), np.float32)
        for w_, off in ((0, OFF_KSL), (1, OFF_KWN)):
            for g in range(2):
                b = blk(w_in[l], off + 64 * g, 64)
                wkd[:, :, w_, g, 0, :] = b
                wkd[:, :, w_, g, 1, :] = b
        wvv = np.stack([blk(w_in[l], OFF_VSL, 128), blk(w_in[l], OFF_VWN, 128)], axis=2).reshape(128, -1)
        wbl = blk(w_in[l], OFF_BLOG, 24).reshape(128, -1)
        nsa.append(np.concatenate([wkv, wbq, wkd.reshape(128, -1), wvv, wbl], axis=1))
    o['wnsa_l'] = np.stack(nsa)
    wbra, wbrb, wout = f(inp['w_br_a']), f(inp['w_br_b']), f(inp['w_out'])
    o['wmrg_l'] = np.stack([np.stack([np.concatenate([blk(w_in[l], OFF_GA + 128 * fi, 128), blk(w_in[l], OFF_GB + 128 * fi, 128),
                                                      blk(wbra[l], 128 * fi, 128), blk(wbrb[l], 128 * fi, 128)], axis=1).reshape(128, -1)
                                      for fi in range(8)]) for l in range(2)])
    o['wout_l'] = np.stack([np.stack([blk(wout[l], 128 * fi, 128).reshape(128, -1) for fi in range(8)]) for l in range(2)])
    wup, wdn = f(inp['w_up']), f(inp['w_down'])
    o['wup_l'] = np.stack([np.stack([np.stack([blk(wup[l], 128 * fc, 128), blk(wup[l], DFF + 128 * fc, 128)], axis=2).reshape(128, -1)
                                     for fc in range(NFF)]) for l in range(2)])
    o['wdn_l'] = np.stack([np.stack([blk(wdn[l], 128 * fi, 128).reshape(128, -1) for fi in range(8)]) for l in range(2)])
    w1 = [f(inp['cmp_w1_k']), f(inp['cmp_w1_v'])]
    o['w1d_l'] = np.stack([np.stack([np.tile(w1[kv][l].reshape(32, 64, 256).transpose(1, 0, 2), (2, 1, 1)).reshape(128, -1)
                                     for kv in range(2)]) for l in range(2)])
    w2k, w2v = f(inp['cmp_w2_k']), f(inp['cmp_w2_v'])
    o['w2d_l'] = np.stack([np.tile(blk(w2k[l], 0, 64), (1, 1, 2)).reshape(128, -1) for l in range(2)])
    o['w2v_l'] = np.stack([blk(w2v[l], 0, 64).reshape(128, -1) for l in range(2)])
    pe = [f(inp['cmp_pe_k']), f(inp['cmp_pe_v'])]
    o['pe_l'] = np.stack([np.stack([np.tile(pe[kv][l].T, (2, 1)) for kv in range(2)]) for l in range(2)])
    return {k_: np.ascontiguousarray(v) for k_, v in o.items()}


SCOPED_CONSTS = ('k_cos', 'k_sin', 'k_cneg', 'k_E')


class Ctx:
    pass


def load_scoped_consts(K, names):
    P = K.P
    for nm in names:
        shp, dt = CONST_SHAPES[nm]
        if nm in ('k_cos', 'k_sin'):
            K.k[nm] = P.sb('c' + nm, shp, BF16)
            K.wslot += 1
            P.dma('pool', K.k[nm][:], K.d[nm], writes=[nm], slot='w%d' % (K.wslot % 6))
        else:
            K.k[nm] = P.sb('c' + nm, shp, dt)
            P.dma('sp', K.k[nm][:], K.d[nm], writes=[nm], slot='c%d' % (len(nm) % 4))


def build(nlayers=2, taps=(), upto='all'):
    nc = bass.Bass("TRN2", target_bir_lowering=False)
    K = Ctx()
    K.nc = nc
    K.d = {}
    for nm, shp in IN_SHAPES.items():
        K.d[nm] = nc.dram_tensor(nm, list(shp), F32, kind="ExternalInput").ap()
    for nm, (shp, dt) in CONST_SHAPES.items():
        K.d[nm] = nc.dram_tensor(nm, list(shp), dt, kind="ExternalInput").ap()
    K.out = nc.dram_tensor("out", [S, D], F32, kind="ExternalOutput").ap()
    K.xpark = nc.dram_tensor("xpark", [128, KC * S], F32).ap()
    K.taps = {}
    K.tapset = set(taps)

    with ExitStack() as gst:
        P = Prog(nc, gst)
        K.P = P
        K.pb = [gst.enter_context(nc.psum_tensor('pb%d' % i, [128, 512], F32)) for i in range(8)]
        K.arena = P.sb('arena', [128, KC * S], F32, glob=True)
        K.xT = K.arena[:].rearrange("p (k s) -> p k s", k=KC)
        K.hT = P.sb('hT', [128, KC, S], BF16, glob=True)
        K.k = {}
        for nm, (shp, dt) in CONST_SHAPES.items():
            if nm in SCOPED_CONSTS:
                continue
            K.k[nm] = P.sb('c' + nm, shp, dt, glob=True)
        K.modc = P.sb('modc', [128, 2, 48], F32, glob=True)
        K.gs = P.sb('gs', [128, 2, 2, 8], F32, glob=True)
        K.gfin = P.sb('gfin', [128, 8], F32, glob=True)
        K.cvw = P.sb('cvw', [128, 2, 4, 44], F32, glob=True)
        K.halo = P.sb('halo', [128, 44, 2], F32, glob=True)
        K.wslot = 0

        phase_start(K)
        order = ['start', 'norm', 'dil', 'nsa', 'merge', 'ffn', 'all']
        lvl = order.index(upto)
        for layer in range(nlayers):
            with ExitStack() as lst:
                P.stack = lst
                K.yaT = P.sb('yaT', [128, 2, S], BF16)
                K.ybT = P.sb('ybT', [128, 4, S], BF16)
                stop_early = False
                with ExitStack() as dsc:
                    P.stack = dsc
                    K.pre_dil = [P.sb('pwq', [128, KC, 256], BF16), P.sb('pwk', [128, KC, 256], BF16), P.sb('pwv', [128, KC, 256], BF16)]
                    for w_i, nm_ in enumerate(('wq', 'wk', 'wv')):
                        ldflat(K, K.pre_dil[w_i][:].rearrange("p k n -> p (k n)"), K.d['wina_l'][layer, 0, w_i], (nm_, 0))
                    P.dma('sp', K.xpark, K.arena[:], reads=['xT'], writes=['xpark'], slot='park')
                    phase_norm(K, layer, 0)
                    tap_h(K, 'h1_%d' % layer)
                    if lvl <= 1:
                        P.barrier()
                        P.emit()
                        stop_early = True
                    else:
                        park_x(K)
                        phase_dilated(K, layer)
                P.stack = lst
                if stop_early:
                    break
                if lvl >= 3:
                    phase_nsa(K, layer)
                if lvl >= 4:
                    unpark_x(K)
                    phase_merge(K, layer)
                P.barrier()
                P.emit()
            P.stack = gst
            if lvl <= 3:
                break
            tap_x(K, 'xm%d' % layer)
            if lvl <= 4:
                break
            with ExitStack() as fst:
                P.stack = fst
                K.pre_wup = [P.sb('pwup%d' % i, [128, KC, 2, 128], BF16) for i in range(4)]
                for fc_ in range(3):
                    ldflat(K, K.pre_wup[fc_][:].rearrange("p k a n -> p (k a n)"), K.d['wup_l'][layer, fc_], ('wup', fc_))
                phase_norm(K, layer, 1)
                phase_ffn(K, layer)
            P.stack = gst
            tap_x(K, 'xf%d' % layer)
        if lvl >= 6:
            phase_final(K)
        P.barrier()
        P.emit()
    return nc, K


def tap_x(K, name):
    if name not in K.tapset:
        return
    P = K.P
    t = K.nc.dram_tensor("tap_" + name, [128, KC * S], F32, kind="ExternalOutput").ap()
    P.dma('sp', t, K.arena[:], reads=['xT'], slot='tap')
    K.taps[name] = t


def tap_h(K, name):
    if name not in K.tapset:
        return
    P = K.P
    t = K.nc.dram_tensor("tap_" + name, [128, KC * S], BF16, kind="ExternalOutput").ap()
    P.dma('sp', t, K.hT[:].rearrange("p k s -> p (k s)"), reads=['hT'], slot='tap')
    K.taps[name] = t


def tap_any(K, name, ap, reads, shape, dt):
    if name not in K.tapset:
        return
    t = K.nc.dram_tensor("tap_" + name, list(shape), dt, kind="ExternalOutput").ap()
    K.P.dma('sp', t, ap, reads=reads, slot='tap')
    K.taps[name] = t


def mm(P, out, lhsT, rhs, start, stop, reads, writes):
    return P.op('pe', lambda e: e.matmul(out, lhsT=lhsT, rhs=rhs, start=start, stop=stop, skip_group_check=True),
                reads=reads, writes=writes)


def load_w(K, dst, src2d, wname, eng='pool'):
    K.wslot += 1
    return K.P.dma(eng, dst, src2d.rearrange("(k p) n -> p k n", p=128), writes=[wname], slot='w%d' % (K.wslot % 6))


def ldflat(K, dst_flat, src, wname, eng='pool'):
    K.wslot += 1
    return K.P.dma(eng, dst_flat, src, writes=[wname], slot='w%d' % (K.wslot % 6))


def proj_fm(K, ps, psname, wt, wname, cols, tok0, ntok, M=128):
    P = K.P
    for kc in range(KC):
        mm(P, ps[0:M, 0:ntok], wt[:, kc, cols], K.hT[:, kc, tok0:tok0 + ntok], kc == 0, kc == KC - 1,
           reads=[wname, 'hT'], writes=[psname])


def phase_start(K):
    P, nc, d = K.P, K.nc, K.d
    _prev_stack = P.stack
    with ExitStack() as st:
        P.stack = st
        for i, (nm, (shp, dt)) in enumerate(CONST_SHAPES.items()):
            if nm in SCOPED_CONSTS:
                continue
            P.dma('sp', K.k[nm][:], d[nm], writes=[nm], slot='c%d' % (i % 4))
        identf = K.k['k_identf']
        rows = P.sb('rows', [128, 128], F32)
        rows2 = P.sb('rows2', [128, 128], F32)
        cols = P.sb('cols', [128, 512], F32)
        P.op('pool', lambda e: e.memset(rows[:], 0.0), writes=['rows'])
        P.op('pool', lambda e: e.memset(rows2[:], 0.0), writes=['rows2'])
        off = 0
        entries = {}
        rnames = []
        for nm, ap, n in [('c', d['c'], 8), ('g10', d['norm1_g'][0], 8), ('g11', d['norm1_g'][1], 8),
                          ('g20', d['norm2_g'][0], 8), ('g21', d['norm2_g'][1], 8), ('gf', d['final_g'], 8),
                          ('bm0', d['b_mod'][0], 48)]:
            entries[nm] = (off, n)
            P.dma('sp', rows[off:off + n, :], ap, reads=['rows'], writes=[('rows', nm)], slot='r%d' % (len(rnames) % 4))
            rnames.append(('rows', nm))
            off += n
        P.dma('sp', rows2[0:48, :], d['b_mod'][1], reads=['rows2'], writes=[('rows2', 0)], slot='r0')
        P.op('pe', lambda e: e.transpose(K.pb[0][:, 0:128], rows[:, :], identf[:]), reads=rnames + ['rows', 'k_identf'], writes=['ps0'])
        P.op('dve', lambda e: e.tensor_copy(out=cols[:, 0:128], in_=K.pb[0][:, 0:128]), reads=['ps0'], writes=['colsA'])
        P.op('pe', lambda e: e.transpose(K.pb[1][:, 0:128], rows2[:, :], identf[:]), reads=[('rows2', 0), 'rows2', 'k_identf'], writes=['ps1'])
        P.op('dve', lambda e: e.tensor_copy(out=cols[:, 128:256], in_=K.pb[1][:, 0:128]), reads=['ps1'], writes=['colsB'])

        def colof(nm):
            o, n = entries[nm]
            return cols[:, o:o + n]
        for l in range(2):
            rc = P.sb('rc%d' % l, [128, 128], F32)
            P.op('pool', lambda e, rc=rc: e.memset(rc[:], 0.0), writes=[('rc', l)])
            for j in range(2):
                P.dma('sp', rc[44 * j:44 * j + 44, :], d['conv_w'][l, j], reads=[('rc', l)], writes=[('rc', l, j)], slot='r2')
            rd = P.sb('rd%d' % l, [128, 128], F32)
            P.op('pool', lambda e, rd=rd: e.memset(rd[:], 0.0), writes=[('rd', l)])
            P.dma('sp', rd[0:44, :], d['conv_w'][l, 2], reads=[('rd', l)], writes=[('rd', l, 0)], slot='r3')
            P.dma('sp', rd[44:88, :], d['conv_b'][l], reads=[('rd', l)], writes=[('rd', l, 1)], slot='r3')
            P.op('pe', lambda e, rc=rc: e.transpose(K.pb[2][:, 0:128], rc[:, :], identf[:]),
                 reads=[('rc', l), ('rc', l, 0), ('rc', l, 1), 'k_identf'], writes=['ps2'])
            P.op('dve', lambda e, l=l: e.tensor_copy(out=K.cvw[:, l, 0:2, :], in_=K.pb[2][:, 0:88].rearrange("p (a b) -> p a b", a=2)),
                 reads=['ps2'], writes=['cvw'])
            P.op('pe', lambda e, rd=rd: e.transpose(K.pb[3][:, 0:128], rd[:, :], identf[:]),
                 reads=[('rd', l), ('rd', l, 0), ('rd', l, 1), 'k_identf'], writes=['ps3'])
            P.op('dve', lambda e, l=l: e.tensor_copy(out=K.cvw[:, l, 2:4, :], in_=K.pb[3][:, 0:88].rearrange("p (a b) -> p a b", a=2)),
                 reads=['ps3'], writes=['cvw'])

        sc = P.sb('sc', [128, 8], F32)
        P.op('act', lambda e: e.activation(out=sc[:], in_=colof('c'), func=AF.Silu), reads=['colsA'], writes=['sc'])
        mrow = [P.sb('mrow%d' % i, [1, 512], F32) for i in range(2)]
        wm = [P.sb('wm%d' % i, [128, KC, 512], F32) for i in range(2)]
        pieces = [(l, j) for l in range(2) for j in range(12)]
        one1 = K.k['k_one1']

        def ld(i):
            l, j = pieces[i]
            ldflat(K, wm[i % 2][:].rearrange("p k n -> p (k n)"), d['wmod_l'][l, j], ('wm', i % 2), eng='sp')
        ld(0)
        for i, (l, j) in enumerate(pieces):
            if i + 1 < len(pieces):
                ld(i + 1)
            ps = K.pb[4 + (i % 2)]
            psn = 'ps%d' % (4 + (i % 2))
            for kc in range(KC):
                mm(P, ps[0:1, :], sc[:, kc:kc + 1], wm[i % 2][:, kc, :], kc == 0, kc == KC - 1,
                   reads=['sc', ('wm', i % 2)], writes=[psn])
            mr = mrow[i % 2]
            P.op('act', lambda e, ps=ps, mr=mr: e.copy(out=mr[0:1, :], in_=ps[0:1, :]), reads=[psn], writes=[('mrow', i % 2)])
            for q_ in range(4):
                col = l * 48 + 4 * j + q_
                mm(P, K.pb[6][:, col:col + 1], mr[0:1, 128 * q_:128 * q_ + 128], one1[0:1, 0:1], True, True,
                   reads=[('mrow', i % 2), 'k_one1'], writes=['ps6'])
        bmc = [colof('bm0'), cols[:, 128:176]]
        for l in range(2):
            P.op('dve', lambda e, l=l: e.tensor_tensor(out=K.modc[:, l, :], in0=K.pb[6][:, l * 48:l * 48 + 48], in1=bmc[l], op=ALU.add),
                 reads=['ps6', 'colsA', 'colsB'], writes=['modc'])
        for l in range(2):
            for w, gname, sidx in ((0, 'g1%d' % l, 1), (1, 'g2%d' % l, 4)):
                P.op('dve', lambda e, l=l, w=w, gname=gname, sidx=sidx: e.scalar_tensor_tensor(
                    out=K.gs[:, l, w, :], in0=K.modc[:, l, 8 * sidx:8 * sidx + 8], scalar=1.0, in1=colof(gname),
                    op0=ALU.add, op1=ALU.mult), reads=['modc', 'colsA'], writes=['gs'])
                P.op('dve', lambda e, l=l, w=w: e.tensor_scalar(out=K.gs[:, l, w, :], in0=K.gs[:, l, w, :], scalar1=32.0, scalar2=None,
                                                             op0=ALU.mult), reads=['gs'], writes=['gs'])
        P.op('dve', lambda e: e.tensor_scalar(out=K.gfin[:], in0=colof('gf'), scalar1=32.0, scalar2=None, op0=ALU.mult),
             reads=['colsA'], writes=['gfin'])
        tap_any(K, 'modc', K.modc[:].rearrange("p a b -> p (a b)"), ['modc'], [128, 96], F32)

        xin = [P.sb('xin%d' % i, [128, D], F32) for i in range(2)]
        for t in range(NT):
            xi = xin[t % 2]
            P.dma('sp', xi[:], d['x'][128 * t:128 * t + 128, :], writes=[('xin', t % 2)], slot='x%d' % (t % 2))
            for half in range(2):
                ps = K.pb[(2 * t + half) % 4]
                psn = 'ps%d' % ((2 * t + half) % 4)
                for j in range(4):
                    kc = half * 4 + j
                    P.op('pe', lambda e, ps=ps, xi=xi, kc=kc, j=j: e.transpose(ps[:, 128 * j:128 * j + 128], xi[:, 128 * kc:128 * kc + 128], identf[:]),
                         reads=[('xin', t % 2), 'k_identf'], writes=[psn])
                eng = 'act' if half == 0 else 'dve'
                if eng == 'act':
                    P.op('act', lambda e, ps=ps, half=half, t=t: e.copy(out=K.xT[:, half * 4:half * 4 + 4, 128 * t:128 * t + 128],
                                                                         in_=ps[:, :].rearrange("p (a b) -> p a b", a=4)),
                         reads=[psn], writes=['xT'])
                else:
                    P.op('dve', lambda e, ps=ps, half=half, t=t: e.tensor_copy(out=K.xT[:, half * 4:half * 4 + 4, 128 * t:128 * t + 128],
                                                                                in_=ps[:, :].rearrange("p (a b) -> p a b", a=4)),
                         reads=[psn], writes=['xT'])
        P.barrier()
        P.emit()
    P.stack = _prev_stack


def norm_A(K, c, st_tiles):
    P = K.P
    sq, rstds, tmp = st_tiles
    rstd = rstds[c % 2]
    ps = K.pb[c % 2]
    psn = 'ps%d' % (c % 2)
    onesf = K.k['k_onesf']
    for kc in range(KC):
        s_ = sq[kc % len(sq)]
        if kc % 2 == 0:
            P.op('act', lambda e, s_=s_, kc=kc: e.activation(out=s_[:], in_=K.xT[:, kc, CH * c:CH * c + CH], func=AF.Square),
                 reads=['xT'], writes=[('sq', kc % len(sq))])
        else:
            P.op('pool', lambda e, s_=s_, kc=kc: e.tensor_tensor(out=s_[:], in0=K.xT[:, kc, CH * c:CH * c + CH],
                                                                 in1=K.xT[:, kc, CH * c:CH * c + CH], op=ALU.mult),
                 reads=['xT'], writes=[('sq', kc % len(sq))])
        mm(P, ps[:, :], onesf[:], s_[:], kc == 0, kc == KC - 1, reads=['k_onesf', ('sq', kc % len(sq))], writes=[psn])
    P.op('act', lambda e: e.activation(out=rstd[:], in_=ps[:, :], func=AF.Sqrt, bias=K.epsb[:, 0:1], scale=1.0),
         reads=[psn, 'epsb'], writes=[('rstd', c % 2)])
    P.op('dve', lambda e: e.reciprocal(out=rstd[:], in_=rstd[:]), reads=[('rstd', c % 2)], writes=[('rstd', c % 2)])


def norm_B(K, c, gcol, out_fn, st_tiles):
    P = K.P
    sq, rstds, tmp = st_tiles
    rstd = rstds[c % 2]
    for kc in range(KC):
        t_ = tmp[kc % len(tmp)]
        P.op('dve', lambda e, t_=t_, kc=kc: e.scalar_tensor_tensor(out=t_[:], in0=K.xT[:, kc, CH * c:CH * c + CH], scalar=gcol[:, kc:kc + 1],
                                                                 in1=rstd[:], op0=ALU.mult, op1=ALU.mult),
             reads=['xT', ('rstd', c % 2), 'gs', 'gfin'], writes=[('ntmp', kc % len(tmp))])
        out_fn(kc, t_, ('ntmp', kc % len(tmp)))


def norm_all(K, gcol, make_out_fn, st_tiles, after_chunk=None):
    norm_A(K, 0, st_tiles)
    for c in range(NCH):
        if c + 1 < NCH:
            norm_A(K, c + 1, st_tiles)
        norm_B(K, c, gcol, make_out_fn(c), st_tiles)
        if after_chunk:
            after_chunk(c)


def phase_norm(K, layer, which):
    P = K.P
    _prev_stack = P.stack
    with ExitStack() as st:
        P.stack = st
        sq = [P.sb('sq%d' % i, [128, CH], F32) for i in range(4)]
        rstd = [P.sb('rstd%d' % i, [128, CH], F32) for i in range(2)]
        tmp = [P.sb('ntmp%d' % i, [128, CH], F32) for i in range(4)]
        K.epsb = P.sb('epsb', [128, 1], F32)
        P.op('pool', lambda e: e.memset(K.epsb[:], 1024.0 * 1e-6), writes=['epsb'])
        gcol = K.gs[:, layer, which, :]
        sh = K.modc[:, layer, (0 if which == 0 else 24):(8 if which == 0 else 32)]

        def make_out_fn(c):
            def out_fn(kc, t_, tname):
                P.op('act', lambda e: e.activation(out=K.hT[:, kc, CH * c:CH * c + CH], in_=t_[:], func=AF.Identity,
                                                   bias=sh[:, kc:kc + 1], scale=1.0),
                     reads=[tname, 'modc'], writes=['hT'])
            return out_fn
        norm_all(K, gcol, make_out_fn, (sq, rstd, tmp))
        P.barrier()
        P.emit()
    P.stack = _prev_stack


def park_x(K):
    P = K.P
    P.barrier()


def unpark_x(K):
    P = K.P
    P.barrier()
    P.dma('sp', K.arena[:], K.xpark, reads=['xpark'], writes=['xT'], slot='park')


def rope_tile(K, ps, psn, tok0, ntok, out_ap, out_name, W, nope_ap=None, nope_name=None, i=0):
    P = K.P
    raw, t1, t2 = W['raw'][i % 2], W['t1'][i % 2], W['t2'][i % 2]
    rn, t1n, t2n = ('raw', i % 2), ('t1', i % 2), ('t2', i % 2)
    swb = W.get('swbanks', (7,))[i % len(W.get('swbanks', (7,)))]
    psw = K.pb[swb]
    if nope_ap is not None:
        P.op('act', lambda e: e.copy(out=nope_ap, in_=ps[:, 0:ntok]), reads=[psn], writes=[nope_name])
        rawap, rawname = nope_ap, nope_name
    else:
        P.op('act', lambda e: e.copy(out=raw[:, 0:ntok], in_=ps[:, 0:ntok]), reads=[psn], writes=[rn])
        rawap, rawname = raw[:, 0:ntok], rn
    P.op('dve', lambda e: e.tensor_tensor(out=t1[:, 0:ntok], in0=ps[:, 0:ntok], in1=K.k['k_cos'][:, tok0:tok0 + ntok], op=ALU.mult),
         reads=[psn, 'k_cos'], writes=[t1n])
    mm(P, psw[:, 0:ntok], K.k['k_pm'][:], rawap, True, True, reads=['k_pm', rawname], writes=['ps%d' % swb])
    P.op('dve', lambda e: e.tensor_tensor(out=t2[:, 0:ntok], in0=psw[:, 0:ntok], in1=K.k['k_sin'][:, tok0:tok0 + ntok], op=ALU.mult),
         reads=['ps%d' % swb, 'k_sin'], writes=[t2n])
    P.op('pool', lambda e: e.tensor_tensor(out=out_ap, in0=t1[:, 0:ntok].rearrange(W['inre'], **W['inkw']) if W.get('inre') else t1[:, 0:ntok],
                                           in1=t2[:, 0:ntok].rearrange(W['inre'], **W['inkw']) if W.get('inre') else t2[:, 0:ntok], op=ALU.add),
         reads=[t1n, t2n], writes=[out_name])


def acc_init(K, bank):
    P = K.P
    mm(P, K.pb[bank][:, :], K.k['k_zeros'][:], K.k['k_mc4'][:], True, False, reads=['k_zeros', 'k_mc4'], writes=['ps%d' % bank])


class Pipe:
    def __init__(self, K, depth=2):
        self.K, self.depth, self.pending = K, depth, []

    def unit(self, st_bank, score_mms, nq, pt, ptname, pv_list, pre=None, post=None, first=False):
        K = self.K
        P = K.P
        ps = K.pb[st_bank]
        psn = 'ps%d' % st_bank
        for idx, (lhsT, rhs, c0, n, rd) in enumerate(score_mms):
            mm(P, ps[0:lhsT.shape[1], c0:c0 + n], lhsT, rhs, idx == 0, idx == len(score_mms) - 1, reads=rd, writes=[psn])
        P.op('act', lambda e: e.activation(out=pt[:, 0:nq], in_=ps[:, 0:nq], func=AF.Exp, scale=SCALE), reads=[psn], writes=[ptname])
        self.pending.append((pv_list, pt, ptname, pre, post, first))
        if len(self.pending) > self.depth:
            self._pop()

    def pair(self, ua, ub):
        K = self.K
        P = K.P
        for u in (ua, ub):
            ps, psn = K.pb[u['bank']], 'ps%d' % u['bank']
            for idx in range(u['nqk']):
                lhsT, rhs, c0, n, rd = u['sm'][idx]
                mm(P, ps[0:lhsT.shape[1], c0:c0 + n], lhsT, rhs, idx == 0, False, reads=rd, writes=[psn])
        for u in (ua, ub):
            ps, psn = K.pb[u['bank']], 'ps%d' % u['bank']
            nsm = len(u['sm'])
            for idx in range(u['nqk'], nsm):
                lhsT, rhs, c0, n, rd = u['sm'][idx]
                mm(P, ps[0:lhsT.shape[1], c0:c0 + n], lhsT, rhs, False, idx == nsm - 1, reads=rd, writes=[psn])
        for u in (ua, ub):
            ps, psn = K.pb[u['bank']], 'ps%d' % u['bank']
            pt, nq = u['pt'], u['nq']
            P.op('act', lambda e, pt=pt, ps=ps, nq=nq: e.activation(out=pt[:, 0:nq], in_=ps[:, 0:nq], func=AF.Exp, scale=SCALE),
                 reads=[psn], writes=[u['ptname']])
            self.pending.append((u['pv'], pt, u['ptname'], None, u.get('post'), u.get('first', False)))
        while len(self.pending) > self.depth:
            self._pop()

    def _pop(self):
        K = self.K
        P = K.P
        pv_list, pt, ptname, pre, post, first = self.pending.pop(0)
        if pre:
            pre()
        started = set()
        for (ab, lhsT, lrd, c0, n, a0) in pv_list:
            st_ = first and ab not in started
            started.add(ab)
            mm(P, K.pb[ab][:, a0:a0 + n], lhsT, pt[:, c0:c0 + n], st_, False, reads=lrd + [ptname], writes=['ps%d' % ab])
        if post:
            post()

    def flush(self):
        while self.pending:
            self._pop()


def phase_dilated(K, layer):
    P, d = K.P, K.d
    ab = K.arena[:].bitcast(BF16)
    QT = ab[:, 0:4096].rearrange("p (a s) -> p a s", a=2)
    KT = ab[:, 4096:8192].rearrange("p (a s) -> p a s", a=2)
    Vp = ab[:, 8192:16384].rearrange("p (t h c) -> p t h c", t=NT, h=4)
    af = K.arena[:]
    numacc = af[:, 8192:12288].rearrange("p (a s) -> p a s", a=2)
    denacc = af[:, 12288:16384].rearrange("p (a s) -> p a s", a=2)
    ident = K.k['k_identb']
    _prev_stack = P.stack
    with ExitStack() as st:
        P.stack = st
        wq = [K.pre_dil[0], P.sb('wq1', [128, KC, 256], BF16)]
        wk = [K.pre_dil[1], P.sb('wk1', [128, KC, 256], BF16)]
        wv = [K.pre_dil[2], P.sb('wv1', [128, KC, 256], BF16)]
        W = {'raw': [P.sb('raw%d' % i, [128, CH], BF16) for i in range(2)],
             't1': [P.sb('t1_%d' % i, [128, CH], F32) for i in range(2)],
             't2': [P.sb('t2_%d' % i, [128, CH], F32) for i in range(2)], 'swbanks': (7, 6)}
        pts = [P.sb('pt%d' % i, [128, CH], BF16) for i in range(4)]
        rden = P.sb('rden', [128, S], F32)
        load_scoped_consts(K, ['k_cos', 'k_sin'])
        P.op('pool', lambda e: e.memset(ab[:, 8192:16384], 0.0), writes=['Vp'])

        def ldw(g):
            ldflat(K, wq[g % 2][:].rearrange("p k n -> p (k n)"), d['wina_l'][layer, g, 0], ('wq', g % 2))
            ldflat(K, wk[g % 2][:].rearrange("p k n -> p (k n)"), d['wina_l'][layer, g, 1], ('wk', g % 2))
            ldflat(K, wv[g % 2][:].rearrange("p k n -> p (k n)"), d['wina_l'][layer, g, 2], ('wv', g % 2))
        rt = 0
        for g in range(3):
            if g + 1 < 3:
                ldw(g + 1)
            dil = (1, 4, 16)[g]
            for which, wt, wn, dst, dname in ((0, wq[g % 2], ('wq', g % 2), QT, 'QT'), (1, wk[g % 2], ('wk', g % 2), KT, 'KT')):
                for pr in range(2):
                    for c in range(NCH):
                        bank = rt % 2
                        proj_fm(K, K.pb[bank], 'ps%d' % bank, wt, wn, slice(128 * pr, 128 * pr + 128), CH * c, CH)
                        if dil == 1:
                            out_ap = dst[:, pr, CH * c:CH * c + CH]
                            W['inre'] = None
                        else:
                            per = CH // dil
                            out_ap = dst[:, pr, :].rearrange("p (r m) -> p m r", r=dil)[:, per * c:per * c + per, :]
                            W['inre'] = "p (m r) -> p m r"
                            W['inkw'] = {'r': dil}
                        rope_tile(K, K.pb[bank], 'ps%d' % bank, CH * c, CH, out_ap, dname, W, i=rt)
                        rt += 1
            for t in range(NT):
                bank = 2 + (t % 2)
                ps = K.pb[bank]
                if dil == 1:
                    tok = lambda kc: K.hT[:, kc, 128 * t:128 * t + 128]
                elif dil == 4:
                    r, j = divmod(t, 4)
                    tok = lambda kc, r=r, j=j: K.hT[:, kc, :].rearrange("p (m r) -> p r m", r=4)[:, r, 128 * j:128 * j + 128]
                else:
                    tok = lambda kc, r=t: K.hT[:, kc, :].rearrange("p (m r) -> p r m", r=16)[:, r, :]
                for kc in range(KC):
                    mm(P, ps[:, 0:256], tok(kc), wv[g % 2][:, kc, :], kc == 0, kc == KC - 1, reads=['hT', ('wv', g % 2)], writes=['ps%d' % bank])
                psv = ps[:, 0:256].rearrange("p (a b c) -> p a b c", a=2, b=2)
                P.op('act', lambda e, t=t, psv=psv: e.copy(out=Vp[:, t, 0:4:2, 0:64], in_=psv[:, :, 0, :]), reads=['ps%d' % bank], writes=['Vp'])
                P.op('dve', lambda e, t=t, psv=psv: e.tensor_copy(out=Vp[:, t, 1:4:2, 64:128], in_=psv[:, :, 1, :]), reads=['ps%d' % bank], writes=['Vp'])
            it = 0
            pipe = Pipe(K, depth=2)
            for pr in range(2):
                for c in range(NCH):
                    nb, db = (3, 4) if (pr * NCH + c) % 2 == 0 else (5, 6)
                    units = []
                    for par in range(2):
                        rows = slice(64 * par, 64 * par + 64)
                        h = 2 * pr + par
                        ones = K.k['k_oneslo'] if par == 0 else K.k['k_oneshi']
                        onesn = 'k_oneslo' if par == 0 else 'k_oneshi'
                        if dil == 16:
                            sm = []
                            pv = []
                            for j in range(4):
                                t = 4 * c + j
                                sm.append((KT[rows, pr, 128 * t:128 * t + 128], QT[rows, pr, 128 * t:128 * t + 128], 128 * j, 128, ['KT', 'QT']))
                                pv.append((nb, Vp[:, t, h, :], ['Vp'], 128 * j, 128, 128 * j))
                                pv.append((db, ones[:], [onesn], 128 * j, 128, 128 * j))
                            sm.append((ident[:], K.k['k_mc4'][:], 0, 512, ['k_identb', 'k_mc4']))
                            units.append((sm, 512, pv, par, 4))
                        else:
                            first_t = 0 if dil == 1 else 4 * c
                            for kt in range(max(4 * c - 1, first_t), 4 * c + 4):
                                q0 = max(128 * kt, CH * c)
                                q1 = min(128 * kt + 256, CH * c + CH)
                                nq = q1 - q0
                                sm = [(KT[rows, pr, 128 * kt:128 * kt + 128], QT[rows, pr, q0:q1], 0, nq, ['KT', 'QT'])]
                                m0 = 0 if q0 == 128 * kt else 128
                                sm.append((ident[:], K.k['k_mpair'][:, m0:m0 + nq], 0, nq, ['k_identb', 'k_mpair']))
                                pv = [(nb, Vp[:, kt, h, :], ['Vp'], 0, nq, q0 - CH * c),
                                      (db, ones[:], [onesn], 0, nq, q0 - CH * c)]
                                units.append((sm, nq, pv, par, 1))

                    def pre(nb=nb, db=db):
                        acc_init(K, nb)
                        acc_init(K, db)

                    def post(nb=nb, db=db, pr=pr, c=c):
                        for bank, acc, an in ((nb, numacc, 'numacc'), (db, denacc, 'denacc')):
                            ps = K.pb[bank]
                            if dil == 1:
                                P.op('act', lambda e, ps=ps, acc=acc: e.copy(out=acc[:, pr, CH * c:CH * c + CH], in_=ps[:, :]),
                                     reads=['ps%d' % bank], writes=[an])
                            elif dil == 4:
                                view = acc[:, pr, :].rearrange("p (m r) -> p r m", r=4)[:, c, :]
                                P.op('dve', lambda e, ps=ps, view=view: e.tensor_tensor(out=view, in0=view, in1=ps[:, :], op=ALU.add),
                                     reads=['ps%d' % bank, an], writes=[an])
                            else:
                                view = acc[:, pr, :].rearrange("p (n r) -> p r n", r=16)[:, 4 * c:4 * c + 4, :]
                                P.op('dve', lambda e, ps=ps, view=view: e.tensor_tensor(out=view, in0=view, in1=ps[:, :].rearrange("p (r n) -> p r n", r=4), op=ALU.add),
                                     reads=['ps%d' % bank, an], writes=[an])
                    ua_ = [u for u in units if u[3] == 0]
                    ub_ = [u for u in units if u[3] == 1]
                    assert len(ua_) == len(ub_)
                    for ui, (a_, b_) in enumerate(zip(ua_, ub_)):
                        da = {'bank': it % 3, 'sm': a_[0], 'nq': a_[1], 'pv': a_[2], 'nqk': a_[4], 'pt': pts[it % 4], 'ptname': ('pt', it % 4),
                              'first': ui == 0}
                        it += 1
                        db_ = {'bank': it % 3, 'sm': b_[0], 'nq': b_[1], 'pv': b_[2], 'nqk': b_[4], 'pt': pts[it % 4], 'ptname': ('pt', it % 4),
                               'post': post if ui == len(ua_) - 1 else None}
                        it += 1
                        pipe.pair(da, db_)
            pipe.flush()
        for pr in range(2):
            P.op('dve', lambda e, pr=pr: e.reciprocal(out=rden[:], in_=denacc[:, pr, :]), reads=['denacc'], writes=['rden'])
            P.op('dve', lambda e, pr=pr: e.tensor_tensor(out=K.yaT[:, pr, :], in0=numacc[:, pr, :], in1=rden[:], op=ALU.mult),
                 reads=['numacc', 'rden'], writes=['yaT'])
        P.report('dil')
        tap_any(K, 'ya%d' % layer, K.yaT[:].rearrange("p a s -> p (a s)"), ['yaT'], [128, 2 * S], BF16)
        P.barrier()
        P.emit()
    P.stack = _prev_stack


def phase_nsa(K, layer):
    P, d = K.P, K.d
    wn = d['wnsa_l'][layer]
    ab = K.arena[:].bitcast(BF16)
    QR = ab[:, 0:8192].rearrange("p (a s) -> p a s", a=4)
    QN = ab[:, 8192:16384].rearrange("p (a s) -> p a s", a=4)
    KS = ab[:, 16384:20480].rearrange("p (g s) -> p g s", g=2)
    KW = ab[:, 20480:24576].rearrange("p (g s) -> p g s", g=2)
    VS = ab[:, 24576:32768].rearrange("p (t g v c) -> p t g v c", t=NT, g=2, v=2)
    ident = K.k['k_identb']
    identf = K.k['k_identf']
    _prev_stack = P.stack
    with ExitStack() as st:
        P.stack = st
        VW = P.sb('VW', [128, NT, 2, 2, 128], BF16)
        gateT = P.sb('gateT', [24, S], BF16)
        negT = P.sb('negT', [128, 2, S], BF16)
        kcT2 = P.sb('kcT2', [128, 2, 128], BF16)
        vcp = P.sb('vcp', [128, 2, 2, 128], BF16)
        P.op('pool', lambda e: e.memset(VW[:].rearrange("p t g v c -> p (t g v c)"), 0.0), writes=['VW'])
        P.op('pool', lambda e: e.memset(vcp[:].rearrange("p g v c -> p (g v c)"), 0.0), writes=['vcp'])
        P.op('pool', lambda e: e.memset(kcT2[:].rearrange("p g c -> p (g c)"), 0.0), writes=['kcT2'])
        P.op('pool', lambda e: e.memset(negT[:].rearrange("p g s -> p (g s)"), 0.0), writes=['negT'])

        wbq = P.sb('wbq', [128, KC, 512], BF16)
        ldflat(K, wbq[:].rearrange("p k n -> p (k n)"), wn[:, 2048:6144], 'wbq')
        with ExitStack() as cst:
            P.stack = cst
            wkv = P.sb('wkv', [128, KC, 256], BF16)
            ldflat(K, wkv[:].rearrange("p k n -> p (k n)"), wn[:, 0:2048], 'wkv')
            cmpT = K.arena[:, 0:4096].rearrange("p (a s) -> p a s", a=2)
            for kv in range(2):
                for c in range(NCH):
                    bank = (kv * NCH + c) % 2
                    proj_fm(K, K.pb[bank], 'ps%d' % bank, wkv, 'wkv', slice(128 * kv, 128 * kv + 128), CH * c, CH)
                    P.op('act', lambda e, bank=bank, kv=kv, c=c: e.copy(out=cmpT[:, kv, CH * c:CH * c + CH], in_=K.pb[bank][:, :]),
                         reads=['ps%d' % bank], writes=['cmpT'])
            w1ds = [ab[:, 8192:16384].rearrange("p (l j) -> p l j", l=32), ab[:, 24576:32768].rearrange("p (l j) -> p l j", l=32)]
            kpes = [ab[:, 16384:20480].rearrange("p (l n) -> p l n", l=32), ab[:, 20480:24576].rearrange("p (l n) -> p l n", l=32)]
            w2d = P.sb('w2d', [128, 2, 128], BF16)
            w2v = P.sb('w2v', [128, 2, 64], BF16)
            peTs = [P.sb('peT%d' % i_, [128, 32], F32) for i_ in range(2)]
            hid = P.sb('hid', [128, 4, 128], BF16)
            for kv in range(2):
                ldflat(K, w1ds[kv].rearrange("p l j -> p (l j)"), d['w1d_l'][layer, kv], ('w1d', kv))
                P.dma('sp', peTs[kv][:], d['pe_l'][layer, kv], writes=[('peT', kv)], slot='r%d' % kv)
            ldflat(K, w2d[:].rearrange("p a b -> p (a b)"), d['w2d_l'][layer], ('w2d', 0))
            ldflat(K, w2v[:].rearrange("p a b -> p (a b)"), d['w2v_l'][layer], 'w2v')
            for kv in range(2):
                w1d, kpe, peT = w1ds[kv], kpes[kv], peTs[kv]
                for l in range(32):
                    P.op('dve', lambda e, l=l, kv=kv, kpe=kpe, peT=peT: e.tensor_scalar(
                        out=kpe[:, l, 0:127], in0=cmpT[:, kv, :].rearrange("p (n r) -> p r n", r=16)[:, l % 16, (l // 16):(l // 16) + 127],
                        scalar1=peT[:, l:l + 1], scalar2=None, op0=ALU.add), reads=['cmpT', ('peT', kv)], writes=[('kpe', kv)])
                for jc in range(2):
                    bks = (3, 4) if jc == 0 else (6, 7)
                    for l in range(32):
                        for g in range(2):
                            rows = slice(64 * g, 64 * g + 64)
                            mm(P, K.pb[bks[g]][:, 0:127], w1d[rows, l, 128 * jc:128 * jc + 128], kpe[rows, l, 0:127], l == 0, l == 31,
                               reads=[('w1d', kv), ('kpe', kv)], writes=['ps%d' % bks[g]])
                    for g in range(2):
                        P.op('act', lambda e, bank=bks[g], g=g, jc=jc: e.activation(out=hid[:, g * 2 + jc, 0:127], in_=K.pb[bank][:, 0:127], func=AF.Silu),
                             reads=['ps%d' % bks[g]], writes=['hid'])
                for g in range(2):
                    if kv == 0:
                        for jc in range(2):
                            mm(P, K.pb[5][:, 0:127], w2d[:, jc, :], hid[:, g * 2 + jc, 0:127], jc == 0, jc == 1,
                               reads=[('w2d', 0), 'hid'], writes=['ps5'])
                        P.op('act', lambda e, g=g: e.copy(out=kcT2[:, g, 0:127], in_=K.pb[5][:, 0:127]), reads=['ps5'], writes=['kcT2'])
                    else:
                        for jc in range(2):
                            mm(P, K.pb[5][0:127, 0:64], hid[:, g * 2 + jc, 0:127], w2v[:, jc, :], jc == 0, jc == 1,
                               reads=['w2v', 'hid'], writes=['ps5'])
                        P.op('act', lambda e, g=g: e.copy(out=vcp[0:127, g, 0, 0:64], in_=K.pb[5][0:127, 0:64]), reads=['ps5'], writes=['vcp'])
                        P.op('dve', lambda e, g=g: e.tensor_copy(out=vcp[0:127, g, 1, 64:128], in_=K.pb[5][0:127, 0:64]), reads=['ps5'], writes=['vcp'])
            P.barrier()
            P.emit()
        P.stack = st
        P.op('pool', lambda e: e.memset(ab[:, 24576:32768], 0.0), writes=['VS'])
        import os as _os
        _stop = int(_os.environ.get('NSA_STOP', '9'))
        if _stop <= 1:
            P.barrier(); P.emit(); P.stack = _prev_stack
            return

        esb = P.sb('esb', [128, 4, 128], F32)
        rs = P.sb('rs', [128, 8], F32)
        ps4 = P.sb('ps4', [128, 128], F32)
        imp = P.sb('imp', [128, 32], F32)
        imp2 = P.sb('imp2', [128, 32], F32)
        m8 = P.sb('m8', [128, 16], F32)
        nsels = [P.sb('nsel%d' % i_, [128, 32], BF16) for i_ in range(3)]

        def selA(k_, i, g):
            nsel, nsn = nsels[k_ % 3], ('nsel', k_ % 3)
            ps = K.pb[6]
            psn = 'ps6'
            first = True
            for par_ in range(2):
                for r in (par_, par_ + 2):
                    h = 4 * g + r
                    pr, par = divmod(h, 2)
                    rows = slice(64 * par, 64 * par + 64)
                    mm(P, ps[:, 128 * r:128 * r + 128], QN[rows, pr, 128 * i:128 * i + 128], kcT2[rows, g, :], first, False,
                       reads=['QN', 'kcT2'], writes=[psn])
                    first = False
                for r in (par_, par_ + 2):
                    mm(P, ps[:, 128 * r:128 * r + 128], ident[:], K.k['k_mwide'][:, 120 - 8 * i:120 - 8 * i + 128], False, (par_ == 1 and r == 3),
                       reads=['k_identb', 'k_mwide'], writes=[psn])
            P.op('act', lambda e: e.activation(out=esb[:].rearrange("p a b -> p (a b)"), in_=ps[:, :], func=AF.Exp, scale=SCALE),
                 reads=[psn], writes=['esb'])
            P.op('dve', lambda e: e.tensor_reduce(out=rs[:, 0:4], in_=esb[:], axis=AX.X, op=ALU.add), reads=['esb'], writes=['rs'])
            P.op('dve', lambda e: e.reciprocal(out=rs[:, 4:8], in_=rs[:, 0:4]), reads=['rs'], writes=['rs'])
            P.op('dve', lambda e: e.tensor_tensor(out=esb[:], in0=esb[:], in1=rs[:, 4:8].unsqueeze(2).to_broadcast([128, 4, 128]), op=ALU.mult),
                 reads=['esb', 'rs'], writes=['esb'])
            P.op('dve', lambda e: e.tensor_reduce(out=ps4[:], in_=esb[:].rearrange("p h n -> p n h"), axis=AX.X, op=ALU.add),
                 reads=['esb'], writes=['ps4'])
            P.op('dve', lambda e: e.tensor_reduce(out=imp[:], in_=ps4[:].rearrange("p (j f) -> p j f", f=4), axis=AX.X, op=ALU.add),
                 reads=['ps4'], writes=['imp'])
            P.op('dve', lambda e: e.tensor_tensor(out=imp[:, 1:32], in0=imp[:, 1:32], in1=ps4[:, 3:124:4], op=ALU.add),
                 reads=['imp', 'ps4'], writes=['imp'])
            P.op('dve', lambda e: e.tensor_tensor(out=imp[:], in0=imp[:], in1=K.k['k_F'][:, 32 * (i - 8):32 * (i - 8) + 32], op=ALU.max),
                 reads=['imp', 'k_F'], writes=['imp'])
            P.op('dve', lambda e: e.max(out=m8[:, 0:8], in_=imp[:]), reads=['imp'], writes=['m8'])
            P.op('dve', lambda e: e.match_replace(out=imp2[:], in_to_replace=m8[:, 0:8], in_values=imp[:], imm_value=-1e30),
                 reads=['imp', 'm8'], writes=['imp2'])
            P.op('dve', lambda e: e.max(out=m8[:, 8:16], in_=imp2[:]), reads=['imp2'], writes=['m8'])
            P.op('dve', lambda e: e.tensor_scalar(out=nsel[:], in0=imp[:], scalar1=m8[:, 15:16], scalar2=NEG, op0=ALU.is_lt, op1=ALU.mult),
                 reads=['imp', 'm8'], writes=[nsn])

        def selB(k_, i, g):
            nsel, nsn = nsels[k_ % 3], ('nsel', k_ % 3)
            pst = K.pb[4][:].bitcast(BF16)
            P.op('pe', lambda e: e.transpose(pst[0:32, 0:128], nsel[:, :], ident[:]), reads=[nsn, 'k_identb'], writes=['ps4'])
            P.op('act', lambda e: e.copy(out=negT[0:32, g, 128 * i:128 * i + 128], in_=pst[0:32, 0:128]), reads=['ps4'], writes=['negT'])

        sel_list = [(i, g) for i in range(8, NT) for g in range(2)]
        sel_pos = [0]

        def sel_step():
            k_ = sel_pos[0]
            sel_pos[0] += 1
            if 0 <= k_ - 2 < len(sel_list):
                selB(k_ - 2, *sel_list[k_ - 2])
            if k_ < len(sel_list):
                selA(k_, *sel_list[k_])

        sel_ticks = [0]

        def sel_tick():
            sel_ticks[0] += 1
            sel_step()

        with ExitStack() as pst:
            P.stack = pst
            W = {'raw': [P.sb('raw%d' % i, [128, CH], BF16) for i in range(2)],
                 't1': [P.sb('t1_%d' % i, [128, CH], F32) for i in range(2)],
                 't2': [P.sb('t2_%d' % i, [128, CH], F32) for i in range(2)], 'inre': None, 'swbanks': (7, 5)}
            load_scoped_consts(K, ['k_cos', 'k_sin'])
            wkd = P.sb('wkd', [128, KC, 2, 2, 2, 64], BF16)
            wvv = P.sb('wvv', [128, KC, 2, 128], BF16)
            wbl = P.sb('wbl', [128, KC, 24], BF16)
            ldflat(K, wkd[:].rearrange("p k a b c d -> p (k a b c d)"), wn[:, 6144:10240], 'wkd')
            ldflat(K, wvv[:].rearrange("p k a n -> p (k a n)"), wn[:, 10240:12288], 'wvv')
            ldflat(K, wbl[:].rearrange("p k n -> p (k n)"), wn[:, 12288:12480], 'wbl')
            P.report('nsa-proj')
            rt = 0
            for pr in range(4):
                for c in range(NCH):
                    bank = rt % 2
                    proj_fm(K, K.pb[bank], 'ps%d' % bank, wbq, 'wbq', slice(128 * pr, 128 * pr + 128), CH * c, CH)
                    rope_tile(K, K.pb[bank], 'ps%d' % bank, CH * c, CH, QR[:, pr, CH * c:CH * c + CH], 'QR', W,
                              nope_ap=QN[:, pr, CH * c:CH * c + CH], nope_name='QN', i=rt)
                    rt += 1
            for w_, dst, dn in ((0, KS, 'KS'), (1, KW, 'KW')):
                for g in range(2):
                    for c in range(NCH):
                        bank = rt % 2
                        for kc in range(KC):
                            mm(P, K.pb[bank][:, :], wkd[:, kc, w_, g, :, :].rearrange("p a b -> p (a b)"), K.hT[:, kc, CH * c:CH * c + CH],
                               kc == 0, kc == KC - 1, reads=['wkd', 'hT'], writes=['ps%d' % bank])
                        rope_tile(K, K.pb[bank], 'ps%d' % bank, CH * c, CH, dst[:, g, CH * c:CH * c + CH], dn, W, i=rt)
                        rt += 1
                        sel_tick()
            for w_, dst, dn in ((0, VS, 'VS'), (1, VW[:], 'VW')):
                for t in range(NT):
                    bank = 2 + (t % 2)
                    for kc in range(KC):
                        mm(P, K.pb[bank][:, 0:128], K.hT[:, kc, 128 * t:128 * t + 128], wvv[:, kc, w_, :], kc == 0, kc == KC - 1,
                           reads=['hT', 'wvv'], writes=['ps%d' % bank])
                    psv = K.pb[bank][:, 0:128].rearrange("p (g dd) -> p g dd", g=2)
                    P.op('act', lambda e, dst=dst, t=t, psv=psv: e.copy(out=dst[:, t, :, 0, 0:64], in_=psv), reads=['ps%d' % bank], writes=[dn])
                    P.op('dve', lambda e, dst=dst, t=t, psv=psv: e.tensor_copy(out=dst[:, t, :, 1, 64:128], in_=psv), reads=['ps%d' % bank], writes=[dn])
                    sel_tick()
            for c in range(NCH):
                bank = 4 + (c % 2)
                for kc in range(KC):
                    mm(P, K.pb[bank][0:24, :], wbl[:, kc, :], K.hT[:, kc, CH * c:CH * c + CH], kc == 0, kc == KC - 1,
                       reads=['wbl', 'hT'], writes=['ps%d' % bank])
                P.op('act', lambda e, bank=bank, c=c: e.activation(out=gateT[:, CH * c:CH * c + CH], in_=K.pb[bank][0:24, :], func=AF.Sigmoid),
                     reads=['ps%d' % bank], writes=['gateT'])
            while sel_pos[0] < len(sel_list) + 2:
                sel_step()
            P.barrier()
            P.emit()
        P.stack = st

        P.report('nsa-afterproj')
        if _stop <= 2:
            P.barrier(); P.emit(); P.stack = _prev_stack
            return
        if _stop <= 3:
            P.barrier(); P.emit(); P.stack = _prev_stack
            return
        load_scoped_consts(K, ['k_cneg', 'k_E'])
        pts = [P.sb('pt%d' % i, [128, CH], BF16) for i in range(4)]
        rden = P.sb('rden', [128, CH], F32)
        wgt = P.sb('wgt', [128, CH], F32)
        yacc = P.sb('yacc', [128, CH], F32)
        ytmp = P.sb('ytmp', [128, CH], F32)
        E = K.k['k_E']
        it = 0
        pc = 0
        pipe = Pipe(K, depth=2)
        for pr in range(4):
            g = pr // 2
            for c in range(NCH):
                for br in range(3):
                    nb, db = (3, 4) if pc % 2 == 0 else (5, 6)
                    pc += 1
                    units = []
                    for par in range(2):
                        rows = slice(64 * par, 64 * par + 64)
                        ones = K.k['k_oneslo'] if par == 0 else K.k['k_oneshi']
                        onesn = 'k_oneslo' if par == 0 else 'k_oneshi'
                        if br == 0:
                            sm = [(kcT2[rows, g, :], QN[rows, pr, CH * c:CH * c + CH], 0, CH, ['kcT2', 'QN']),
                                  (ident[:], K.k['k_cneg'][:, CH * c:CH * c + CH], 0, CH, ['k_identb', 'k_cneg'])]
                            pv = [(nb, vcp[:, g, par, :], ['vcp'], 0, CH, 0), (db, ones[:], [onesn], 0, CH, 0)]
                            units.append((sm, CH, pv, par, 1))
                        else:
                            Kt = KS if br == 1 else KW
                            Kn = 'KS' if br == 1 else 'KW'
                            Vt = VS if br == 1 else VW
                            Vn = 'VS' if br == 1 else 'VW'
                            kt0 = 0 if br == 1 else max(0, 4 * c - 4)
                            for kt in range(kt0, 4 * c + 4):
                                q0 = max(128 * kt, CH * c)
                                q1 = CH * c + CH if br == 1 else min(128 * kt + 640, CH * c + CH)
                                nq = q1 - q0
                                sm = [(Kt[rows, g, 128 * kt:128 * kt + 128], QR[rows, pr, q0:q1], 0, nq, [Kn, 'QR'])]
                                if br == 1 and c >= 2:
                                    sm.append((E[:, 128 * kt:128 * kt + 128], negT[:, g, q0:q1], 0, nq, ['k_E', 'negT']))
                                if q0 == 128 * kt:
                                    sm.append((ident[:], K.k['k_mpair'][:, 0:128], 0, 128, ['k_identb', 'k_mpair']))
                                if br == 2 and q1 == 128 * kt + 640:
                                    sm.append((ident[:], K.k['k_mfar'][:], nq - 128, 128, ['k_identb', 'k_mfar']))
                                pv = [(nb, Vt[:, kt, g, par, :], [Vn], 0, nq, q0 - CH * c), (db, ones[:], [onesn], 0, nq, q0 - CH * c)]
                                units.append((sm, nq, pv, par, 1))

                    def pre(nb=nb, db=db):
                        acc_init(K, nb)
                        acc_init(K, db)

                    def post(nb=nb, db=db, pr=pr, c=c, br=br):
                        mm(P, K.pb[7][:, :], K.k['k_selg'][:, 128 * (pr * 3 + br):128 * (pr * 3 + br) + 128], gateT[:, CH * c:CH * c + CH], True, True,
                           reads=['k_selg', 'gateT'], writes=['ps7'])
                        if br == 0:
                            P.op('dve', lambda e: e.tensor_scalar(out=rden[:], in0=K.pb[db][:, :], scalar1=1e-30, scalar2=None, op0=ALU.max),
                                 reads=['ps%d' % db], writes=['rden'])
                            P.op('dve', lambda e: e.reciprocal(out=rden[:], in_=rden[:]), reads=['rden'], writes=['rden'])
                        else:
                            P.op('dve', lambda e: e.reciprocal(out=rden[:], in_=K.pb[db][:, :]), reads=['ps%d' % db], writes=['rden'])
                        P.op('dve', lambda e: e.tensor_tensor(out=wgt[:], in0=K.pb[7][:, :], in1=rden[:], op=ALU.mult), reads=['ps7', 'rden'], writes=['wgt'])
                        if br == 0:
                            P.op('dve', lambda e: e.tensor_tensor(out=yacc[:], in0=K.pb[nb][:, :], in1=wgt[:], op=ALU.mult),
                                 reads=['ps%d' % nb, 'wgt'], writes=['yacc'])
                        elif br == 1:
                            P.op('dve', lambda e: e.tensor_tensor(out=ytmp[:], in0=K.pb[nb][:, :], in1=wgt[:], op=ALU.mult),
                                 reads=['ps%d' % nb, 'wgt'], writes=['ytmp'])
                            P.op('pool', lambda e: e.tensor_tensor(out=yacc[:], in0=yacc[:], in1=ytmp[:], op=ALU.add), reads=['yacc', 'ytmp'], writes=['yacc'])
                        else:
                            P.op('dve', lambda e: e.tensor_tensor(out=ytmp[:], in0=K.pb[nb][:, :], in1=wgt[:], op=ALU.mult),
                                 reads=['ps%d' % nb, 'wgt'], writes=['ytmp'])
                            P.op('pool', lambda e: e.tensor_tensor(out=K.ybT[:, pr, CH * c:CH * c + CH], in0=yacc[:], in1=ytmp[:], op=ALU.add),
                                 reads=['yacc', 'ytmp'], writes=['ybT'])
                    ua_ = [u for u in units if u[3] == 0]
                    ub_ = [u for u in units if u[3] == 1]
                    assert len(ua_) == len(ub_)
                    for ui, (a_, b_) in enumerate(zip(ua_, ub_)):
                        da = {'bank': it % 3, 'sm': a_[0], 'nq': a_[1], 'pv': a_[2], 'nqk': a_[4], 'pt': pts[it % 4], 'ptname': ('pt', it % 4),
                              'first': ui == 0}
                        it += 1
                        db_ = {'bank': it % 3, 'sm': b_[0], 'nq': b_[1], 'pv': b_[2], 'nqk': b_[4], 'pt': pts[it % 4], 'ptname': ('pt', it % 4),
                               'post': post if ui == len(ua_) - 1 else None}
                        it += 1
                        pipe.pair(da, db_)
        pipe.flush()
        P.report('nsa-attn')
        tap_any(K, 'yb%d' % layer, K.ybT[:].rearrange("p a s -> p (a s)"), ['ybT'], [128, 4 * S], BF16)
        P.barrier()
        P.emit()
    P.stack = _prev_stack


def phase_merge(K, layer):
    P, d = K.P, K.d
    _prev_stack = P.stack
    with ExitStack() as st:
        P.stack = st
        mT = P.sb('mT', [128, KC, S], BF16)
        wmg = [P.sb('wmg%d' % i, [128, 22, 128], BF16) for i in range(2)]
        wo = [P.sb('wo%d' % i, [128, KC, 128], BF16) for i in range(2)]
        sga = [P.sb('sga%d' % i, [128, CH], F32) for i in range(1)]
        sgb = [P.sb('sgb%d' % i, [128, CH], F32) for i in range(1)]
        t1 = [P.sb('mt1_%d' % i, [128, CH], F32) for i in range(1)]
        t2 = [P.sb('mt2_%d' % i, [128, CH], F32) for i in range(1)]

        def ldf(f):
            ldflat(K, wmg[f % 2][:].rearrange("p k n -> p (k n)"), d['wmrg_l'][layer, f], ('wmg', f % 2))
        ldf(0)
        it = 0
        for f in range(KC):
            if f + 1 < KC:
                ldf(f + 1)
            s = f % 2
            for c in range(NCH):
                j = 0
                tk = slice(CH * c, CH * c + CH)
                for kc in range(KC):
                    mm(P, K.pb[0][:, :], wmg[s][:, kc, :], K.hT[:, kc, tk], kc == 0, kc == KC - 1, reads=[('wmg', s), 'hT'], writes=['ps0'])
                P.op('act', lambda e, j=j: e.activation(out=sga[j][:], in_=K.pb[0][:, :], func=AF.Sigmoid), reads=['ps0'], writes=[('sga', j)])
                for kc in range(KC):
                    mm(P, K.pb[1][:, :], wmg[s][:, 8 + kc, :], K.hT[:, kc, tk], kc == 0, kc == KC - 1, reads=[('wmg', s), 'hT'], writes=['ps1'])
                P.op('act', lambda e, j=j: e.activation(out=sgb[j][:], in_=K.pb[1][:, :], func=AF.Sigmoid), reads=['ps1'], writes=[('sgb', j)])
                for kc in range(2):
                    mm(P, K.pb[2][:, :], wmg[s][:, 16 + kc, :], K.yaT[:, kc, tk], kc == 0, kc == 1, reads=[('wmg', s), 'yaT'], writes=['ps2'])
                P.op('dve', lambda e, j=j: e.tensor_tensor(out=t1[j][:], in0=K.pb[2][:, :], in1=sga[j][:], op=ALU.mult),
                     reads=['ps2', ('sga', j)], writes=[('mt1', j)])
                for kc in range(4):
                    mm(P, K.pb[3][:, :], wmg[s][:, 18 + kc, :], K.ybT[:, kc, tk], kc == 0, kc == 3, reads=[('wmg', s), 'ybT'], writes=['ps3'])
                P.op('dve', lambda e, j=j: e.tensor_tensor(out=t2[j][:], in0=K.pb[3][:, :], in1=sgb[j][:], op=ALU.mult),
                     reads=['ps3', ('sgb', j)], writes=[('mt2', j)])
                P.op('pool', lambda e, j=j, f=f, tk=tk: e.tensor_tensor(out=mT[:, f, tk], in0=t1[j][:], in1=t2[j][:], op=ALU.add),
                     reads=[('mt1', j), ('mt2', j)], writes=['mT'])
                it += 1
        P.report('merge')
        ldflat(K, wo[0][:].rearrange("p k n -> p (k n)"), d['wout_l'][layer, 0], ('wo', 0))
        for f in range(KC):
            if f + 1 < KC:
                ldflat(K, wo[(f + 1) % 2][:].rearrange("p k n -> p (k n)"), d['wout_l'][layer, f + 1], ('wo', (f + 1) % 2))
            s = f % 2
            for c in range(NCH):
                bank = 4 + (c % 2)
                tk = slice(CH * c, CH * c + CH)
                for kc in range(KC):
                    mm(P, K.pb[bank][:, :], wo[s][:, kc, :], mT[:, kc, tk], kc == 0, kc == KC - 1, reads=[('wo', s), 'mT'], writes=['ps%d' % bank])
                P.op('dve', lambda e, bank=bank, f=f, tk=tk: e.scalar_tensor_tensor(
                    out=K.xT[:, f, tk], in0=K.pb[bank][:, :], scalar=K.modc[:, layer, 16 + f:17 + f], in1=K.xT[:, f, tk],
                    op0=ALU.mult, op1=ALU.add), reads=['ps%d' % bank, 'xT', 'modc'], writes=['xT'])


def phase_ffn(K, layer):
    P, d = K.P, K.d
    HALF = S // 2
    _prev_stack = P.stack
    with ExitStack() as st:
        P.stack = st
        actT = P.sb('actT', [128, NFF, HALF], BF16)
        NWU = 4
        wup = K.pre_wup
        NWD = 3
        wd = [P.sb('wd%d' % i, [128, NFF, 128], BF16) for i in range(NWD)]
        ub = [[P.sb('ub%d_%d' % (gv, i), [128, CH + 2], F32) for i in range(2)] for gv in range(2)]
        tt = [[P.sb('tt%d_%d' % (gv, i), [128, CH], F32) for i in range(2)] for gv in range(2)]
        sg = [P.sb('sgt%d' % i, [128, CH], F32) for i in range(1)]
        P.op('pool', lambda e: e.memset(K.halo[:].rearrange("p a b -> p (a b)"), 0.0), writes=['halo'])

        def ldu(fc):
            s = fc % NWU
            ldflat(K, wup[s][:].rearrange("p k a n -> p (k a n)"), d['wup_l'][layer, fc], ('wup', s))

        def ldd(f):
            ldflat(K, wd[f % NWD][:].rearrange("p k n -> p (k n)"), d['wdn_l'][layer, f], ('wd', f % NWD))
        it = 0
        tail = [None]
        for hf in range(2):
            if hf > 0:
                for f_ in range(NWU - 1):
                    ldu(f_)
            for fc in range(NFF):
                if fc + NWU - 1 < NFF:
                    ldu(fc + NWU - 1)
                if fc == NFF - 4:
                    ldd(0)
                if fc == NFF - 2:
                    ldd(1)
                s = fc % NWU
                for cc in range(2):
                    c = 2 * hf + cc
                    tk = slice(CH * c, CH * c + CH)
                    j = it % 2
                    it += 1
                    for gv in range(2):
                        bank = (0, 2, 6)[(it - 1) % 3] + gv
                        ch = fc + NFF * gv
                        for kc in range(KC):
                            mm(P, K.pb[bank][:, :], wup[s][:, kc, gv, :], K.hT[:, kc, tk], kc == 0, kc == KC - 1,
                               reads=[('wup', s), 'hT'], writes=['ps%d' % bank])
                        u = ub[gv][j]
                        un = ('ub', gv, j)
                        t = tt[gv][j]
                        tn = ('tt', gv, j)
                        P.op('act', lambda e, u=u, bank=bank: e.copy(out=u[:, 2:CH + 2], in_=K.pb[bank][:, :]), reads=['ps%d' % bank], writes=[un])
                        P.op('pool', lambda e, u=u, ch=ch: e.tensor_copy(out=u[:, 0:2], in_=K.halo[:, ch, :]), reads=['halo'], writes=[un])
                        P.op('pool', lambda e, u=u, ch=ch: e.tensor_copy(out=K.halo[:, ch, :], in_=u[:, CH:CH + 2]), reads=[un], writes=['halo'])
                        cw = K.cvw[:, layer, :, ch:ch + 1]
                        P.op('act', lambda e, t=t, cw=cw, bank=bank: e.activation(out=t[:], in_=K.pb[bank][:, :], func=AF.Identity,
                                                                                  scale=cw[:, 2, :], bias=cw[:, 3, :]),
                             reads=['ps%d' % bank, 'cvw'], writes=[tn])
                        P.op('dve', lambda e, u=u, t=t, cw=cw: e.scalar_tensor_tensor(out=t[:], in0=u[:, 1:CH + 1], scalar=cw[:, 1, :], in1=t[:],
                                                                                      op0=ALU.mult, op1=ALU.add), reads=[un, tn, 'cvw'], writes=[tn])
                        P.op('dve', lambda e, u=u, t=t, cw=cw: e.scalar_tensor_tensor(out=t[:], in0=u[:, 0:CH], scalar=cw[:, 0, :], in1=t[:],
                                                                                      op0=ALU.mult, op1=ALU.add), reads=[un, tn, 'cvw'], writes=[tn])
                    if tail[0] is not None:
                        tail[0]()

                    def _tail(j=j, fc=fc, cc=cc):
                        P.op('act', lambda e: e.activation(out=sg[0][:], in_=tt[0][j][:], func=AF.Silu), reads=[('tt', 0, j)], writes=[('sgt', 0)])
                        P.op('pool', lambda e: e.tensor_tensor(out=actT[:, fc, CH * cc:CH * cc + CH], in0=sg[0][:], in1=tt[1][j][:], op=ALU.mult),
                             reads=[('sgt', 0), ('tt', 1, j)], writes=['actT'])
                    tail[0] = _tail
            tail[0]()
            tail[0] = None
            for f in range(KC):
                if f + 2 < KC:
                    ldd(f + 2)
                s = f % NWD
                for cc in range(2):
                    c = 2 * hf + cc
                    bank = 4 + cc
                    tk = slice(CH * c, CH * c + CH)
                    for k2 in range(NFF):
                        mm(P, K.pb[bank][:, :], wd[s][:, k2, :], actT[:, k2, CH * cc:CH * cc + CH], k2 == 0, k2 == NFF - 1,
                           reads=[('wd', s), 'actT'], writes=['ps%d' % bank])
                    P.op('dve', lambda e, bank=bank, f=f, tk=tk: e.scalar_tensor_tensor(
                        out=K.xT[:, f, tk], in0=K.pb[bank][:, :], scalar=K.modc[:, layer, 40 + f:41 + f], in1=K.xT[:, f, tk],
                        op0=ALU.mult, op1=ALU.add), reads=['ps%d' % bank, 'xT', 'modc'], writes=['xT'])
        P.report('ffn')
        P.barrier()
        P.emit()
    P.stack = _prev_stack


def phase_final(K):
    P = K.P
    identf = K.k['k_identf']
    _prev_stack = P.stack
    with ExitStack() as st:
        P.stack = st
        sq = [P.sb('sq%d' % i, [128, CH], F32) for i in range(4)]
        rstd = [P.sb('rstd%d' % i, [128, CH], F32) for i in range(2)]
        tmp = [P.sb('ntmp%d' % i, [128, CH], F32) for i in range(4)]
        K.epsb = P.sb('epsb', [128, 1], F32)
        P.op('pool', lambda e: e.memset(K.epsb[:], 1024.0 * 1e-6), writes=['epsb'])
        ot = [P.sb('ot%d' % i, [128, D], F32) for i in range(4)]

        def make_out_fn(c):
            def out_fn(kc, t_, tname):
                bank = 2 + (kc % 4)
                for j in range(4):
                    P.op('pe', lambda e, j=j: e.transpose(K.pb[bank][:, 128 * j:128 * j + 128], t_[:, 128 * j:128 * j + 128], identf[:]),
                         reads=[tname, 'k_identf'], writes=['ps%d' % bank])
                dstv = [ot[j][:, 128 * kc:128 * kc + 128] for j in range(4)]
                for j in range(4):
                    if kc % 2 == 0:
                        P.op('act', lambda e, j=j: e.copy(out=dstv[j], in_=K.pb[bank][:, 128 * j:128 * j + 128]), reads=['ps%d' % bank], writes=[('ot', j)])
                    else:
                        P.op('dve', lambda e, j=j: e.tensor_copy(out=dstv[j], in_=K.pb[bank][:, 128 * j:128 * j + 128]), reads=['ps%d' % bank], writes=[('ot', j)])
            return out_fn

        def after_chunk(c):
            for j in range(4):
                t = 4 * c + j
                P.dma('sp', K.out[128 * t:128 * t + 128, :], ot[j][:], reads=[('ot', j)], slot='o%d' % j)
        norm_all(K, K.gfin[:], make_out_fn, (sq, rstd, tmp), after_chunk)
        P.barrier()
        P.emit()
    P.stack = _prev_stack


_CACHE = {}


def _prep_inputs(inputs):
    f = lambda a: np.ascontiguousarray(np.asarray(a, dtype=np.float32))
    shared = relayout_weights(inputs)
    for nm in ('norm1_g', 'norm2_g', 'final_g', 'b_mod', 'conv_w', 'conv_b'):
        shared[nm] = f(inputs[nm]).reshape(IN_SHAPES[nm])
    shared.update(make_consts())
    maps = []
    x = f(inputs['x'])
    c = f(inputs['c'])
    for b in range(8):
        m = dict(shared)
        m['x'] = x[b]
        m['c'] = c[b].reshape(8, 128)
        maps.append(m)
    return maps


def kernel(**inputs):
    if 'nc' not in _CACHE:
        _CACHE['nc'] = build(2)[0]
    nc = _CACHE['nc']
    maps = _prep_inputs(inputs)
    res = run_bass_kernel_spmd(nc, maps, core_ids=list(range(8)))
    out = np.stack([np.asarray(res.results[b]['out'], dtype=np.float32) for b in range(8)], axis=0)
    return out
```
